# Optimizing a Trainium2 kernel written in Bass

```python
import jax
import jax.numpy as jnp
from jax import lax
import numpy as np

D_MODEL = 2048
BATCH = 2
SEQ = 16384
DEPTH = 2

GRID_W = 64
CTX_LEN = 256
N_BRANCH = 4
W_BR = D_MODEL // 4
W_A = W_BR
W_C = W_BR
HEAD_DIM = 128
N_Q = W_BR // HEAD_DIM
N_KV = N_Q // 2
GROUP = N_Q // N_KV
Q_W = N_Q * HEAD_DIM
KV_W = N_KV * HEAD_DIM
CONV_A = 3
CONV_C = 31
BLOCK = 128
WINDOW = 128
ROT_AXIS = HEAD_DIM // 2
ROPE_THETA = 10000.0
FFN_HIDDEN = -(-8 * D_MODEL // (3 * 256)) * 256
EPS = 1e-6
ATTN_SCALE = HEAD_DIM ** -0.5

OFF_A = 0
OFF_BQ = OFF_A + 3 * W_A
OFF_BK = OFF_BQ + Q_W
OFF_BV = OFF_BK + KV_W
OFF_C = OFF_BV + KV_W
OFF_DQ = OFF_C + 2 * W_C
OFF_DK = OFF_DQ + Q_W
OFF_DV = OFF_DK + KV_W
OFF_G = OFF_DV + KV_W
P_IN = OFF_G + N_BRANCH * D_MODEL

kernel_name = 'hybrid_parallel_gated_dit_block'


def rms_norm(x, g):
    xf = x.astype(jnp.float32)
    y = xf * lax.rsqrt(jnp.mean(xf * xf, axis=-1, keepdims=True) + EPS)
    return (y * g.astype(jnp.float32)).astype(x.dtype)


def layer_norm(x, g, b):
    xf = x.astype(jnp.float32)
    mu = jnp.mean(xf, axis=-1, keepdims=True)
    var = jnp.mean(jnp.square(xf - mu), axis=-1, keepdims=True)
    y = (xf - mu) * lax.rsqrt(var + EPS) * g.astype(jnp.float32) + b.astype(jnp.float32)
    return y.astype(x.dtype)


def depthwise_conv(h, w, b=None):
    k, ch = w.shape
    y = lax.conv_general_dilated(h, w[:, None, :].astype(h.dtype), window_strides=(1,),
                                 padding=[(k // 2, k // 2)],
                                 dimension_numbers=('NWC', 'WIO', 'NWC'),
                                 feature_group_count=ch)
    return y if b is None else y + b


def heads(p, off, n):
    return p[..., off:off + n * HEAD_DIM].reshape(p.shape[:2] + (n, HEAD_DIM))


def axial_rope_tables(row, col):
    inv = ROPE_THETA ** (-jnp.arange(0, ROT_AXIS, 2, dtype=jnp.float32) / ROT_AXIS)
    ar = row.astype(jnp.float32)[:, None] * inv
    ac = col.astype(jnp.float32)[:, None] * inv
    ang = jnp.concatenate([ar, ar, ac, ac], axis=-1)
    return jnp.cos(ang), jnp.sin(ang)


def apply_rope(x, cos, sin):
    xr = x.reshape(x.shape[:-1] + (2, 2, ROT_AXIS // 2))
    rot = jnp.stack([-xr[..., 1, :], xr[..., 0, :]], axis=-2).reshape(x.shape)
    return x * cos + rot * sin


def short_conv_mixer(pa, w):
    bg, cg, h = jnp.split(pa, 3, axis=-1)
    return bg * depthwise_conv(cg * h, w)


def conformer_conv(pc, w, b, g, beta):
    val, gate = jnp.split(pc, 2, axis=-1)
    h = depthwise_conv(val * jax.nn.sigmoid(gate), w, b)
    return jax.nn.silu(layer_norm(h, g, beta))


def window_attention(q, k, v, kc, vc, sink):
    bsz, n_tok = q.shape[:2]
    nb = n_tok // BLOCK
    qb = q.reshape(bsz, nb, BLOCK, N_KV, GROUP, HEAD_DIM)
    pad = ((0, 0), (1, 1), (0, 0), (0, 0), (0, 0))
    kp = jnp.pad(k.reshape(bsz, nb, BLOCK, N_KV, HEAD_DIM), pad)
    vp = jnp.pad(v.reshape(bsz, nb, BLOCK, N_KV, HEAD_DIM), pad)
    kw = jnp.concatenate([kp[:, :-2], kp[:, 1:-1], kp[:, 2:]], axis=2)
    vw = jnp.concatenate([vp[:, :-2], vp[:, 1:-1], vp[:, 2:]], axis=2)
    s_loc = jnp.einsum('bnqhgd,bnkhd->bnhgqk', qb, kw).astype(jnp.float32)
    s_ctx = jnp.einsum('bnqhgd,bchd->bnhgqc', qb, kc).astype(jnp.float32)
    k_off = jnp.arange(3 * BLOCK) - BLOCK
    rel = k_off[None, :] - jnp.arange(BLOCK)[:, None]
    k_abs = jnp.arange(nb)[:, None, None] * BLOCK + k_off[None, None, :]
    valid = (jnp.abs(rel) <= WINDOW)[None] & (k_abs >= 0) & (k_abs < n_tok)
    s_loc = jnp.where(valid[None, :, None, None], s_loc, -jnp.inf)
    s_sink = jnp.broadcast_to(sink.astype(jnp.float32).reshape(N_KV, GROUP, 1, 1), s_loc.shape[:-1] + (1,))
    p = jax.nn.softmax(jnp.concatenate([s_loc, s_ctx, s_sink], axis=-1), axis=-1).astype(v.dtype)
    n_loc = 3 * BLOCK
    n_ctx = kc.shape[1]
    o = (jnp.einsum('bnhgqk,bnkhd->bnqhgd', p[..., :n_loc], vw)
         + jnp.einsum('bnhgqc,bchd->bnqhgd', p[..., n_loc:n_loc + n_ctx], vc))
    return o.reshape(bsz, n_tok, Q_W)


def global_attention(q, k, v, kc, vc):
    bsz, n_tok = q.shape[:2]
    nb = n_tok // BLOCK
    qb = jnp.moveaxis(q.reshape(bsz, nb, BLOCK, N_KV, GROUP, HEAD_DIM), 1, 0)
    k_all = jnp.concatenate([k, kc], axis=1)
    v_all = jnp.concatenate([v, vc], axis=1)

    def one_block(q_blk):
        s = jnp.einsum('bqhgd,bkhd->bhgqk', q_blk, k_all).astype(jnp.float32)
        p = jax.nn.softmax(s, axis=-1).astype(v_all.dtype)
        return jnp.einsum('bhgqk,bkhd->bqhgd', p, v_all)

    o = lax.map(one_block, qb)
    return jnp.moveaxis(o, 0, 1).reshape(bsz, n_tok, Q_W)


def context_attention(qc, kc, vc, sink=None):
    bsz, n = qc.shape[:2]
    qg = qc.reshape(bsz, n, N_KV, GROUP, HEAD_DIM)
    s = jnp.einsum('bqhgd,bkhd->bhgqk', qg, kc).astype(jnp.float32)
    n_k = kc.shape[1]
    if sink is not None:
        s_sink = jnp.broadcast_to(sink.astype(jnp.float32).reshape(N_KV, GROUP, 1, 1), s.shape[:-1] + (1,))
        s = jnp.concatenate([s, s_sink], axis=-1)
    p = jax.nn.softmax(s, axis=-1)[..., :n_k].astype(vc.dtype)
    return jnp.einsum('bhgqk,bkhd->bqhgd', p, vc).reshape(bsz, n, Q_W)


def merge_branches(ys, gate_logits, b_gate, w_branch, w_out):
    g = jax.nn.sigmoid(gate_logits.reshape(gate_logits.shape[:2] + (N_BRANCH, D_MODEL)) + b_gate)
    m = g[:, :, 0] * (ys[0] @ w_branch[0])
    for i in range(1, N_BRANCH):
        m = m + g[:, :, i] * (ys[i] @ w_branch[i])
    return m @ w_out


def swiglu(h, w_in, w_out):
    a, b = jnp.split(h @ w_in, 2, axis=-1)
    return (jax.nn.silu(a) * b) @ w_out


def setup_inputs(seed: int = 0) -> dict:
    key = jax.random.key(seed)
    ks = jax.random.split(key, 24)
    f32 = jnp.float32
    D = D_MODEL

    def nrm(k, shape, scale):
        return jax.random.normal(k, shape, f32) * scale

    return {
        'x': nrm(ks[0], (BATCH, SEQ, D), 1.0),
        'c': nrm(ks[1], (BATCH, D), 1.0),
        'ctx': nrm(ks[2], (BATCH, CTX_LEN, D), 1.0),
        'c_ctx': nrm(ks[3], (D,), 1.0),
        'w_ada': nrm(ks[4], (DEPTH, D, 6 * D), 0.5 * D ** -0.5),
        'b_ada': nrm(ks[5], (DEPTH, 6 * D), 0.02),
        'norm_mix': 1.0 + nrm(ks[6], (DEPTH, D), 0.02),
        'norm_ffn': 1.0 + nrm(ks[7], (DEPTH, D), 0.02),
        'w_in': nrm(ks[8], (DEPTH, D, P_IN), D ** -0.5),
        'b_gate': nrm(ks[9], (DEPTH, N_BRANCH, D), 0.02),
        'conv_a_w': nrm(ks[10], (DEPTH, CONV_A, W_A), CONV_A ** -0.5),
        'sink_b': nrm(ks[11], (DEPTH, N_Q), 0.5),
        'qk_norm_q': 1.0 + nrm(ks[12], (DEPTH, HEAD_DIM), 0.02),
        'qk_norm_k': 1.0 + nrm(ks[13], (DEPTH, HEAD_DIM), 0.02),
        'conv_c_w': nrm(ks[14], (DEPTH, CONV_C, W_C), CONV_C ** -0.5),
        'conv_c_b': nrm(ks[15], (DEPTH, W_C), 0.02),
        'ln_c_g': 1.0 + nrm(ks[16], (DEPTH, W_C), 0.02),
        'ln_c_b': nrm(ks[17], (DEPTH, W_C), 0.02),
        'w_branch': nrm(ks[18], (DEPTH, N_BRANCH, W_BR, D), W_BR ** -0.5),
        'w_out': nrm(ks[19], (DEPTH, D, D), D ** -0.5),
        'w_ffn_in': nrm(ks[20], (DEPTH, D, 2 * FFN_HIDDEN), D ** -0.5),
        'w_ffn_out': nrm(ks[21], (DEPTH, FFN_HIDDEN, D), FFN_HIDDEN ** -0.5),
        'norm_final': 1.0 + nrm(ks[22], (D,), 0.02),
    }


def reference(x, c, ctx, c_ctx, w_ada, b_ada, norm_mix, norm_ffn, w_in, b_gate, conv_a_w, sink_b,
              qk_norm_q, qk_norm_k, conv_c_w, conv_c_b, ln_c_g, ln_c_b, w_branch, w_out,
              w_ffn_in, w_ffn_out, norm_final):
    n_tok = x.shape[1]
    rows = n_tok // GRID_W
    row = jnp.repeat(jnp.arange(rows), GRID_W)
    col = jnp.tile(jnp.arange(GRID_W), rows)
    cos, sin = axial_rope_tables(row, col)
    cos = cos.astype(x.dtype)[:, None, :]
    sin = sin.astype(x.dtype)[:, None, :]

    xc = ctx
    s_c = jax.nn.silu(c)
    s_cc = jax.nn.silu(c_ctx)
    for l in range(DEPTH):
        last = l == DEPTH - 1
        mod = (s_c @ w_ada[l] + b_ada[l])[:, None, :]
        sh_m, sc_m, g_m, sh_f, sc_f, g_f = jnp.split(mod, 6, axis=-1)
        n_c = 2 if last else 6
        mod_c = s_cc @ w_ada[l][:, :n_c * D_MODEL] + b_ada[l][:n_c * D_MODEL]
        mc = jnp.split(mod_c, n_c)

        h = rms_norm(x, norm_mix[l]) * (1.0 + sc_m) + sh_m
        hc = rms_norm(xc, norm_mix[l]) * (1.0 + mc[1]) + mc[0]

        if last:
            kvb_c = hc @ w_in[l][:, OFF_BK:OFF_BK + 2 * KV_W]
            kvd_c = hc @ w_in[l][:, OFF_DK:OFF_DK + 2 * KV_W]
        else:
            pc = hc @ w_in[l]
            kvb_c = pc[..., OFF_BK:OFF_BK + 2 * KV_W]
            kvd_c = pc[..., OFF_DK:OFF_DK + 2 * KV_W]
        kc_b = heads(kvb_c, 0, N_KV)
        vc_b = heads(kvb_c, KV_W, N_KV)
        kc_d = rms_norm(heads(kvd_c, 0, N_KV), qk_norm_k[l])
        vc_d = heads(kvd_c, KV_W, N_KV)

        p = h @ w_in[l]
        y_a = short_conv_mixer(p[..., OFF_A:OFF_A + 3 * W_A], conv_a_w[l])
        q_b = apply_rope(heads(p, OFF_BQ, N_Q), cos, sin) * ATTN_SCALE
        k_b = apply_rope(heads(p, OFF_BK, N_KV), cos, sin)
        y_b = window_attention(q_b, k_b, heads(p, OFF_BV, N_KV), kc_b, vc_b, sink_b[l])
        y_c = conformer_conv(p[..., OFF_C:OFF_C + 2 * W_C], conv_c_w[l], conv_c_b[l], ln_c_g[l], ln_c_b[l])
        q_d = apply_rope(rms_norm(heads(p, OFF_DQ, N_Q), qk_norm_q[l]), cos, sin) * ATTN_SCALE
        k_d = apply_rope(rms_norm(heads(p, OFF_DK, N_KV), qk_norm_k[l]), cos, sin)
        y_d = global_attention(q_d, k_d, heads(p, OFF_DV, N_KV), kc_d, vc_d)
        mix = merge_branches([y_a, y_b, y_c, y_d], p[..., OFF_G:], b_gate[l], w_branch[l], w_out[l])
        x = x + g_m * mix

        if not last:
            yc_a = short_conv_mixer(pc[..., OFF_A:OFF_A + 3 * W_A], conv_a_w[l])
            yc_b = context_attention(heads(pc, OFF_BQ, N_Q) * ATTN_SCALE, kc_b, vc_b, sink_b[l])
            yc_c = conformer_conv(pc[..., OFF_C:OFF_C + 2 * W_C], conv_c_w[l], conv_c_b[l], ln_c_g[l], ln_c_b[l])
            qc_d = rms_norm(heads(pc, OFF_DQ, N_Q), qk_norm_q[l]) * ATTN_SCALE
            yc_d = context_attention(qc_d, kc_d, vc_d)
            mix_c = merge_branches([yc_a, yc_b, yc_c, yc_d], pc[..., OFF_G:], b_gate[l], w_branch[l], w_out[l])
            xc = xc + mc[2] * mix_c
            hc2 = rms_norm(xc, norm_ffn[l]) * (1.0 + mc[4]) + mc[3]
            xc = xc + mc[5] * swiglu(hc2, w_ffn_in[l], w_ffn_out[l])

        h2 = rms_norm(x, norm_ffn[l]) * (1.0 + sc_f) + sh_f
        x = x + g_f * swiglu(h2, w_ffn_in[l], w_ffn_out[l])

    return rms_norm(x, norm_final)
```

```python
import numpy as np
from contextlib import ExitStack
import concourse.bass as bass
import concourse.mybir as mybir
from concourse.bass_utils import run_bass_kernel_spmd
import ml_dtypes

F32 = mybir.dt.float32
BF16 = mybir.dt.bfloat16
AF = mybir.ActivationFunctionType
ALU = mybir.AluOpType
NPBF = ml_dtypes.bfloat16

D = 2048; KC = 16; SEQ = 16384; NTOK = 4096; CTX = 256; NT = 8; NQ = 512
HD = 128; EPS = 1e-6; SCALE = HD ** -0.5
FH = 5632; FKC = 44
TOKA = NTOK + CTX
WL = 8192


class Sched:
    COMPUTE = ('pe', 'act', 'dve')
    QUEUES = ('sp', 'pool')
    RING = 8
    SEM_LIMIT = 30000

    def __init__(self):
        self.ops = []
        self.lastw = {}
        self.readers = {}

    def add(self, eng, fn, reads=(), writes=(), dma=False):
        i = len(self.ops)
        raw = set(); war = set()
        for k in reads:
            w = self.lastw.get(k)
            if w is not None:
                raw.add(w)
        for k in writes:
            w = self.lastw.get(k)
            if w is not None:
                war.add(w)
            for r in self.readers.get(k, ()):
                war.add(r)
        for k in reads:
            self.readers.setdefault(k, []).append(i)
        for k in writes:
            self.lastw[k] = i
            self.readers[k] = []
        raw.discard(i); war.discard(i)
        self.ops.append([eng, fn, raw, war, dma])
        return i

    def emit(self, nc, block, stack):
        ops = self.ops
        n = len(ops)
        eff = []
        needed = set()
        for i, (eng, fn, raw, war, dma) in enumerate(ops):
            ds = set()
            for d in raw | war:
                de, _, _, _, ddma = ops[d]
                if not ddma and not dma and de == eng:
                    if eng == 'pe':
                        continue
                ds.add(d)
            eff.append(ds)
            needed |= ds
        sig = {}
        sems = {}
        def newsem(name):
            s = stack.enter_context(nc.semaphore(name))
            return s
        cur = {}
        cnt = {}
        dq = {q: [newsem(f"dq_{q}_{r}") for r in range(self.RING)] for q in self.QUEUES}
        dcount = {q: 0 for q in self.QUEUES}
        dma_prev = {}
        for i, (eng, fn, raw, war, dma) in enumerate(ops):
            if dma:
                k = dcount[eng]; dcount[eng] += 1
                s = dq[eng][k % self.RING]
                v = 16 * (k // self.RING + 1)
                sig[i] = (s, v)
                if k >= self.RING:
                    dma_prev[i] = (s, v - 16)
            elif i in needed:
                if eng not in cur or cnt[eng] >= self.SEM_LIMIT:
                    cur[eng] = newsem(f"c_{eng}_{len(sems)}")
                    sems[len(sems)] = cur[eng]
                    cnt[eng] = 0
                cnt[eng] += 1
                sig[i] = (cur[eng], cnt[eng])
        per = {e: [] for e in self.COMPUTE + self.QUEUES}
        for i, o in enumerate(ops):
            per[o[0]].append(i)
        final_waits = [sig[i] for i, o in enumerate(ops) if o[4]]

        def run(engname, e):
            waited = {}
            def w(sv):
                s, v = sv
                key = id(s)
                if waited.get(key, 0) >= v:
                    return
                waited[key] = v
                e.wait_ge(s, v)
            for i in per[engname]:
                eng, fn, raw, war, dma = ops[i]
                for d in sorted(eff[i]):
                    w(sig[d])
                if i in dma_prev:
                    w(dma_prev[i])
                ins = fn(e)
                if i in sig:
                    s, v = sig[i]
                    ins.then_inc(s, 16 if dma else 1)
            if engname == 'sp':
                last = {}
                for s, v in final_waits:
                    if last.get(id(s), (None, 0))[1] < v:
                        last[id(s)] = (s, v)
                for s, v in last.values():
                    w((s, v))

        block.sync(lambda e: run('sp', e))
        block.gpsimd(lambda e: run('pool', e))
        block.tensor(lambda e: run('pe', e))
        block.scalar(lambda e: run('act', e))
        block.vector(lambda e: run('dve', e))


def fm(v):
    v = np.asarray(v, np.float32)
    return np.ascontiguousarray(v.reshape(-1, 128).T)


def blk(W):
    K, Fb = W.shape
    kc = K // 128
    return W.reshape(kc, 128, Fb).transpose(1, 0, 2).reshape(128, kc * Fb)


OFF_BG, OFF_CG, OFF_HA = 0, 512, 1024
OFF_BQ, OFF_BK, OFF_BV = 1536, 2048, 2304
OFF_VAL, OFF_GATE = 2560, 3072
OFF_DQ, OFF_DK, OFF_DV = 3584, 4096, 4352
OFF_G = 4608


def cols(W, starts, width=128):
    return np.concatenate([W[:, s:s + width] for s in starts], axis=1)


def pack_A(w_in_l):
    W = np.asarray(w_in_l, np.float32)
    b = []
    b.append(blk(cols(W, [OFF_CG, OFF_HA, OFF_CG + 128, OFF_HA + 128])))
    b.append(blk(cols(W, [OFF_CG + 256, OFF_HA + 256, OFF_CG + 384, OFF_HA + 384])))
    b.append(blk(cols(W, [OFF_GATE, OFF_VAL, OFF_GATE + 128, OFF_VAL + 128])))
    b.append(blk(cols(W, [OFF_GATE + 256, OFF_VAL + 256, OFF_GATE + 384, OFF_VAL + 384])))
    b.append(blk(cols(W, [OFF_BK, OFF_BK + 128, OFF_DK, OFF_DK + 128])))
    b.append(blk(cols(W, [OFF_BV, OFF_BV + 128, OFF_DV, OFF_DV + 128])))
    return np.ascontiguousarray(np.concatenate(b, axis=1))


def rope_tables():
    inv = (10000.0 ** (-np.arange(0, 64, 2, dtype=np.float32) / np.float32(64))).astype(np.float32)
    t = np.arange(SEQ)
    row = (t // 64).astype(np.float32); col = (t % 64).astype(np.float32)
    ar = row[:, None] * inv; ac = col[:, None] * inv
    ang = np.concatenate([ar, ar, ac, ac], axis=-1).astype(np.float32)
    return np.cos(ang).astype(np.float32).T.copy(), np.sin(ang).astype(np.float32).T.copy()


def rot_matrix_T():
    R = np.zeros((128, 128), np.float32)
    for a in range(2):
        for i in range(32):
            R[a * 64 + i, a * 64 + 32 + i] = -1.0
            R[a * 64 + 32 + i, a * 64 + i] = 1.0
    return np.ascontiguousarray(R.T)


class VT:
    def __init__(self):
        self.off = {}
        self.n = 0
    def add(self, name, width):
        self.off[name] = (self.n, width)
        self.n += width
    def sl(self, name, i=None):
        o, w = self.off[name]
        if i is None:
            return slice(o, o + w)
        return slice(o + i, o + i + 1)


class Ctx:
    pass


def emit_norm(S, c, xT, hT, N, gain, shift, tagx, out_f32=None, extra_reads=(), out_key='xo'):
    ps = c.ps[c.psn % c.psmod]; pk = ('ps', c.psn % c.psmod); c.psn += 1
    for kc in range(KC):
        sq = c.t32[kc % 2]; sk = ('t32', kc % 2)
        S.add('act', lambda e, kc=kc, sq=sq: e.activation(out=sq[:, :N], in_=xT[:, kc, :N], func=AF.Square),
              reads=[tagx] + list(extra_reads), writes=[sk])
        S.add('pe', lambda e, kc=kc, sq=sq: e.matmul(ps[:, :N], c.ones32[:, :], sq[:, :N], start=(kc == 0), stop=(kc == KC - 1)),
              reads=[sk], writes=[pk])
    rs = c.t32[2]; rk = ('t32', 2)
    S.add('act', lambda e: e.activation(out=rs[:, :N], in_=ps[:, :N], func=AF.Sqrt, bias=float(EPS * D)), reads=[pk], writes=[rk])
    S.add('dve', lambda e: e.reciprocal(out=rs[:, :N], in_=rs[:, :N]), reads=[rk], writes=[rk])
    for kc in range(KC):
        tmp = c.t32[kc % 2]; tk = ('t32', kc % 2)
        S.add('dve', lambda e, kc=kc, tmp=tmp: e.tensor_tensor(out=tmp[:, :N], in0=xT[:, kc, :N], in1=rs[:, :N], op=ALU.mult),
              reads=[tagx, rk], writes=[tk])
        if out_f32 is None:
            S.add('act', lambda e, kc=kc, tmp=tmp: e.activation(out=hT[:, kc, :N], in_=tmp[:, :N], func=AF.Identity,
                                                                 bias=shift[:, kc:kc + 1], scale=gain[:, kc:kc + 1]),
                  reads=[tk], writes=['hT'])
        else:
            S.add('act', lambda e, kc=kc, tmp=tmp: e.activation(out=out_f32[:, kc, :N], in_=tmp[:, :N], func=AF.Copy,
                                                                 scale=gain[:, kc:kc + 1]),
                  reads=[tk], writes=[out_key])


def emit_wload(S, c, off, L=WL, src=None):
    slot = c.wn % c.nw; c.wn += 1
    wb = c.wring[slot]
    S.add('pool', lambda e: e.dma_start(out=wb[:, :L], in_=(c.wpack if src is None else src)[:, off:off + L]), writes=[('w', slot)], dma=True)
    return wb, ('w', slot)


def emit_proj(S, c, wb, wk, j, Fb, N, rhs_tag='hT', kcn=KC, rhs=None, rhs_keys=None, sub=128):
    idx = c.psn % c.psmod; c.psn += 1
    ps = c.ps[idx]; pk = ('ps', idx)
    src = c.hT if rhs is None else rhs
    def fn(e):
        for kc in range(kcn):
            ins = e.matmul(ps[:, :N], wb[:, kc * Fb + j * sub: kc * Fb + j * sub + 128], src[:, kc, :N],
                           start=(kc == 0), stop=(kc == kcn - 1))
        return ins
    S.add('pe', fn, reads=[wk] + ([rhs_tag] if rhs_keys is None else list(rhs_keys)), writes=[pk])
    return ps, pk


def emit_rope(S, c, src, sk, dst, dk, N, tok_cos, tok_sin):
    idx = c.psn % c.psmod; c.psn += 1
    ps = c.ps[idx]; pk = ('ps', idx)
    S.add('pe', lambda e: e.matmul(ps[:, :N], c.rotT[:, :], src[:, :N], start=True, stop=True), reads=[sk, 'rotT'], writes=[pk])
    t1 = c.t32[3]; k1 = ('t32', 3)
    t2 = c.t32[4]; k2 = ('t32', 4)
    S.add('dve', lambda e: e.tensor_tensor(out=t1[:, :N], in0=src[:, :N], in1=tok_cos, op=ALU.mult), reads=[sk, 'rope'], writes=[k1])
    S.add('dve', lambda e: e.tensor_tensor(out=t2[:, :N], in0=ps[:, :N], in1=tok_sin, op=ALU.mult), reads=[pk, 'rope'], writes=[k2])
    S.add('dve', lambda e: e.tensor_tensor(out=dst, in0=t1[:, :N], in1=t2[:, :N], op=ALU.add), reads=[k1, k2], writes=[dk])


def emit_headnorm(S, c, ps, pk, gvec, dst, dk, N):
    sq = c.t32[5]; sk = ('t32', 5)
    S.add('act', lambda e: e.activation(out=sq[:, :N], in_=ps[:, :N], func=AF.Square), reads=[pk], writes=[sk])
    idx = c.psn % c.psmod; c.psn += 1
    ps2 = c.ps[idx]; pk2 = ('ps', idx)
    S.add('pe', lambda e: e.matmul(ps2[:, :N], c.ones32[:, :], sq[:, :N], start=True, stop=True), reads=[sk], writes=[pk2])
    rs = c.t32[6]; rk = ('t32', 6)
    S.add('act', lambda e: e.activation(out=rs[:, :N], in_=ps2[:, :N], func=AF.Sqrt, bias=float(EPS * HD * 1.0000001)), reads=[pk2], writes=[rk])
    S.add('dve', lambda e: e.reciprocal(out=rs[:, :N], in_=rs[:, :N]), reads=[rk], writes=[rk])
    S.add('dve', lambda e: e.scalar_tensor_tensor(out=dst, in0=ps[:, :N], scalar=gvec, in1=rs[:, :N], op0=ALU.mult, op1=ALU.mult),
          reads=[pk, rk], writes=[dk])


MCOLS = 1536; MFC = 12

def build_M():
    nc = bass.Bass("TRN2", target_bir_lowering=False)
    c = Ctx()
    c.wpack = nc.dram_tensor("wpack", [128, 2 * MFC * 2048], F32, kind="ExternalInput").ap()
    cT = nc.dram_tensor("cT", [128, 16 * 3], F32, kind="ExternalInput").ap()
    bT = nc.dram_tensor("bT", [128, 2 * MFC], F32, kind="ExternalInput").ap()
    out = nc.dram_tensor("mod", [128, 2 * MFC * 3], F32, kind="ExternalOutput").ap()
    S = Sched()
    with ExitStack() as st:
        sb = lambda name, shape, dt: st.enter_context(nc.sbuf_tensor(name, shape, dt))
        c.nw = 3; c.wn = 0; c.psn = 0; c.psmod = 8
        c.wring = [sb(f"w{i}", [128, 2048], BF16) for i in range(c.nw)]
        c.ps = [st.enter_context(nc.psum_tensor(f"ps{i}", [128, 512], F32)) for i in range(8)]
        c32 = sb("c32", [128, 48], F32); s32 = sb("s32", [128, 48], F32); sbf = sb("sbf", [128, 16, 3], BF16)
        b32 = sb("b32", [128, 2 * MFC], F32); o32 = sb("o32", [128, 2 * MFC, 3], F32)
        S.add('sp', lambda e: e.dma_start(out=c32[:, :], in_=cT[:, :]), writes=['c32'], dma=True)
        S.add('sp', lambda e: e.dma_start(out=b32[:, :], in_=bT[:, :]), writes=['b32'], dma=True)
        S.add('act', lambda e: e.activation(out=s32[:, :], in_=c32[:, :], func=AF.Silu), reads=['c32'], writes=['s32'])
        S.add('dve', lambda e: e.tensor_copy(out=sbf[:].rearrange("p k v -> p (k v)"), in_=s32[:, :]), reads=['s32'], writes=['sbf'])
        for i in range(2 * MFC):
            wb, wk = emit_wload(S, c, i * 2048, 2048)
            idx = c.psn % c.psmod; c.psn += 1
            ps = c.ps[idx]; pk = ('ps', idx)
            def fn(e, wb=wb, ps=ps):
                for kc in range(KC):
                    ins = e.matmul(ps[:, :3], wb[:, kc * 128:(kc + 1) * 128], sbf[:, kc, :], start=(kc == 0), stop=(kc == KC - 1))
                return ins
            S.add('pe', fn, reads=[wk, 'sbf'], writes=[pk])
            S.add('dve', lambda e, i=i, ps=ps: e.tensor_scalar(out=o32[:, i, :], in0=ps[:, :3], scalar1=b32[:, i:i + 1], scalar2=None, op0=ALU.add),
                  reads=[pk, 'b32'], writes=['o32'])
        S.add('sp', lambda e: e.dma_start(out=out[:, :], in_=o32[:].rearrange("p i v -> p (i v)")), reads=['o32'], dma=True)
        with nc.Block() as block:
            S.emit(nc, block, st)
    return nc


def inputs_M(c, c_ctx, w_ada, b_ada):
    cs = np.stack([fm(c[0]), fm(c[1]), fm(c_ctx)], axis=-1).reshape(128, 48)
    maps = []
    for r in range(8):
        blocks = []
        bs = []
        for l in range(2):
            for fc in range(MFC):
                c0 = r * MCOLS + fc * 128
                blocks.append(blk(np.asarray(w_ada[l][:, c0:c0 + 128], np.float32)))
                bs.append(np.asarray(b_ada[l][c0:c0 + 128], np.float32))
        maps.append({"wpack": np.ascontiguousarray(np.concatenate(blocks, axis=1)),
                     "cT": np.ascontiguousarray(cs), "bT": np.ascontiguousarray(np.stack(bs, axis=1))})
    return maps


def gather_M(results):
    mod = np.zeros((2, 3, 6 * D), np.float32)
    for r in range(8):
        o = results[r]["mod"].reshape(128, 2, MFC, 3)
        for l in range(2):
            for fc in range(MFC):
                c0 = r * MCOLS + fc * 128
                mod[l, :, c0:c0 + 128] = o[:, l, fc, :].T
    return mod


def vt_A():
    vt = VT()
    for nm in ("norm_mix", "sc_lat", "sh_lat", "sc_ctx", "sh_ctx"):
        vt.add(nm, 16)
    vt.add("qk_k", 1)
    return vt


def alloc_common(nc, st, c, n_t32=10, nw=3):
    sb = lambda name, shape, dt: st.enter_context(nc.sbuf_tensor(name, shape, dt))
    c.sb = sb
    c.nw = nw; c.wn = 0; c.psn = 0; c.psmod = 8
    c.wring = [sb(f"w{i}", [128, WL], BF16) for i in range(nw)]
    c.ps = [st.enter_context(nc.psum_tensor(f"ps{i}", [128, 512], F32)) for i in range(8)]
    c.t32 = [sb(f"t32_{i}", [128, 512], F32) for i in range(n_t32)]
    c.xT = sb("xT_sb", [128, KC, NQ], F32)
    c.hT = sb("hT_sb", [128, KC, NQ], BF16)
    c.ones32 = sb("ones32", [128, 128], F32)
    c.rotT = sb("rotT_sb", [128, 128], F32)
    c.cos = sb("cos", [128, NQ], F32)
    c.sin = sb("sin", [128, NQ], F32)


def emit_A_tile(S, c, A, t0, N, is_ctx, load=True):
    if load:
        if is_ctx:
            S.add('sp', lambda e: e.dma_start(out=c.xT[:, :, :CTX], in_=A.ctxT[:, :, :]), writes=['xT'], dma=True)
        else:
            S.add('sp', lambda e: e.dma_start(out=c.xT[:, :, :], in_=A.xT[:, :, t0:t0 + NQ]), writes=['xT'], dma=True)
            S.add('sp', lambda e: e.dma_start(out=c.cos[:, :], in_=A.cosT[:, t0:t0 + NQ]), writes=['rope'], dma=True)
            S.add('sp', lambda e: e.dma_start(out=c.sin[:, :], in_=A.sinT[:, t0:t0 + NQ]), writes=['rope'], dma=True)
    gain, shift = (A.gc, A.shc) if is_ctx else (A.gl, A.shl)
    gk = A.gk
    emit_norm(S, c, c.xT, c.hT, N, gain, shift, 'xT', extra_reads=['gains', 'V', 'ones'])
    for b in range(4):
        wb, wk = emit_wload(S, c, A.woff + b * WL, src=A.wpack)
        for pr in range(2):
            j = (b % 2) * 2 + pr
            ps1, pk1 = emit_proj(S, c, wb, wk, 2 * pr, 512, N)
            ps2, pk2 = emit_proj(S, c, wb, wk, 2 * pr + 1, 512, N)
            tmp = c.t32[7]; tk = ('t32', 7)
            fnc = AF.Copy if b < 2 else AF.Sigmoid
            S.add('act', lambda e, ps1=ps1, fnc=fnc, tmp=tmp: e.activation(out=tmp[:, :N], in_=ps1[:, :N], func=fnc), reads=[pk1], writes=[tk])
            ub = A.ubuf[A.un % 2]; uk = A.ukeys[A.un % 2]; A.un += 1
            S.add('dve', lambda e, ub=ub, tmp=tmp, ps2=ps2: e.tensor_tensor(out=ub[:, :N], in0=tmp[:, :N], in1=ps2[:, :N], op=ALU.mult),
                  reads=[tk, pk2], writes=[uk])
            jj = j + (0 if b < 2 else 4)
            S.add('sp', lambda e, ub=ub, jj=jj: e.dma_start(out=A.uT[:, jj, t0:t0 + N], in_=ub[:, :N]), reads=[uk], dma=True)
    wb, wk = emit_wload(S, c, A.woff + 4 * WL, src=A.wpack)
    for hh in range(4):
        ps, pk = emit_proj(S, c, wb, wk, hh, 512, N)
        kb = A.kbuf[A.kn % 2]; kk = ('kbuf', A.kn % 2); A.kn += 1
        k32 = c.t32[6]; k32k = ('t32', 6)
        if hh < 2:
            if is_ctx:
                S.add('act', lambda e, ps=ps, kb=kb: e.activation(out=kb[:, :N], in_=ps[:, :N], func=AF.Copy), reads=[pk], writes=[kk])
            else:
                S.add('act', lambda e, ps=ps, k32=k32: e.activation(out=k32[:, :N], in_=ps[:, :N], func=AF.Copy), reads=[pk], writes=[k32k])
                emit_rope(S, c, k32, k32k, kb[:, :N], kk, N, c.cos[:, :N], c.sin[:, :N])
        else:
            if is_ctx:
                emit_headnorm(S, c, ps, pk, gk[:, 0:1], kb[:, :N], kk, N)
            else:
                emit_headnorm(S, c, ps, pk, gk[:, 0:1], k32[:, :N], k32k, N)
                emit_rope(S, c, k32, k32k, kb[:, :N], kk, N, c.cos[:, :N], c.sin[:, :N])
        S.add('sp', lambda e, kb=kb, hh=hh: e.dma_start(out=A.kT[:, hh, t0:t0 + N], in_=kb[:, :N]), reads=[kk], dma=True)
    wb, wk = emit_wload(S, c, A.woff + 5 * WL, src=A.wpack)
    for s in range(N // 128):
        idx = c.psn % c.psmod; c.psn += 1
        ps = c.ps[idx]; pk = ('ps', idx)
        def fn(e, s=s, ps=ps, wb=wb):
            for kc in range(KC):
                ins = e.matmul(ps[:, :], c.hT[:, kc, s * 128:(s + 1) * 128], wb[:, kc * 512:(kc + 1) * 512], start=(kc == 0), stop=(kc == KC - 1))
            return ins
        S.add('pe', fn, reads=[wk, 'hT'], writes=[pk])
        vb = A.vbuf[A.vn % 2]; vk = ('vbuf', A.vn % 2); A.vn += 1
        S.add('act', lambda e, ps=ps, vb=vb: e.activation(out=vb[:, :], in_=ps[:, :], func=AF.Copy), reads=[pk], writes=[vk])
        S.add('sp', lambda e, vb=vb, s=s: e.dma_start(out=A.vO[t0 + s * 128:t0 + (s + 1) * 128, :], in_=vb[:, :]), reads=[vk], dma=True)


def setup_A(nc, S, c, A, V, vt, sfx=""):
    sb = c.sb
    A.gl = sb("A_gain_lat", [128, 16], F32); A.gc = sb("A_gain_ctx", [128, 16], F32); A.gk = sb("A_gk", [128, 1], F32)
    A.shl = V[:, vt.sl("sh_lat" + sfx)]; A.shc = V[:, vt.sl("sh_ctx" + sfx)]
    A.kbuf = [sb(f"kbuf{i}", [128, NQ], BF16) for i in range(2)]
    A.vbuf = [sb(f"vbuf{i}", [128, 512], BF16) for i in range(2)]
    A.un = A.kn = A.vn = 0
    if not hasattr(A, 'ukeys'):
        A.ukeys = [('ubuf', 0), ('ubuf', 1)]
    sqD = float(np.sqrt(D))
    for g, scn in ((A.gl, "sc_lat" + sfx), (A.gc, "sc_ctx" + sfx)):
        S.add('dve', lambda e, g=g, scn=scn: e.tensor_scalar(out=g[:, :], in0=V[:, vt.sl(scn)], scalar1=1.0, scalar2=sqD, op0=ALU.add, op1=ALU.mult),
              reads=['V'], writes=['gains'])
        S.add('dve', lambda e, g=g: e.tensor_tensor(out=g[:, :], in0=g[:, :], in1=V[:, vt.sl("norm_mix" + sfx)], op=ALU.mult),
              reads=['V', 'gains'], writes=['gains'])
    S.add('dve', lambda e: e.tensor_scalar(out=A.gk[:, :], in0=V[:, vt.sl("qk_k" + sfx)], scalar1=float(np.sqrt(HD)), scalar2=None, op0=ALU.mult),
          reads=['V'], writes=['gains'])


def declare_A_outputs(nc, A):
    A.uT = nc.dram_tensor("uT", [1024, TOKA], F32, kind="ExternalOutput").ap().rearrange("(j p) t -> p j t", p=128)
    A.kT = nc.dram_tensor("kT", [128, 4 * TOKA], BF16, kind="ExternalOutput").ap().rearrange("p (h t) -> p h t", h=4)
    A.vO = nc.dram_tensor("v", [TOKA, 512], BF16, kind="ExternalOutput").ap()


def build_A():
    nc = bass.Bass("TRN2", target_bir_lowering=False)
    c = Ctx(); A = Ctx()
    vt = vt_A()
    A.wpack = nc.dram_tensor("wpack", [128, 6 * WL], F32, kind="ExternalInput").ap(); A.woff = 0
    A.xT = nc.dram_tensor("xT", [D, NTOK], F32, kind="ExternalInput").ap().rearrange("(k p) t -> p k t", p=128)
    A.ctxT = nc.dram_tensor("ctxT", [D, CTX], F32, kind="ExternalInput").ap().rearrange("(k p) t -> p k t", p=128)
    vec = nc.dram_tensor("vec", [128, vt.n], F32, kind="ExternalInput").ap()
    A.cosT = nc.dram_tensor("cosT", [128, NTOK], F32, kind="ExternalInput").ap()
    A.sinT = nc.dram_tensor("sinT", [128, NTOK], F32, kind="ExternalInput").ap()
    rotT = nc.dram_tensor("rotT", [128, 128], F32, kind="ExternalInput").ap()
    declare_A_outputs(nc, A)
    S = Sched()
    with ExitStack() as st:
        alloc_common(nc, st, c)
        V = c.sb("vecs", [128, vt.n], F32)
        A.ubuf = [c.sb(f"ubuf{i}", [128, NQ], F32) for i in range(2)]
        S.add('sp', lambda e: e.dma_start(out=V[:, :], in_=vec[:, :]), writes=['V'], dma=True)
        S.add('sp', lambda e: e.dma_start(out=c.rotT[:, :], in_=rotT[:, :]), writes=['rotT'], dma=True)
        S.add('dve', lambda e: e.memset(c.ones32[:, :], 1.0), writes=['ones'])
        setup_A(nc, S, c, A, V, vt)
        for t in range(NT):
            emit_A_tile(S, c, A, t * NQ, NQ, False)
        emit_A_tile(S, c, A, NTOK, CTX, True)
        with nc.Block() as block:
            S.emit(nc, block, st)
    return nc


def vec_A(vt, l, b, mod, norm_mix, qk_norm_k):
    V = np.zeros((128, vt.n), np.float32)
    V[:, vt.sl("norm_mix")] = fm(norm_mix[l])
    V[:, vt.sl("sh_lat")] = fm(mod[l, b, 0:D]); V[:, vt.sl("sc_lat")] = fm(mod[l, b, D:2 * D])
    V[:, vt.sl("sh_ctx")] = fm(mod[l, 2, 0:D]); V[:, vt.sl("sc_ctx")] = fm(mod[l, 2, D:2 * D])
    V[:, vt.sl("qk_k")] = fm(qk_norm_k[l])
    return V


NB_G = 16
BPACK = 3 * WL + 16 * (WL + 2048) + 4 * WL + 22 * WL + 16 * 5632


def pack_B(w_in_l, w_branch_l, w_out_l, w_fi_l, w_fo_l):
    W = np.asarray(w_in_l, np.float32)
    b = [blk(W[:, OFF_BG:OFF_BG + 512]), blk(W[:, OFF_BQ:OFF_BQ + 512]), blk(W[:, OFF_DQ:OFF_DQ + 512])]
    for cc in range(16):
        b.append(blk(cols(W, [OFF_G + i * D + cc * 128 for i in range(4)])))
        b.append(np.concatenate([blk(np.asarray(w_branch_l[i][:, cc * 128:(cc + 1) * 128], np.float32)) for i in range(4)], axis=1))
    Wo = np.asarray(w_out_l, np.float32)
    for ob in range(4):
        b.append(blk(Wo[:, ob * 512:(ob + 1) * 512]))
    Wi = np.asarray(w_fi_l, np.float32); Wf = np.asarray(w_fo_l, np.float32)
    for hf in range(2):
        for fb in range(11):
            j0 = hf * 22 + 2 * fb
            b.append(blk(cols(Wi, [j0 * 128, FH + j0 * 128, (j0 + 1) * 128, FH + (j0 + 1) * 128])))
        for cp in range(8):
            b.append(blk(Wf[hf * 2816:(hf + 1) * 2816, cp * 256:(cp + 1) * 256]))
    out = np.ascontiguousarray(np.concatenate(b, axis=1))
    assert out.shape == (128, BPACK), out.shape
    return out


def vt_B(with_A1, last):
    vt = VT()
    for nm in ("norm_mix", "norm_ffn"):
        vt.add(nm, 16)
    for who in ("lat", "ctx"):
        for nm in ("sh_m", "sc_m", "g_m", "sh_f", "sc_f", "g_f"):
            vt.add(f"{nm}_{who}", 16)
    vt.add("b_gate", 64)
    vt.add("conv_a", 12)
    vt.add("conv_c", 124)
    vt.add("conv_cb", 4); vt.add("ln_g", 4); vt.add("ln_b", 4)
    vt.add("qk_q", 1); vt.add("sink", 4)
    if last:
        vt.add("norm_final", 16)
    if with_A1:
        for nm in ("norm_mix1", "sc_lat1", "sh_lat1", "sc_ctx1", "sh_ctx1"):
            vt.add(nm, 16)
        vt.add("qk_k1", 1)
    return vt


def vec_B(vt, l, b, mod, inp, with_A1, last):
    V = np.zeros((128, vt.n), np.float32)
    V[:, vt.sl("norm_mix")] = fm(inp['norm_mix'][l]); V[:, vt.sl("norm_ffn")] = fm(inp['norm_ffn'][l])
    for who, mv in (("lat", mod[l, b]), ("ctx", mod[l, 2])):
        for i, nm in enumerate(("sh_m", "sc_m", "g_m", "sh_f", "sc_f", "g_f")):
            V[:, vt.sl(f"{nm}_{who}")] = fm(mv[i * D:(i + 1) * D])
    V[:, vt.sl("b_gate")] = np.concatenate([fm(inp['b_gate'][l][i]) for i in range(4)], axis=1)
    V[:, vt.sl("conv_a")] = np.concatenate([fm(inp['conv_a_w'][l][j]) for j in range(3)], axis=1)
    V[:, vt.sl("conv_c")] = np.concatenate([fm(inp['conv_c_w'][l][j]) for j in range(31)], axis=1)
    V[:, vt.sl("conv_cb")] = fm(inp['conv_c_b'][l]); V[:, vt.sl("ln_g")] = fm(inp['ln_c_g'][l]); V[:, vt.sl("ln_b")] = fm(inp['ln_c_b'][l])
    V[:, vt.sl("qk_q")] = fm(inp['qk_norm_q'][l])
    V[:, vt.sl("sink")] = np.broadcast_to(np.asarray(inp['sink_b'][l], np.float32)[None, :], (128, 4))
    if last:
        V[:, vt.sl("norm_final")] = fm(inp['norm_final'])
    if with_A1:
        l1 = l + 1
        V[:, vt.sl("norm_mix1")] = fm(inp['norm_mix'][l1])
        V[:, vt.sl("sh_lat1")] = fm(mod[l1, b, 0:D]); V[:, vt.sl("sc_lat1")] = fm(mod[l1, b, D:2 * D])
        V[:, vt.sl("sh_ctx1")] = fm(mod[l1, 2, 0:D]); V[:, vt.sl("sc_ctx1")] = fm(mod[l1, 2, D:2 * D])
        V[:, vt.sl("qk_k1")] = fm(inp['qk_norm_k'][l1])
    return V


def band_masks(first, last_):
    kk = np.arange(128)[:, None]; q = np.arange(512)[None, :]
    m = np.zeros((128, 8, 512), np.float32)
    for j in range(6):
        m[:, j, :] = (np.abs((j - 1) * 128 + kk - q) <= 128)
    m[:, 6, :] = 0.0 if first else m[:, 0, :]
    m[:, 7, :] = 0.0 if last_ else m[:, 5, :]
    return m.reshape(128, 8 * 512).astype(NPBF)


class Attn:
    def __init__(self, S, c, N):
        self.S, self.c, self.N, self.pend = S, c, N, None

    def push(self, q, qk, KT, V, kvk, acc, first, last, mask=None):
        S, c, N = self.S, self.c, self.N
        idx = c.psn % c.psmod; c.psn += 1
        psS = c.ps[idx]; sk = ('ps', idx)
        S.add('pe', lambda e: e.matmul(psS[:, :N], KT, q, start=True, stop=True), reads=[qk] + list(kvk), writes=[sk])
        pi = c.pn % len(c.P); c.pn += 1
        P = c.P[pi]; pk = ('P', pi)
        S.add('act', lambda e: e.activation(out=P[:, :N], in_=psS[:, :N], func=AF.Exp), reads=[sk], writes=[pk])
        if mask is not None:
            S.add('dve', lambda e: e.tensor_tensor(out=P[:, :N], in0=P[:, :N], in1=mask, op=ALU.mult), reads=[pk, 'masks'], writes=[pk])
        prev = self.pend
        self.pend = (P, pk, V, list(kvk), acc, first, last)
        if prev is not None:
            self._pv(prev)

    def _pv(self, it):
        S, c, N = self.S, self.c, self.N
        P, pk, V, kvk, acc, first, last = it
        psO = c.ps[4 + 2 * acc]; ok = ('ps', 4 + 2 * acc)
        psZ = c.ps[5 + 2 * acc]; zk = ('ps', 5 + 2 * acc)
        S.add('pe', lambda e: e.matmul(psO[:, :N], V, P[:, :N], start=first, stop=last), reads=[pk] + kvk, writes=[ok])
        S.add('pe', lambda e: e.matmul(psZ[:, :N], c.onesbf[:, :], P[:, :N], start=first, stop=last), reads=[pk, 'ones'], writes=[zk])

    def flush(self):
        if self.pend is not None:
            self._pv(self.pend)
            self.pend = None

    def final(self, acc, sink, out, out_key):
        S, c, N = self.S, self.c, self.N
        psO = c.ps[4 + 2 * acc]; ok = ('ps', 4 + 2 * acc)
        psZ = c.ps[5 + 2 * acc]; zk = ('ps', 5 + 2 * acc)
        z = c.t32[5]; zkk = ('t32', 5)
        if sink is not None:
            S.add('dve', lambda e: e.tensor_scalar(out=z[:, :N], in0=psZ[:, :N], scalar1=sink, scalar2=None, op0=ALU.add), reads=[zk, 'esink'], writes=[zkk])
        else:
            S.add('dve', lambda e: e.tensor_copy(out=z[:, :N], in_=psZ[:, :N]), reads=[zk], writes=[zkk])
        S.add('dve', lambda e: e.reciprocal(out=z[:, :N], in_=z[:, :N]), reads=[zkk], writes=[zkk])
        S.add('dve', lambda e: e.tensor_tensor(out=out, in0=psO[:, :N], in1=z[:, :N], op=ALU.mult), reads=[ok, zkk], writes=[out_key])


def build_B(last=False, with_A1=False, tiles=None, do_ctx=True):
    nc = bass.Bass("TRN2", target_bir_lowering=False)
    c = Ctx(); A = Ctx()
    vt = vt_B(with_A1, last)
    dt = lambda name, shape, dty, kind="ExternalInput": nc.dram_tensor(name, shape, dty, kind=kind).ap()
    c.wpack = dt("wpack", [128, BPACK], F32)
    xT = dt("xT", [D, NTOK], F32).rearrange("(k p) t -> p k t", p=128)
    ctxT = dt("ctxT", [D, CTX], F32).rearrange("(k p) t -> p k t", p=128)
    vec = dt("vec", [128, vt.n], F32)
    cosT = dt("cosT", [128, NTOK], F32); sinT = dt("sinT", [128, NTOK], F32); rotT = dt("rotT", [128, 128], F32)
    masksD = dt("masks", [128, 8 * 512], BF16)
    kTg = dt("kTg", [128, 2 * SEQ], BF16).rearrange("p (g t) -> p g t", g=2)
    vg = dt("vg", [2 * NB_G * 128, 1024], BF16).rearrange("(g n p) f -> g n p f", g=2, n=NB_G)
    kTw = dt("kTw", [128, 2 * (NTOK + 256)], BF16).rearrange("p (g t) -> p g t", g=2)
    vw = dt("vw", [2 * (NTOK + 256), 128], BF16).rearrange("(g t) d -> g t d", g=2)
    kTc = dt("kTc", [128, 4 * CTX], BF16)
    vcD = dt("vc", [CTX, 512], BF16)
    uh = dt("uh", [1024, NTOK + 30], F32).rearrange("(j p) t -> p j t", p=128)
    uhc = dt("uhc", [1024, CTX + 30], F32).rearrange("(j p) t -> p j t", p=128)
    xo = dt("xo", [D, NTOK], F32, kind="ExternalOutput").rearrange("(k p) t -> p k t", p=128)
    if not last:
        co = dt("co", [D, CTX], F32, kind="ExternalOutput").rearrange("(k p) t -> p k t", p=128)
    if with_A1:
        A.wpack = dt("wpackA", [128, 6 * WL], F32); A.woff = 0
        declare_A_outputs(nc, A)
    S = Sched()
    with ExitStack() as st:
        alloc_common(nc, st, c, n_t32=8, nw=2)
        c.psmod = 4
        sb = c.sb
        V = sb("vecs", [128, vt.n], F32)
        c.wsm = [sb(f"wsm{i}", [128, 2048], BF16) for i in range(2)]; c.wsn = 0
        R1 = sb("R1", [128, 32, NQ], BF16)
        c.onesbf = sb("onesbf", [128, 128], BF16)
        masks = sb("masks_sb", [128, 8, 512], BF16)
        qb = sb("qb", [128, 4, NQ], BF16); qd = sb("qd", [128, 4, NQ], BF16)
        kring = [sb(f"kring{i}", [128, 1024], BF16) for i in range(2)]
        vring = [sb(f"vring{i}", [128, 8, 128], BF16) for i in range(2)]
        kw = sb("kw", [128, 2, 768], BF16); vws = sb("vws", [128, 2, 6, 128], BF16)
        kcs = sb("kcs", [128, 4, CTX], BF16); vcs = sb("vcs", [128, 2, 512], BF16)
        c.P = [sb(f"P{i}", [128, NQ], BF16) for i in range(2)]; c.pn = 0
        ua = [sb(f"ua{i}", [128, NQ + 2], F32) for i in range(2)]
        uc = [sb(f"uc{i}", [128, NQ + 30], F32) for i in range(2)]
        hcv = sb("hcv", [128, 4, NQ], F32)
        gm = {w: sb(f"gain_m_{w}", [128, 16], F32) for w in ("lat", "ctx")}
        gf = {w: sb(f"gain_f_{w}", [128, 16], F32) for w in ("lat", "ctx")}
        gq = sb("gq", [128, 1], F32); esink = sb("esink", [128, 4], F32)
        if last:
            gfin = sb("gfin", [128, 16], F32)
        S.add('sp', lambda e: e.dma_start(out=V[:, :], in_=vec[:, :]), writes=['V'], dma=True)
        S.add('sp', lambda e: e.dma_start(out=c.rotT[:, :], in_=rotT[:, :]), writes=['rotT'], dma=True)
        S.add('sp', lambda e: e.dma_start(out=masks[:].rearrange("p a b -> p (a b)"), in_=masksD[:, :]), writes=['masks'], dma=True)
        S.add('sp', lambda e: e.dma_start(out=kcs[:].rearrange("p a b -> p (a b)"), in_=kTc[:, :]), writes=['kvc'], dma=True)
        S.add('sp', lambda e: e.dma_start(out=vcs[:], in_=vcD.rearrange("(c p) f -> p c f", p=128)), writes=['kvc'], dma=True)
        S.add('dve', lambda e: e.memset(c.ones32[:, :], 1.0), writes=['ones'])
        S.add('dve', lambda e: e.memset(c.onesbf[:, :], 1.0), writes=['ones'])
        sqD = float(np.sqrt(D))
        for w in ("lat", "ctx"):
            for g, scn, nn in ((gm[w], f"sc_m_{w}", "norm_mix"), (gf[w], f"sc_f_{w}", "norm_ffn")):
                S.add('dve', lambda e, g=g, scn=scn: e.tensor_scalar(out=g[:, :], in0=V[:, vt.sl(scn)], scalar1=1.0, scalar2=sqD, op0=ALU.add, op1=ALU.mult),
                      reads=['V'], writes=['gains'])
                S.add('dve', lambda e, g=g, nn=nn: e.tensor_tensor(out=g[:, :], in0=g[:, :], in1=V[:, vt.sl(nn)], op=ALU.mult),
                      reads=['V', 'gains'], writes=['gains'])
        S.add('dve', lambda e: e.tensor_scalar(out=gq[:, :], in0=V[:, vt.sl("qk_q")], scalar1=float(np.sqrt(HD) * SCALE), scalar2=None, op0=ALU.mult),
              reads=['V'], writes=['gains'])
        S.add('act', lambda e: e.activation(out=esink[:, :], in_=V[:, vt.sl("sink")], func=AF.Exp), reads=['V'], writes=['esink'])
        if last:
            S.add('dve', lambda e: e.tensor_scalar(out=gfin[:, :], in0=V[:, vt.sl("norm_final")], scalar1=sqD, scalar2=None, op0=ALU.mult),
                  reads=['V'], writes=['gains'])
        if with_A1:
            A.ubuf = [c.t32[0], c.t32[1]]; A.ukeys = [('t32', 0), ('t32', 1)]
            setup_A(nc, S, c, A, V, vt, sfx="1")
            A.ubuf_keys = True
        WOFF = [0]

        def wnext(L=WL):
            off = WOFF[0]; WOFF[0] += L
            return off

        def wsmall(off):
            slot = c.wsn % 2; c.wsn += 1
            wbs = c.wsm[slot]
            S.add('pool', lambda e: e.dma_start(out=wbs[:, :], in_=c.wpack[:, off:off + 2048]), writes=[('wsm', slot)], dma=True)
            return wbs, ('wsm', slot)

        def do_tile(ti, t0, N, is_ctx):
            who = "ctx" if is_ctx else "lat"
            WOFF[0] = 0
            vs = lambda nm, i=None: V[:, vt.sl(f"{nm}_{who}", i)]
            if is_ctx:
                S.add('sp', lambda e: e.dma_start(out=c.xT[:, :, :CTX], in_=ctxT[:, :, :]), writes=['xT'], dma=True)
            else:
                S.add('sp', lambda e: e.dma_start(out=c.xT[:, :, :], in_=xT[:, :, t0:t0 + NQ]), writes=['xT'], dma=True)
                S.add('sp', lambda e: e.dma_start(out=c.cos[:, :], in_=cosT[:, t0:t0 + NQ]), writes=['rope'], dma=True)
                S.add('sp', lambda e: e.dma_start(out=c.sin[:, :], in_=sinT[:, t0:t0 + NQ]), writes=['rope'], dma=True)
            emit_norm(S, c, c.xT, c.hT, N, gm[who], vs("sh_m"), 'xT', extra_reads=['gains', 'V', 'ones'])
            wb, wk = emit_wload(S, c, wnext())
            usrc = uhc if is_ctx else uh
            for j in range(4):
                ub = ua[j % 2]; uk = ('ua', j % 2)
                S.add('sp', lambda e, ub=ub, j=j: e.dma_start(out=ub[:, :N + 2], in_=usrc[:, j, t0 + 14:t0 + 14 + N + 2]), writes=[uk], dma=True)
                acc = c.t32[3]; ak = ('t32', 3)
                for tap in range(3):
                    wcol = V[:, vt.off["conv_a"][0] + tap * 4 + j: vt.off["conv_a"][0] + tap * 4 + j + 1]
                    if tap == 0:
                        S.add('dve', lambda e, ub=ub, wcol=wcol: e.tensor_scalar(out=acc[:, :N], in0=ub[:, 0:N], scalar1=wcol, scalar2=None, op0=ALU.mult),
                              reads=[uk, 'V'], writes=[ak])
                    else:
                        S.add('dve', lambda e, ub=ub, wcol=wcol, tap=tap: e.scalar_tensor_tensor(out=acc[:, :N], in0=ub[:, tap:tap + N], scalar=wcol, in1=acc[:, :N], op0=ALU.mult, op1=ALU.add),
                              reads=[uk, 'V', ak], writes=[ak])
                ps, pk = emit_proj(S, c, wb, wk, j, 512, N)
                S.add('dve', lambda e, ps=ps, j=j: e.tensor_tensor(out=R1[:, j, :N], in0=acc[:, :N], in1=ps[:, :N], op=ALU.mult), reads=[ak, pk], writes=[('r1', j)])
            wb, wk = emit_wload(S, c, wnext())
            q32 = c.t32[6]; q32k = ('t32', 6)
            for hq in range(4):
                ps, pk = emit_proj(S, c, wb, wk, hq, 512, N)
                if is_ctx:
                    S.add('act', lambda e, ps=ps, hq=hq: e.activation(out=qb[:, hq, :N], in_=ps[:, :N], func=AF.Copy, scale=float(SCALE)), reads=[pk], writes=[('qb', hq)])
                else:
                    S.add('act', lambda e, ps=ps: e.activation(out=q32[:, :N], in_=ps[:, :N], func=AF.Copy, scale=float(SCALE)), reads=[pk], writes=[q32k])
                    emit_rope(S, c, q32, q32k, qb[:, hq, :N], ('qb', hq), N, c.cos[:, :N], c.sin[:, :N])
            wb, wk = emit_wload(S, c, wnext())
            for hq in range(4):
                ps, pk = emit_proj(S, c, wb, wk, hq, 512, N)
                if is_ctx:
                    emit_headnorm(S, c, ps, pk, gq[:, 0:1], qd[:, hq, :N], ('qd', hq), N)
                else:
                    emit_headnorm(S, c, ps, pk, gq[:, 0:1], q32[:, :N], q32k, N)
                    emit_rope(S, c, q32, q32k, qd[:, hq, :N], ('qd', hq), N, c.cos[:, :N], c.sin[:, :N])
            for j in range(4):
                ub = uc[j % 2]; uk = ('uc', j % 2)
                S.add('sp', lambda e, ub=ub, j=j: e.dma_start(out=ub[:, :N + 30], in_=usrc[:, 4 + j, t0:t0 + N + 30]), writes=[uk], dma=True)
                hk = ('hcv', j)
                for tap in range(31):
                    wcol = V[:, vt.off["conv_c"][0] + tap * 4 + j: vt.off["conv_c"][0] + tap * 4 + j + 1]
                    if tap == 0:
                        S.add('dve', lambda e, ub=ub, wcol=wcol, j=j: e.tensor_scalar(out=hcv[:, j, :N], in0=ub[:, 0:N], scalar1=wcol, scalar2=V[:, vt.sl("conv_cb", j)], op0=ALU.mult, op1=ALU.add),
                              reads=[uk, 'V'], writes=[hk])
                    else:
                        S.add('dve', lambda e, ub=ub, wcol=wcol, tap=tap, j=j: e.scalar_tensor_tensor(out=hcv[:, j, :N], in0=ub[:, tap:tap + N], scalar=wcol, in1=hcv[:, j, :N], op0=ALU.mult, op1=ALU.add),
                                  reads=[uk, 'V', hk], writes=[hk])
            i1 = c.psn % c.psmod; c.psn += 1
            i2 = c.psn % c.psmod; c.psn += 1
            psM, mk_ = c.ps[i1], ('ps', i1); psQ, qk_ = c.ps[i2], ('ps', i2)
            for j in range(4):
                S.add('pe', lambda e, j=j: e.matmul(psM[:, :N], c.ones32[:, :], hcv[:, j, :N], start=(j == 0), stop=(j == 3)), reads=[('hcv', j), 'ones'], writes=[mk_])
                sq = c.t32[j % 2]; sk = ('t32', j % 2)
                S.add('act', lambda e, j=j, sq=sq: e.activation(out=sq[:, :N], in_=hcv[:, j, :N], func=AF.Square), reads=[('hcv', j)], writes=[sk])
                S.add('pe', lambda e, j=j, sq=sq: e.matmul(psQ[:, :N], c.ones32[:, :], sq[:, :N], start=(j == 0), stop=(j == 3)), reads=[sk, 'ones'], writes=[qk_])
            m32 = c.t32[2]; m32k = ('t32', 2); v32 = c.t32[3]; v32k = ('t32', 3); t4 = c.t32[4]; t4k = ('t32', 4)
            S.add('dve', lambda e: e.tensor_scalar(out=m32[:, :N], in0=psM[:, :N], scalar1=1.0 / 512, scalar2=None, op0=ALU.mult), reads=[mk_], writes=[m32k])
            S.add('dve', lambda e: e.tensor_tensor(out=t4[:, :N], in0=m32[:, :N], in1=m32[:, :N], op=ALU.mult), reads=[m32k], writes=[t4k])
            S.add('dve', lambda e: e.scalar_tensor_tensor(out=v32[:, :N], in0=psQ[:, :N], scalar=1.0 / 512, in1=t4[:, :N], op0=ALU.mult, op1=ALU.subtract), reads=[qk_, t4k], writes=[v32k])
            S.add('act', lambda e: e.activation(out=v32[:, :N], in_=v32[:, :N], func=AF.Sqrt, bias=float(EPS)), reads=[v32k], writes=[v32k])
            S.add('dve', lambda e: e.reciprocal(out=v32[:, :N], in_=v32[:, :N]), reads=[v32k], writes=[v32k])
            for j in range(4):
                S.add('dve', lambda e, j=j: e.tensor_tensor(out=t4[:, :N], in0=hcv[:, j, :N], in1=m32[:, :N], op=ALU.subtract), reads=[('hcv', j), m32k], writes=[t4k])
                S.add('dve', lambda e: e.tensor_tensor(out=t4[:, :N], in0=t4[:, :N], in1=v32[:, :N], op=ALU.mult), reads=[t4k, v32k], writes=[t4k])
                S.add('act', lambda e, j=j: e.activation(out=R1[:, 8 + j, :N], in_=t4[:, :N], func=AF.Silu, bias=V[:, vt.sl("ln_b", j)], scale=V[:, vt.sl("ln_g", j)]),
                      reads=[t4k, 'V'], writes=[('r1', 8 + j)])
            at = Attn(S, c, N)
            if not is_ctx:
                for g in range(2):
                    S.add('sp', lambda e, g=g: e.dma_start(out=kw[:, g, :], in_=kTw[:, g, t0:t0 + 768]), writes=[('kw', g)], dma=True)
                    S.add('sp', lambda e, g=g: e.dma_start(out=vws[:, g, :, :], in_=vw[g, t0:t0 + 768, :].rearrange("(c p) d -> p c d", p=128)), writes=[('kw', g)], dma=True)
                for g in range(2):
                    for hq in (2 * g, 2 * g + 1):
                        for j in range(6):
                            mi = 6 if (j == 0 and ti == 0) else (7 if (j == 5 and ti == NT - 1) else j)
                            at.push(qb[:, hq, :N], ('qb', hq), kw[:, g, j * 128:(j + 1) * 128], vws[:, g, j, :], [('kw', g)], hq % 2, j == 0, False, mask=masks[:, mi, :N])
                        for ch in range(2):
                            at.push(qb[:, hq, :N], ('qb', hq), kcs[:, g, ch * 128:(ch + 1) * 128], vcs[:, ch, g * 128:(g + 1) * 128], ['kvc'], hq % 2, False, ch == 1)
                    at.flush()
                    for hq in (2 * g, 2 * g + 1):
                        at.final(hq % 2, esink[:, hq:hq + 1], R1[:, 4 + hq, :N], ('r1', 4 + hq))
                for g in range(2):
                    for cg in range(NB_G):
                        ri = c.kvn % 2; c.kvn += 1
                        kr, vr, kvk = kring[ri], vring[ri], ('kvr', ri)
                        S.add('sp', lambda e, kr=kr, g=g, cg=cg: e.dma_start(out=kr[:, :], in_=kTg[:, g, cg * 1024:(cg + 1) * 1024]), writes=[kvk], dma=True)
                        S.add('sp', lambda e, vr=vr, g=g, cg=cg: e.dma_start(out=vr[:].rearrange("p a b -> p (a b)"), in_=vg[g, cg, :, :]), writes=[kvk], dma=True)
                        for ch in range(8):
                            for hq in (2 * g, 2 * g + 1):
                                at.push(qd[:, hq, :N], ('qd', hq), kr[:, ch * 128:(ch + 1) * 128], vr[:, ch, :], [kvk], hq % 2, cg == 0 and ch == 0, False)
                    for ch in range(2):
                        for hq in (2 * g, 2 * g + 1):
                            at.push(qd[:, hq, :N], ('qd', hq), kcs[:, 2 + g, ch * 128:(ch + 1) * 128], vcs[:, ch, (2 + g) * 128:(3 + g) * 128], ['kvc'], hq % 2, False, ch == 1)
                    at.flush()
                    for hq in (2 * g, 2 * g + 1):
                        at.final(hq % 2, None, R1[:, 12 + hq, :N], ('r1', 12 + hq))
            else:
                for br, qq, qn, ko, r0 in ((1, qb, 'qb', 0, 4), (3, qd, 'qd', 2, 12)):
                    for g in range(2):
                        for hq in (2 * g, 2 * g + 1):
                            for ch in range(2):
                                at.push(qq[:, hq, :N], (qn, hq), kcs[:, ko + g, ch * 128:(ch + 1) * 128], vcs[:, ch, (ko + g) * 128:(ko + g + 1) * 128], ['kvc'], hq % 2, ch == 0, ch == 1)
                        at.flush()
                        for hq in (2 * g, 2 * g + 1):
                            at.final(hq % 2, esink[:, hq:hq + 1] if br == 1 else None, R1[:, r0 + hq, :N], ('r1', r0 + hq))
            for cc in range(16):
                wb, wk = emit_wload(S, c, wnext())
                wbs, wsk = wsmall(wnext(2048))
                acc = c.t32[0]; ak = ('t32', 0)
                for i in range(4):
                    psG, gk_ = emit_proj(S, c, wb, wk, i, 512, N)
                    psB, bk_ = emit_proj(S, c, wbs[:, i * 512:(i + 1) * 512], wsk, 0, 128, N, kcn=4, rhs=R1[:, 4 * i:4 * i + 4, :], rhs_keys=[('r1', 4 * i + k) for k in range(4)])
                    g32 = c.t32[1]; g32k = ('t32', 1)
                    S.add('act', lambda e, psG=psG, i=i, cc=cc: e.activation(out=g32[:, :N], in_=psG[:, :N], func=AF.Sigmoid, bias=V[:, vt.sl("b_gate", i * 16 + cc)]),
                          reads=[gk_, 'V'], writes=[g32k])
                    if i == 0:
                        S.add('dve', lambda e, psB=psB: e.tensor_tensor(out=acc[:, :N], in0=g32[:, :N], in1=psB[:, :N], op=ALU.mult), reads=[g32k, bk_], writes=[ak])
                    else:
                        S.add('dve', lambda e, psB=psB: e.tensor_tensor(out=g32[:, :N], in0=g32[:, :N], in1=psB[:, :N], op=ALU.mult), reads=[g32k, bk_], writes=[g32k])
                        if i < 3:
                            S.add('dve', lambda e: e.tensor_tensor(out=acc[:, :N], in0=acc[:, :N], in1=g32[:, :N], op=ALU.add), reads=[ak, g32k], writes=[ak])
                        else:
                            S.add('dve', lambda e, cc=cc: e.tensor_tensor(out=R1[:, 16 + cc, :N], in0=acc[:, :N], in1=g32[:, :N], op=ALU.add), reads=[ak, g32k], writes=[('r1', 16 + cc)])
            mkeys = [('r1', 16 + k) for k in range(16)]
            for ob in range(4):
                wb, wk = emit_wload(S, c, wnext())
                for jj in range(4):
                    cc = ob * 4 + jj
                    ps, pk = emit_proj(S, c, wb, wk, jj, 512, N, rhs=R1[:, 16:32, :], rhs_keys=mkeys)
                    S.add('dve', lambda e, ps=ps, cc=cc: e.scalar_tensor_tensor(out=c.xT[:, cc, :N], in0=ps[:, :N], scalar=vs("g_m", cc), in1=c.xT[:, cc, :N], op0=ALU.mult, op1=ALU.add),
                          reads=[pk, 'xT', 'V'], writes=['xT'])
            emit_norm(S, c, c.xT, c.hT, N, gf[who], vs("sh_f"), 'xT')
            for hf in range(2):
                for fb in range(11):
                    wb, wk = emit_wload(S, c, wnext())
                    for pr in range(2):
                        jl = 2 * fb + pr
                        psA, ak_ = emit_proj(S, c, wb, wk, 2 * pr, 512, N)
                        psB, bk_ = emit_proj(S, c, wb, wk, 2 * pr + 1, 512, N)
                        s32 = c.t32[jl % 2]; sk = ('t32', jl % 2)
                        S.add('act', lambda e, psA=psA, s32=s32: e.activation(out=s32[:, :N], in_=psA[:, :N], func=AF.Silu), reads=[ak_], writes=[sk])
                        S.add('dve', lambda e, psB=psB, s32=s32, jl=jl: e.tensor_tensor(out=R1[:, jl, :N], in0=s32[:, :N], in1=psB[:, :N], op=ALU.mult), reads=[sk, bk_], writes=[('r1', jl)])
                ukeys = [('r1', k) for k in range(22)]
                for cp in range(8):
                    wb, wk = emit_wload(S, c, wnext(5632), L=5632)
                    for o in range(2):
                        cc = cp * 2 + o
                        ps, pk = emit_proj(S, c, wb, wk, o, 256, N, kcn=22, rhs=R1[:, 0:22, :], rhs_keys=ukeys)
                        S.add('dve', lambda e, ps=ps, cc=cc: e.scalar_tensor_tensor(out=c.xT[:, cc, :N], in0=ps[:, :N], scalar=vs("g_f", cc), in1=c.xT[:, cc, :N], op0=ALU.mult, op1=ALU.add),
                              reads=[pk, 'xT', 'V'], writes=['xT'])
            assert WOFF[0] == BPACK, (WOFF[0], BPACK)
            if last:
                emit_norm(S, c, c.xT, None, N, gfin, None, 'xT', out_f32=c.xT, out_key='xT')
                S.add('sp', lambda e: e.dma_start(out=xo[:, :, t0:t0 + N], in_=c.xT[:, :, :N]), reads=['xT'], dma=True)
            else:
                if is_ctx:
                    S.add('sp', lambda e: e.dma_start(out=co[:, :, :], in_=c.xT[:, :, :CTX]), reads=['xT'], dma=True)
                else:
                    S.add('sp', lambda e: e.dma_start(out=xo[:, :, t0:t0 + N], in_=c.xT[:, :, :N]), reads=['xT'], dma=True)
                if with_A1:
                    emit_A_tile(S, c, A, NTOK if is_ctx else t0, N, is_ctx, load=False)

        c.kvn = 0
        for t in (range(NT) if tiles is None else tiles):
            do_tile(t, t * NQ, NQ, False)
        if not last and do_ctx:
            do_tile(NT, 0, CTX, True)
        with nc.Block() as block:
            S.emit(nc, block, st)
    c.nops = len(S.ops)
    return nc


def bf(a):
    return np.ascontiguousarray(a)


def inputs_B(l, inp, mod, xTs, ctxTs, Aout, with_A1, last, wpB, wpA1=None):
    vt = vt_B(with_A1, last)
    cosT, sinT = rope_tables(); rT = rot_matrix_T()
    maps = []
    kT = [np.asarray(Aout[r]["kT"]).reshape(128, 4, TOKA) for r in range(8)]
    vv = [np.asarray(Aout[r]["v"]) for r in range(8)]
    uT = [np.asarray(Aout[r]["uT"]) for r in range(8)]
    per_batch = {}
    for b in range(2):
        rs = range(4 * b, 4 * b + 4)
        kTg = np.concatenate([kT[r][:, 2:4, :NTOK] for r in rs], axis=2)
        vfull = np.concatenate([vv[r][:NTOK] for r in rs], axis=0)
        vg = np.stack([vfull[:, 256 + g * 128:256 + (g + 1) * 128].reshape(NB_G, 8, 128, 128).transpose(0, 2, 1, 3).reshape(NB_G * 128, 1024) for g in range(2)], 0)
        kb = np.concatenate([kT[r][:, 0:2, :NTOK] for r in rs], axis=2)
        kbp = np.zeros((128, 2, SEQ + 256), kb.dtype); kbp[:, :, 128:128 + SEQ] = kb
        vbp = np.zeros((2, SEQ + 256, 128), vfull.dtype)
        for g in range(2):
            vbp[g, 128:128 + SEQ] = vfull[:, g * 128:(g + 1) * 128]
        ufull = np.concatenate([uT[r][:, :NTOK] for r in rs], axis=1)
        up = np.zeros((1024, SEQ + 30), np.float32); up[:, 15:15 + SEQ] = ufull
        uhc = np.zeros((1024, CTX + 30), np.float32); uhc[:, 15:15 + CTX] = uT[4 * b][:, NTOK:]
        per_batch[b] = dict(kTg=bf(kTg.reshape(128, 2 * SEQ)), vg=bf(vg.reshape(2 * NB_G * 128, 1024)), kbp=kbp, vbp=vbp, up=up, uhc=uhc,
                            kTc=bf(kT[4 * b][:, :, NTOK:].reshape(128, 4 * CTX)), vc=bf(vv[4 * b][NTOK:]))
    for r in range(8):
        b = r // 4; T0 = (r % 4) * NTOK
        pb = per_batch[b]
        m = {"wpack": wpB, "xT": xTs[r], "ctxT": ctxTs[b], "vec": vec_B(vt, l, b, mod, inp, with_A1, last),
             "cosT": bf(cosT[:, T0:T0 + NTOK]), "sinT": bf(sinT[:, T0:T0 + NTOK]), "rotT": rT,
             "masks": band_masks(r % 4 == 0, r % 4 == 3),
             "kTg": pb["kTg"], "vg": pb["vg"],
             "kTw": bf(pb["kbp"][:, :, T0:T0 + NTOK + 256].reshape(128, 2 * (NTOK + 256))),
             "vw": bf(pb["vbp"][:, T0:T0 + NTOK + 256].reshape(2 * (NTOK + 256), 128)),
             "kTc": pb["kTc"], "vc": pb["vc"],
             "uh": bf(pb["up"][:, T0:T0 + NTOK + 30]), "uhc": pb["uhc"]}
        if with_A1:
            m["wpackA"] = wpA1
        maps.append(m)
    return maps


_CACHE = {}


def _prog(name, fn):
    if name not in _CACHE:
        _CACHE[name] = fn()
    return _CACHE[name]


def kernel(x, c, ctx, c_ctx, w_ada, b_ada, norm_mix, norm_ffn, w_in, b_gate, conv_a_w, sink_b,
           qk_norm_q, qk_norm_k, conv_c_w, conv_c_b, ln_c_g, ln_c_b, w_branch, w_out,
           w_ffn_in, w_ffn_out, norm_final):
    inp = dict(x=x, c=c, ctx=ctx, c_ctx=c_ctx, w_ada=w_ada, b_ada=b_ada, norm_mix=norm_mix, norm_ffn=norm_ffn, w_in=w_in,
               b_gate=b_gate, conv_a_w=conv_a_w, sink_b=sink_b, qk_norm_q=qk_norm_q, qk_norm_k=qk_norm_k, conv_c_w=conv_c_w,
               conv_c_b=conv_c_b, ln_c_g=ln_c_g, ln_c_b=ln_c_b, w_branch=w_branch, w_out=w_out, w_ffn_in=w_ffn_in,
               w_ffn_out=w_ffn_out, norm_final=norm_final)
    inp = {k: np.asarray(v, np.float32) for k, v in inp.items()}
    x = inp['x']; ctx = inp['ctx']
    cores = list(range(8))
    res = run_bass_kernel_spmd(_prog('M', build_M), inputs_M(inp['c'], inp['c_ctx'], inp['w_ada'], inp['b_ada']), core_ids=cores)
    mod = gather_M(res.results)
    vtA = vt_A()
    cosT, sinT = rope_tables(); rT = rot_matrix_T()
    xTs = [np.ascontiguousarray(x[r // 4, (r % 4) * NTOK:(r % 4 + 1) * NTOK].T) for r in range(8)]
    ctxTs = [np.ascontiguousarray(ctx[b].T) for b in range(2)]
    wpA0 = pack_A(inp['w_in'][0])
    maps = []
    for r in range(8):
        b = r // 4; t0 = (r % 4) * NTOK
        maps.append({"wpack": wpA0, "xT": xTs[r], "ctxT": ctxTs[b], "vec": vec_A(vtA, 0, b, mod, inp['norm_mix'], inp['qk_norm_k']),
                     "cosT": np.ascontiguousarray(cosT[:, t0:t0 + NTOK]), "sinT": np.ascontiguousarray(sinT[:, t0:t0 + NTOK]), "rotT": rT})
    res = run_bass_kernel_spmd(_prog('A', build_A), maps, core_ids=cores)
    Aout = [dict(uT=res.results[r]["uT"], kT=res.results[r]["kT"], v=res.results[r]["v"]) for r in range(8)]
    del maps, wpA0
    wpB = pack_B(inp['w_in'][0], inp['w_branch'][0], inp['w_out'][0], inp['w_ffn_in'][0], inp['w_ffn_out'][0])
    wpA1 = pack_A(inp['w_in'][1])
    maps = inputs_B(0, inp, mod, xTs, ctxTs, Aout, True, False, wpB, wpA1)
    res = run_bass_kernel_spmd(_prog('B0', lambda: build_B(last=False, with_A1=True)), maps, core_ids=cores)
    xTs = [np.ascontiguousarray(res.results[r]["xo"]) for r in range(8)]
    ctxTs = [np.ascontiguousarray(res.results[4 * b]["co"]) for b in range(2)]
    Aout = [dict(uT=res.results[r]["uT"], kT=res.results[r]["kT"], v=res.results[r]["v"]) for r in range(8)]
    del maps, wpB, wpA1, res
    wpB = pack_B(inp['w_in'][1], inp['w_branch'][1], inp['w_out'][1], inp['w_ffn_in'][1], inp['w_ffn_out'][1])
    maps = inputs_B(1, inp, mod, xTs, ctxTs, Aout, False, True, wpB)
    res = run_bass_kernel_spmd(_prog('B1', lambda: build_B(last=True, with_A1=False)), maps, core_ids=cores)
    out = np.empty((2, SEQ, D), np.float32)
    for r in range(8):
        out[r // 4, (r % 4) * NTOK:(r % 4 + 1) * NTOK, :] = res.results[r]["xo"].T
    return out
```

```python
import numpy as np
from contextlib import ExitStack
import concourse.bass as bass
import concourse.mybir as mybir
from concourse.bass_utils import run_bass_kernel_spmd
import ml_dtypes

F32 = mybir.dt.float32
BF16 = mybir.dt.bfloat16
AF = mybir.ActivationFunctionType
ALU = mybir.AluOpType
NPBF = ml_dtypes.bfloat16

D = 2048; KC = 16; SEQ = 16384; NTOK = 4096; CTX = 256; NT = 8; NQ = 512
HD = 128; EPS = 1e-6; SCALE = HD ** -0.5
FH = 5632; FKC = 44
TOKA = NTOK + CTX
WL = 8192


class Sched:
    COMPUTE = ('pe', 'act', 'dve')
    QUEUES = ('sp', 'pool')
    RING = 8
    SEM_LIMIT = 30000

    def __init__(self):
        self.ops = []
        self.lastw = {}
        self.readers = {}

    def add(self, eng, fn, reads=(), writes=(), dma=False):
        i = len(self.ops)
        raw = set(); war = set()
        for k in reads:
            w = self.lastw.get(k)
            if w is not None:
                raw.add(w)
        for k in writes:
            w = self.lastw.get(k)
            if w is not None:
                war.add(w)
            for r in self.readers.get(k, ()):
                war.add(r)
        for k in reads:
            self.readers.setdefault(k, []).append(i)
        for k in writes:
            self.lastw[k] = i
            self.readers[k] = []
        raw.discard(i); war.discard(i)
        self.ops.append([eng, fn, raw, war, dma])
        return i

    def emit(self, nc, block, stack):
        ops = self.ops
        n = len(ops)
        eff = []
        needed = set()
        for i, (eng, fn, raw, war, dma) in enumerate(ops):
            ds = set()
            for d in raw | war:
                de, _, _, _, ddma = ops[d]
                if not ddma and not dma and de == eng:
                    if eng == 'pe':
                        continue
                ds.add(d)
            eff.append(ds)
            needed |= ds
        sig = {}
        sems = {}
        def newsem(name):
            s = stack.enter_context(nc.semaphore(name))
            return s
        cur = {}
        cnt = {}
        dq = {q: [newsem(f"dq_{q}_{r}") for r in range(self.RING)] for q in self.QUEUES}
        dcount = {q: 0 for q in self.QUEUES}
        dma_prev = {}
        for i, (eng, fn, raw, war, dma) in enumerate(ops):
            if dma:
                k = dcount[eng]; dcount[eng] += 1
                s = dq[eng][k % self.RING]
                v = 16 * (k // self.RING + 1)
                sig[i] = (s, v)
                if k >= self.RING:
                    dma_prev[i] = (s, v - 16)
            elif i in needed:
                if eng not in cur or cnt[eng] >= self.SEM_LIMIT:
                    cur[eng] = newsem(f"c_{eng}_{len(sems)}")
                    sems[len(sems)] = cur[eng]
                    cnt[eng] = 0
                cnt[eng] += 1
                sig[i] = (cur[eng], cnt[eng])
        per = {e: [] for e in self.COMPUTE + self.QUEUES}
        for i, o in enumerate(ops):
            per[o[0]].append(i)
        final_waits = [sig[i] for i, o in enumerate(ops) if o[4]]

        def run(engname, e):
            waited = {}
            def w(sv):
                s, v = sv
                key = id(s)
                if waited.get(key, 0) >= v:
                    return
                waited[key] = v
                e.wait_ge(s, v)
            for i in per[engname]:
                eng, fn, raw, war, dma = ops[i]
                for d in sorted(eff[i]):
                    w(sig[d])
                if i in dma_prev:
                    w(dma_prev[i])
                ins = fn(e)
                if i in sig:
                    s, v = sig[i]
                    ins.then_inc(s, 16 if dma else 1)
            if engname == 'sp':
                last = {}
                for s, v in final_waits:
                    if last.get(id(s), (None, 0))[1] < v:
                        last[id(s)] = (s, v)
                for s, v in last.values():
                    w((s, v))

        block.sync(lambda e: run('sp', e))
        block.gpsimd(lambda e: run('pool', e))
        block.tensor(lambda e: run('pe', e))
        block.scalar(lambda e: run('act', e))
        block.vector(lambda e: run('dve', e))


def fm(v):
    v = np.asarray(v, np.float32)
    return np.ascontiguousarray(v.reshape(-1, 128).T)


def blk(W):
    K, Fb = W.shape
    kc = K // 128
    return W.reshape(kc, 128, Fb).transpose(1, 0, 2).reshape(128, kc * Fb)


OFF_BG, OFF_CG, OFF_HA = 0, 512, 1024
OFF_BQ, OFF_BK, OFF_BV = 1536, 2048, 2304
OFF_VAL, OFF_GATE = 2560, 3072
OFF_DQ, OFF_DK, OFF_DV = 3584, 4096, 4352
OFF_G = 4608


def cols(W, starts, width=128):
    return np.concatenate([W[:, s:s + width] for s in starts], axis=1)


def pack_A(w_in_l):
    W = np.asarray(w_in_l, np.float32)
    b = []
    b.append(blk(cols(W, [OFF_CG, OFF_HA, OFF_CG + 128, OFF_HA + 128])))
    b.append(blk(cols(W, [OFF_CG + 256, OFF_HA + 256, OFF_CG + 384, OFF_HA + 384])))
    b.append(blk(cols(W, [OFF_GATE, OFF_VAL, OFF_GATE + 128, OFF_VAL + 128])))
    b.append(blk(cols(W, [OFF_GATE + 256, OFF_VAL + 256, OFF_GATE + 384, OFF_VAL + 384])))
    b.append(blk(cols(W, [OFF_BK, OFF_BK + 128, OFF_DK, OFF_DK + 128])))
    b.append(blk(cols(W, [OFF_BV, OFF_BV + 128, OFF_DV, OFF_DV + 128])))
    return np.ascontiguousarray(np.concatenate(b, axis=1))


def rope_tables():
    inv = (10000.0 ** (-np.arange(0, 64, 2, dtype=np.float32) / np.float32(64))).astype(np.float32)
    t = np.arange(SEQ)
    row = (t // 64).astype(np.float32); col = (t % 64).astype(np.float32)
    ar = row[:, None] * inv; ac = col[:, None] * inv
    ang = np.concatenate([ar, ar, ac, ac], axis=-1).astype(np.float32)
    return np.cos(ang).astype(np.float32).T.copy(), np.sin(ang).astype(np.float32).T.copy()


def rot_matrix_T():
    R = np.zeros((128, 128), np.float32)
    for a in range(2):
        for i in range(32):
            R[a * 64 + i, a * 64 + 32 + i] = -1.0
            R[a * 64 + 32 + i, a * 64 + i] = 1.0
    return np.ascontiguousarray(R.T)


class VT:
    def __init__(self):
        self.off = {}
        self.n = 0
    def add(self, name, width):
        self.off[name] = (self.n, width)
        self.n += width
    def sl(self, name, i=None):
        o, w = self.off[name]
        if i is None:
            return slice(o, o + w)
        return slice(o + i, o + i + 1)


class Ctx:
    pass


def emit_norm(S, c, xT, hT, N, gain, shift, tagx, out_f32=None, extra_reads=(), out_key='xo'):
    ps = c.ps[c.psn % c.psmod]; pk = ('ps', c.psn % c.psmod); c.psn += 1
    for kc in range(KC):
        sq = c.sqb[kc % 2]; sk = ('sqb', kc % 2)
        S.add('act', lambda e, kc=kc, sq=sq: e.activation(out=sq[:, :N], in_=xT[:, kc, :N], func=AF.Square),
              reads=[tagx] + list(extra_reads), writes=[sk])
        S.add('pe', lambda e, kc=kc, sq=sq: e.matmul(ps[:, :N], c.onesbf[:, :], sq[:, :N], start=(kc == 0), stop=(kc == KC - 1)),
              reads=[sk, 'ones'], writes=[pk])
    rs = c.t32[2]; rk = ('t32', 2)
    S.add('act', lambda e: e.activation(out=rs[:, :N], in_=ps[:, :N], func=AF.Sqrt, bias=float(EPS * D)), reads=[pk], writes=[rk])
    S.add('dve', lambda e: e.reciprocal(out=rs[:, :N], in_=rs[:, :N]), reads=[rk], writes=[rk])
    for kc in range(KC):
        tmp = c.t32[kc % 2]; tk = ('t32', kc % 2)
        S.add('dve', lambda e, kc=kc, tmp=tmp: e.tensor_tensor(out=tmp[:, :N], in0=xT[:, kc, :N], in1=rs[:, :N], op=ALU.mult),
              reads=[tagx, rk], writes=[tk])
        if out_f32 is None:
            S.add('act', lambda e, kc=kc, tmp=tmp: e.activation(out=hT[:, kc, :N], in_=tmp[:, :N], func=AF.Identity,
                                                                 bias=shift[:, kc:kc + 1], scale=gain[:, kc:kc + 1]),
                  reads=[tk], writes=['hT'])
        else:
            S.add('act', lambda e, kc=kc, tmp=tmp: e.activation(out=out_f32[:, kc, :N], in_=tmp[:, :N], func=AF.Copy,
                                                                 scale=gain[:, kc:kc + 1]),
                  reads=[tk], writes=[out_key])


def emit_wload(S, c, off, L=WL, src=None):
    slot = c.wn % c.nw; c.wn += 1
    wb = c.wring[slot]
    S.add('pool', lambda e: e.dma_start(out=wb[:, :L], in_=(c.wpack if src is None else src)[:, off:off + L]), writes=[('w', slot)], dma=True)
    return wb, ('w', slot)


def emit_proj(S, c, wb, wk, j, Fb, N, rhs_tag='hT', kcn=KC, rhs=None, rhs_keys=None, sub=128):
    idx = c.psn % c.psmod; c.psn += 1
    ps = c.ps[idx]; pk = ('ps', idx)
    src = c.hT if rhs is None else rhs
    def fn(e):
        for kc in range(kcn):
            ins = e.matmul(ps[:, :N], wb[:, kc * Fb + j * sub: kc * Fb + j * sub + 128], src[:, kc, :N],
                           start=(kc == 0), stop=(kc == kcn - 1))
        return ins
    S.add('pe', fn, reads=[wk] + ([rhs_tag] if rhs_keys is None else list(rhs_keys)), writes=[pk])
    return ps, pk


def emit_rope(S, c, src, sk, dst, dk, N, tok_cos, tok_sin):
    idx = c.psn % c.psmod; c.psn += 1
    ps = c.ps[idx]; pk = ('ps', idx)
    S.add('pe', lambda e: e.matmul(ps[:, :N], c.rotT[:, :], src[:, :N], start=True, stop=True), reads=[sk, 'rotT'], writes=[pk])
    t1 = c.t32[3]; k1 = ('t32', 3)
    t2 = c.t32[4]; k2 = ('t32', 4)
    S.add('dve', lambda e: e.tensor_tensor(out=t1[:, :N], in0=src[:, :N], in1=tok_cos, op=ALU.mult), reads=[sk, 'rope'], writes=[k1])
    S.add('dve', lambda e: e.tensor_tensor(out=t2[:, :N], in0=ps[:, :N], in1=tok_sin, op=ALU.mult), reads=[pk, 'rope'], writes=[k2])
    S.add('dve', lambda e: e.tensor_tensor(out=dst, in0=t1[:, :N], in1=t2[:, :N], op=ALU.add), reads=[k1, k2], writes=[dk])


def emit_headnorm(S, c, ps, pk, gvec, dst, dk, N):
    sq = c.sqb[0]; sk = ('sqb', 0)
    S.add('act', lambda e: e.activation(out=sq[:, :N], in_=ps[:, :N], func=AF.Square), reads=[pk], writes=[sk])
    idx = c.psn % c.psmod; c.psn += 1
    ps2 = c.ps[idx]; pk2 = ('ps', idx)
    S.add('pe', lambda e: e.matmul(ps2[:, :N], c.onesbf[:, :], sq[:, :N], start=True, stop=True), reads=[sk, 'ones'], writes=[pk2])
    rs = c.t32[6]; rk = ('t32', 6)
    S.add('act', lambda e: e.activation(out=rs[:, :N], in_=ps2[:, :N], func=AF.Sqrt, bias=float(EPS * HD * 1.0000001)), reads=[pk2], writes=[rk])
    S.add('dve', lambda e: e.reciprocal(out=rs[:, :N], in_=rs[:, :N]), reads=[rk], writes=[rk])
    S.add('dve', lambda e: e.scalar_tensor_tensor(out=dst, in0=ps[:, :N], scalar=gvec, in1=rs[:, :N], op0=ALU.mult, op1=ALU.mult),
          reads=[pk, rk], writes=[dk])


MCOLS = 1536; MFC = 12

def build_M():
    nc = bass.Bass("TRN2", target_bir_lowering=False)
    c = Ctx()
    c.wpack = nc.dram_tensor("wpack", [128, 2 * MFC * 2048], F32, kind="ExternalInput").ap()
    cT = nc.dram_tensor("cT", [128, 16 * 3], F32, kind="ExternalInput").ap()
    bT = nc.dram_tensor("bT", [128, 2 * MFC], F32, kind="ExternalInput").ap()
    out = nc.dram_tensor("mod", [128, 2 * MFC * 3], F32, kind="ExternalOutput").ap()
    S = Sched()
    with ExitStack() as st:
        sb = lambda name, shape, dt: st.enter_context(nc.sbuf_tensor(name, shape, dt))
        c.nw = 3; c.wn = 0; c.psn = 0; c.psmod = 8
        c.wring = [sb(f"w{i}", [128, 2048], BF16) for i in range(c.nw)]
        c.ps = [st.enter_context(nc.psum_tensor(f"ps{i}", [128, 512], F32)) for i in range(8)]
        c32 = sb("c32", [128, 48], F32); s32 = sb("s32", [128, 48], F32); sbf = sb("sbf", [128, 16, 3], BF16)
        b32 = sb("b32", [128, 2 * MFC], F32); o32 = sb("o32", [128, 2 * MFC, 3], F32)
        S.add('sp', lambda e: e.dma_start(out=c32[:, :], in_=cT[:, :]), writes=['c32'], dma=True)
        S.add('sp', lambda e: e.dma_start(out=b32[:, :], in_=bT[:, :]), writes=['b32'], dma=True)
        S.add('act', lambda e: e.activation(out=s32[:, :], in_=c32[:, :], func=AF.Silu), reads=['c32'], writes=['s32'])
        S.add('dve', lambda e: e.tensor_copy(out=sbf[:].rearrange("p k v -> p (k v)"), in_=s32[:, :]), reads=['s32'], writes=['sbf'])
        for i in range(2 * MFC):
            wb, wk = emit_wload(S, c, i * 2048, 2048)
            idx = c.psn % c.psmod; c.psn += 1
            ps = c.ps[idx]; pk = ('ps', idx)
            def fn(e, wb=wb, ps=ps):
                for kc in range(KC):
                    ins = e.matmul(ps[:, :3], wb[:, kc * 128:(kc + 1) * 128], sbf[:, kc, :], start=(kc == 0), stop=(kc == KC - 1))
                return ins
            S.add('pe', fn, reads=[wk, 'sbf'], writes=[pk])
            S.add('dve', lambda e, i=i, ps=ps: e.tensor_scalar(out=o32[:, i, :], in0=ps[:, :3], scalar1=b32[:, i:i + 1], scalar2=None, op0=ALU.add),
                  reads=[pk, 'b32'], writes=['o32'])
        S.add('sp', lambda e: e.dma_start(out=out[:, :], in_=o32[:].rearrange("p i v -> p (i v)")), reads=['o32'], dma=True)
        with nc.Block() as block:
            S.emit(nc, block, st)
    return nc


def inputs_M(c, c_ctx, w_ada, b_ada):
    cs = np.stack([fm(c[0]), fm(c[1]), fm(c_ctx)], axis=-1).reshape(128, 48)
    maps = []
    for r in range(8):
        blocks = []
        bs = []
        for l in range(2):
            for fc in range(MFC):
                c0 = r * MCOLS + fc * 128
                blocks.append(blk(np.asarray(w_ada[l][:, c0:c0 + 128], np.float32)))
                bs.append(np.asarray(b_ada[l][c0:c0 + 128], np.float32))
        maps.append({"wpack": np.ascontiguousarray(np.concatenate(blocks, axis=1)),
                     "cT": np.ascontiguousarray(cs), "bT": np.ascontiguousarray(np.stack(bs, axis=1))})
    return maps


def gather_M(results):
    mod = np.zeros((2, 3, 6 * D), np.float32)
    for r in range(8):
        o = results[r]["mod"].reshape(128, 2, MFC, 3)
        for l in range(2):
            for fc in range(MFC):
                c0 = r * MCOLS + fc * 128
                mod[l, :, c0:c0 + 128] = o[:, l, fc, :].T
    return mod


def vt_A():
    vt = VT()
    for nm in ("norm_mix", "sc_lat", "sh_lat", "sc_ctx", "sh_ctx"):
        vt.add(nm, 16)
    vt.add("qk_k", 1)
    return vt


def alloc_common(nc, st, c, n_t32=10, nw=3):
    sb = lambda name, shape, dt: st.enter_context(nc.sbuf_tensor(name, shape, dt))
    c.sb = sb
    c.nw = nw; c.wn = 0; c.psn = 0; c.psmod = 8
    c.wring = [sb(f"w{i}", [128, WL], BF16) for i in range(nw)]
    c.ps = [st.enter_context(nc.psum_tensor(f"ps{i}", [128, 512], F32)) for i in range(8)]
    c.t32 = [sb(f"t32_{i}", [128, 512], F32) for i in range(n_t32)]
    c.xT = sb("xT_sb", [128, KC, NQ], F32)
    c.hT = sb("hT_sb", [128, KC, NQ], BF16)
    c.ones32 = sb("ones32", [128, 128], F32)
    c.onesbf = sb("onesbf", [128, 128], BF16)
    c.sqb = [sb(f"sqb{i}", [128, NQ], BF16) for i in range(2)]
    c.rotT = sb("rotT_sb", [128, 128], F32)
    c.cos = sb("cos", [128, NQ], F32)
    c.sin = sb("sin", [128, NQ], F32)


def emit_A_tile(S, c, A, t0, N, is_ctx, load=True):
    if load:
        if is_ctx:
            S.add('sp', lambda e: e.dma_start(out=c.xT[:, :, :CTX], in_=A.ctxT[:, :, :]), writes=['xT'], dma=True)
        else:
            S.add('sp', lambda e: e.dma_start(out=c.xT[:, :, :], in_=A.xT[:, :, t0:t0 + NQ]), writes=['xT'], dma=True)
            S.add('sp', lambda e: e.dma_start(out=c.cos[:, :], in_=A.cosT[:, t0:t0 + NQ]), writes=['rope'], dma=True)
            S.add('sp', lambda e: e.dma_start(out=c.sin[:, :], in_=A.sinT[:, t0:t0 + NQ]), writes=['rope'], dma=True)
    gain, shift = (A.gc, A.shc) if is_ctx else (A.gl, A.shl)
    gk = A.gk
    emit_norm(S, c, c.xT, c.hT, N, gain, shift, 'xT', extra_reads=['gains', 'V', 'ones'])
    for b in range(4):
        wb, wk = emit_wload(S, c, A.woff + b * WL, src=A.wpack)
        for pr in range(2):
            j = (b % 2) * 2 + pr
            ps1, pk1 = emit_proj(S, c, wb, wk, 2 * pr, 512, N)
            ps2, pk2 = emit_proj(S, c, wb, wk, 2 * pr + 1, 512, N)
            tmp = c.t32[7]; tk = ('t32', 7)
            fnc = AF.Copy if b < 2 else AF.Sigmoid
            S.add('act', lambda e, ps1=ps1, fnc=fnc, tmp=tmp: e.activation(out=tmp[:, :N], in_=ps1[:, :N], func=fnc), reads=[pk1], writes=[tk])
            ub = A.ubuf[A.un % 2]; uk = A.ukeys[A.un % 2]; A.un += 1
            S.add('dve', lambda e, ub=ub, tmp=tmp, ps2=ps2: e.tensor_tensor(out=ub[:, :N], in0=tmp[:, :N], in1=ps2[:, :N], op=ALU.mult),
                  reads=[tk, pk2], writes=[uk])
            jj = j + (0 if b < 2 else 4)
            S.add('sp', lambda e, ub=ub, jj=jj: e.dma_start(out=A.uT[:, jj, t0:t0 + N], in_=ub[:, :N]), reads=[uk], dma=True)
    wb, wk = emit_wload(S, c, A.woff + 4 * WL, src=A.wpack)
    for hh in range(4):
        ps, pk = emit_proj(S, c, wb, wk, hh, 512, N)
        kb = A.kbuf[A.kn % 2]; kk = ('kbuf', A.kn % 2); A.kn += 1
        k32 = c.t32[6]; k32k = ('t32', 6)
        if hh < 2:
            if is_ctx:
                S.add('act', lambda e, ps=ps, kb=kb: e.activation(out=kb[:, :N], in_=ps[:, :N], func=AF.Copy), reads=[pk], writes=[kk])
            else:
                S.add('act', lambda e, ps=ps, k32=k32: e.activation(out=k32[:, :N], in_=ps[:, :N], func=AF.Copy), reads=[pk], writes=[k32k])
                emit_rope(S, c, k32, k32k, kb[:, :N], kk, N, c.cos[:, :N], c.sin[:, :N])
        else:
            if is_ctx:
                emit_headnorm(S, c, ps, pk, gk[:, 0:1], kb[:, :N], kk, N)
            else:
                emit_headnorm(S, c, ps, pk, gk[:, 0:1], k32[:, :N], k32k, N)
                emit_rope(S, c, k32, k32k, kb[:, :N], kk, N, c.cos[:, :N], c.sin[:, :N])
        S.add('sp', lambda e, kb=kb, hh=hh: e.dma_start(out=A.kT[:, hh, t0:t0 + N], in_=kb[:, :N]), reads=[kk], dma=True)
    wb, wk = emit_wload(S, c, A.woff + 5 * WL, src=A.wpack)
    for s in range(N // 128):
        idx = c.psn % c.psmod; c.psn += 1
        ps = c.ps[idx]; pk = ('ps', idx)
        def fn(e, s=s, ps=ps, wb=wb):
            for kc in range(KC):
                ins = e.matmul(ps[:, :], c.hT[:, kc, s * 128:(s + 1) * 128], wb[:, kc * 512:(kc + 1) * 512], start=(kc == 0), stop=(kc == KC - 1))
            return ins
        S.add('pe', fn, reads=[wk, 'hT'], writes=[pk])
        vb = A.vbuf[A.vn % 2]; vk = ('vbuf', A.vn % 2); A.vn += 1
        S.add('act', lambda e, ps=ps, vb=vb: e.activation(out=vb[:, :], in_=ps[:, :], func=AF.Copy), reads=[pk], writes=[vk])
        S.add('sp', lambda e, vb=vb, s=s: e.dma_start(out=A.vO[t0 + s * 128:t0 + (s + 1) * 128, :], in_=vb[:, :]), reads=[vk], dma=True)


def setup_A(nc, S, c, A, V, vt, sfx=""):
    sb = c.sb
    A.gl = sb("A_gain_lat", [128, 16], F32); A.gc = sb("A_gain_ctx", [128, 16], F32); A.gk = sb("A_gk", [128, 1], F32)
    A.shl = V[:, vt.sl("sh_lat" + sfx)]; A.shc = V[:, vt.sl("sh_ctx" + sfx)]
    A.kbuf = [sb(f"kbuf{i}", [128, NQ], BF16) for i in range(2)]
    A.vbuf = [sb(f"vbuf{i}", [128, 512], BF16) for i in range(2)]
    A.un = A.kn = A.vn = 0
    if not hasattr(A, 'ukeys'):
        A.ukeys = [('ubuf', 0), ('ubuf', 1)]
    sqD = float(np.sqrt(D))
    for g, scn in ((A.gl, "sc_lat" + sfx), (A.gc, "sc_ctx" + sfx)):
        S.add('dve', lambda e, g=g, scn=scn: e.tensor_scalar(out=g[:, :], in0=V[:, vt.sl(scn)], scalar1=1.0, scalar2=sqD, op0=ALU.add, op1=ALU.mult),
              reads=['V'], writes=['gains'])
        S.add('dve', lambda e, g=g: e.tensor_tensor(out=g[:, :], in0=g[:, :], in1=V[:, vt.sl("norm_mix" + sfx)], op=ALU.mult),
              reads=['V', 'gains'], writes=['gains'])
    S.add('dve', lambda e: e.tensor_scalar(out=A.gk[:, :], in0=V[:, vt.sl("qk_k" + sfx)], scalar1=float(np.sqrt(HD)), scalar2=None, op0=ALU.mult),
          reads=['V'], writes=['gains'])


def declare_A_outputs(nc, A):
    A.uT = nc.dram_tensor("uT", [1024, TOKA], F32, kind="ExternalOutput").ap().rearrange("(j p) t -> p j t", p=128)
    A.kT = nc.dram_tensor("kT", [128, 4 * TOKA], BF16, kind="ExternalOutput").ap().rearrange("p (h t) -> p h t", h=4)
    A.vO = nc.dram_tensor("v", [TOKA, 512], BF16, kind="ExternalOutput").ap()


def build_A():
    nc = bass.Bass("TRN2", target_bir_lowering=False)
    c = Ctx(); A = Ctx()
    vt = vt_A()
    A.wpack = nc.dram_tensor("wpack", [128, 6 * WL], F32, kind="ExternalInput").ap(); A.woff = 0
    A.xT = nc.dram_tensor("xT", [D, NTOK], F32, kind="ExternalInput").ap().rearrange("(k p) t -> p k t", p=128)
    A.ctxT = nc.dram_tensor("ctxT", [D, CTX], F32, kind="ExternalInput").ap().rearrange("(k p) t -> p k t", p=128)
    vec = nc.dram_tensor("vec", [128, vt.n], F32, kind="ExternalInput").ap()
    A.cosT = nc.dram_tensor("cosT", [128, NTOK], F32, kind="ExternalInput").ap()
    A.sinT = nc.dram_tensor("sinT", [128, NTOK], F32, kind="ExternalInput").ap()
    rotT = nc.dram_tensor("rotT", [128, 128], F32, kind="ExternalInput").ap()
    declare_A_outputs(nc, A)
    S = Sched()
    with ExitStack() as st:
        alloc_common(nc, st, c)
        V = c.sb("vecs", [128, vt.n], F32)
        A.ubuf = [c.sb(f"ubuf{i}", [128, NQ], F32) for i in range(2)]
        S.add('sp', lambda e: e.dma_start(out=V[:, :], in_=vec[:, :]), writes=['V'], dma=True)
        S.add('sp', lambda e: e.dma_start(out=c.rotT[:, :], in_=rotT[:, :]), writes=['rotT'], dma=True)
        S.add('dve', lambda e: e.memset(c.ones32[:, :], 1.0), writes=['ones'])
        S.add('dve', lambda e: e.memset(c.onesbf[:, :], 1.0), writes=['ones'])
        setup_A(nc, S, c, A, V, vt)
        for t in range(NT):
            emit_A_tile(S, c, A, t * NQ, NQ, False)
        emit_A_tile(S, c, A, NTOK, CTX, True)
        with nc.Block() as block:
            S.emit(nc, block, st)
    return nc


def vec_A(vt, l, b, mod, norm_mix, qk_norm_k):
    V = np.zeros((128, vt.n), np.float32)
    V[:, vt.sl("norm_mix")] = fm(norm_mix[l])
    V[:, vt.sl("sh_lat")] = fm(mod[l, b, 0:D]); V[:, vt.sl("sc_lat")] = fm(mod[l, b, D:2 * D])
    V[:, vt.sl("sh_ctx")] = fm(mod[l, 2, 0:D]); V[:, vt.sl("sc_ctx")] = fm(mod[l, 2, D:2 * D])
    V[:, vt.sl("qk_k")] = fm(qk_norm_k[l])
    return V


NB_G = 16
BPACK = 3 * WL + 16 * (WL + 2048) + 4 * WL + 22 * WL + 16 * 5632


def pack_B(w_in_l, w_branch_l, w_out_l, w_fi_l, w_fo_l):
    W = np.asarray(w_in_l, np.float32)
    b = [blk(W[:, OFF_BG:OFF_BG + 512]), blk(W[:, OFF_BQ:OFF_BQ + 512]), blk(W[:, OFF_DQ:OFF_DQ + 512])]
    for cc in range(16):
        b.append(blk(cols(W, [OFF_G + i * D + cc * 128 for i in range(4)])))
        b.append(np.concatenate([blk(np.asarray(w_branch_l[i][:, cc * 128:(cc + 1) * 128], np.float32)) for i in range(4)], axis=1))
    Wo = np.asarray(w_out_l, np.float32)
    for ob in range(4):
        b.append(blk(Wo[:, ob * 512:(ob + 1) * 512]))
    Wi = np.asarray(w_fi_l, np.float32); Wf = np.asarray(w_fo_l, np.float32)
    for hf in range(2):
        for fb in range(11):
            j0 = hf * 22 + 2 * fb
            b.append(blk(cols(Wi, [j0 * 128, FH + j0 * 128, (j0 + 1) * 128, FH + (j0 + 1) * 128])))
        for cp in range(8):
            b.append(blk(Wf[hf * 2816:(hf + 1) * 2816, cp * 256:(cp + 1) * 256]))
    out = np.ascontiguousarray(np.concatenate(b, axis=1))
    assert out.shape == (128, BPACK), out.shape
    return out


def vt_B(with_A1, last):
    vt = VT()
    for nm in ("norm_mix", "norm_ffn"):
        vt.add(nm, 16)
    for who in ("lat", "ctx"):
        for nm in ("sh_m", "sc_m", "g_m", "sh_f", "sc_f", "g_f"):
            vt.add(f"{nm}_{who}", 16)
    vt.add("b_gate", 64)
    vt.add("conv_a", 12)
    vt.add("conv_c", 124)
    vt.add("conv_cb", 4); vt.add("ln_g", 4); vt.add("ln_b", 4)
    vt.add("qk_q", 1); vt.add("sink", 4)
    if last:
        vt.add("norm_final", 16)
    if with_A1:
        for nm in ("norm_mix1", "sc_lat1", "sh_lat1", "sc_ctx1", "sh_ctx1"):
            vt.add(nm, 16)
        vt.add("qk_k1", 1)
    return vt


def vec_B(vt, l, b, mod, inp, with_A1, last):
    V = np.zeros((128, vt.n), np.float32)
    V[:, vt.sl("norm_mix")] = fm(inp['norm_mix'][l]); V[:, vt.sl("norm_ffn")] = fm(inp['norm_ffn'][l])
    for who, mv in (("lat", mod[l, b]), ("ctx", mod[l, 2])):
        for i, nm in enumerate(("sh_m", "sc_m", "g_m", "sh_f", "sc_f", "g_f")):
            V[:, vt.sl(f"{nm}_{who}")] = fm(mv[i * D:(i + 1) * D])
    V[:, vt.sl("b_gate")] = np.concatenate([fm(inp['b_gate'][l][i]) for i in range(4)], axis=1)
    V[:, vt.sl("conv_a")] = np.concatenate([fm(inp['conv_a_w'][l][j]) for j in range(3)], axis=1)
    V[:, vt.sl("conv_c")] = np.concatenate([fm(inp['conv_c_w'][l][j]) for j in range(31)], axis=1)
    V[:, vt.sl("conv_cb")] = fm(inp['conv_c_b'][l]); V[:, vt.sl("ln_g")] = fm(inp['ln_c_g'][l]); V[:, vt.sl("ln_b")] = fm(inp['ln_c_b'][l])
    V[:, vt.sl("qk_q")] = fm(inp['qk_norm_q'][l])
    V[:, vt.sl("sink")] = np.broadcast_to(np.asarray(inp['sink_b'][l], np.float32)[None, :], (128, 4))
    if last:
        V[:, vt.sl("norm_final")] = fm(inp['norm_final'])
    if with_A1:
        l1 = l + 1
        V[:, vt.sl("norm_mix1")] = fm(inp['norm_mix'][l1])
        V[:, vt.sl("sh_lat1")] = fm(mod[l1, b, 0:D]); V[:, vt.sl("sc_lat1")] = fm(mod[l1, b, D:2 * D])
        V[:, vt.sl("sh_ctx1")] = fm(mod[l1, 2, 0:D]); V[:, vt.sl("sc_ctx1")] = fm(mod[l1, 2, D:2 * D])
        V[:, vt.sl("qk_k1")] = fm(inp['qk_norm_k'][l1])
    return V


def band_masks(first, last_):
    kk = np.arange(128)[:, None]; q = np.arange(512)[None, :]
    m = np.zeros((128, 8, 512), np.float32)
    for j in range(6):
        m[:, j, :] = (np.abs((j - 1) * 128 + kk - q) <= 128)
    m[:, 6, :] = 0.0 if first else m[:, 0, :]
    m[:, 7, :] = 0.0 if last_ else m[:, 5, :]
    return m.reshape(128, 8 * 512).astype(NPBF)


class Attn:
    def __init__(self, S, c, N):
        self.S, self.c, self.N, self.pend = S, c, N, None

    def push(self, q, qk, KT, V, kvk, acc, first, last, mask=None):
        S, c, N = self.S, self.c, self.N
        idx = c.psn % c.psmod; c.psn += 1
        psS = c.ps[idx]; sk = ('ps', idx)
        S.add('pe', lambda e: e.matmul(psS[:, :N], KT, q, start=True, stop=True), reads=[qk] + list(kvk), writes=[sk])
        pi = c.pn % len(c.P); c.pn += 1
        P = c.P[pi]; pk = ('P', pi)
        S.add('act', lambda e: e.activation(out=P[:, :N], in_=psS[:, :N], func=AF.Exp), reads=[sk], writes=[pk])
        if mask is not None:
            S.add('dve', lambda e: e.tensor_tensor(out=P[:, :N], in0=P[:, :N], in1=mask, op=ALU.mult), reads=[pk, 'masks'], writes=[pk])
        prev = self.pend
        self.pend = (P, pk, V, list(kvk), acc, first, last)
        if prev is not None:
            self._pv(prev)

    def _pv(self, it):
        S, c, N = self.S, self.c, self.N
        P, pk, V, kvk, acc, first, last = it
        psO = c.ps[4 + 2 * acc]; ok = ('ps', 4 + 2 * acc)
        psZ = c.ps[5 + 2 * acc]; zk = ('ps', 5 + 2 * acc)
        S.add('pe', lambda e: e.matmul(psO[:, :N], V, P[:, :N], start=first, stop=last), reads=[pk] + kvk, writes=[ok])
        za = c.zacc[acc]; zak = ('zacc', acc)
        if first:
            S.add('dve', lambda e: e.tensor_copy(out=za[:, :N], in_=P[:, :N]), reads=[pk], writes=[zak])
        else:
            S.add('dve', lambda e: e.tensor_tensor(out=za[:, :N], in0=za[:, :N], in1=P[:, :N], op=ALU.add), reads=[pk, zak], writes=[zak])
        if last:
            S.add('pe', lambda e: e.matmul(psZ[:, :N], c.ones32[:, :], za[:, :N], start=True, stop=True), reads=[zak, 'ones'], writes=[zk])

    def flush(self):
        if self.pend is not None:
            self._pv(self.pend)
            self.pend = None

    def final(self, acc, sink, out, out_key):
        S, c, N = self.S, self.c, self.N
        psO = c.ps[4 + 2 * acc]; ok = ('ps', 4 + 2 * acc)
        psZ = c.ps[5 + 2 * acc]; zk = ('ps', 5 + 2 * acc)
        z = c.t32[5]; zkk = ('t32', 5)
        if sink is not None:
            S.add('dve', lambda e: e.tensor_scalar(out=z[:, :N], in0=psZ[:, :N], scalar1=sink, scalar2=None, op0=ALU.add), reads=[zk, 'esink'], writes=[zkk])
        else:
            S.add('dve', lambda e: e.tensor_copy(out=z[:, :N], in_=psZ[:, :N]), reads=[zk], writes=[zkk])
        S.add('dve', lambda e: e.reciprocal(out=z[:, :N], in_=z[:, :N]), reads=[zkk], writes=[zkk])
        S.add('dve', lambda e: e.tensor_tensor(out=out, in0=psO[:, :N], in1=z[:, :N], op=ALU.mult), reads=[ok, zkk], writes=[out_key])


def build_B(last=False, with_A1=False, tiles=None, do_ctx=True):
    nc = bass.Bass("TRN2", target_bir_lowering=False)
    c = Ctx(); A = Ctx()
    vt = vt_B(with_A1, last)
    dt = lambda name, shape, dty, kind="ExternalInput": nc.dram_tensor(name, shape, dty, kind=kind).ap()
    c.wpack = dt("wpack", [128, BPACK], F32)
    xT = dt("xT", [D, NTOK], F32).rearrange("(k p) t -> p k t", p=128)
    ctxT = dt("ctxT", [D, CTX], F32).rearrange("(k p) t -> p k t", p=128)
    vec = dt("vec", [128, vt.n], F32)
    cosT = dt("cosT", [128, NTOK], F32); sinT = dt("sinT", [128, NTOK], F32); rotT = dt("rotT", [128, 128], F32)
    masksD = dt("masks", [128, 8 * 512], BF16)
    kTg = dt("kTg", [128, 2 * SEQ], BF16).rearrange("p (g t) -> p g t", g=2)
    vg = dt("vg", [2 * NB_G * 128, 1024], BF16).rearrange("(g n p) f -> g n p f", g=2, n=NB_G)
    kTw = dt("kTw", [128, 2 * (NTOK + 256)], BF16).rearrange("p (g t) -> p g t", g=2)
    vw = dt("vw", [2 * (NTOK + 256), 128], BF16).rearrange("(g t) d -> g t d", g=2)
    kTc = dt("kTc", [128, 4 * CTX], BF16)
    vcD = dt("vc", [CTX, 512], BF16)
    uh = dt("uh", [1024, NTOK + 30], F32).rearrange("(j p) t -> p j t", p=128)
    uhc = dt("uhc", [1024, CTX + 30], F32).rearrange("(j p) t -> p j t", p=128)
    xo = dt("xo", [D, NTOK], F32, kind="ExternalOutput").rearrange("(k p) t -> p k t", p=128)
    if not last:
        co = dt("co", [D, CTX], F32, kind="ExternalOutput").rearrange("(k p) t -> p k t", p=128)
    if with_A1:
        A.wpack = dt("wpackA", [128, 6 * WL], F32); A.woff = 0
        declare_A_outputs(nc, A)
    S = Sched()
    with ExitStack() as st:
        alloc_common(nc, st, c, n_t32=8, nw=2)
        c.psmod = 4
        sb = c.sb
        V = sb("vecs", [128, vt.n], F32)
        c.wsm = [sb(f"wsm{i}", [128, 2048], BF16) for i in range(2)]; c.wsn = 0
        R1 = sb("R1", [128, 32, NQ], BF16)
        masks = sb("masks_sb", [128, 8, 512], BF16)
        qb = sb("qb", [128, 4, NQ], BF16); qd = sb("qd", [128, 4, NQ], BF16)
        kring = [sb(f"kring{i}", [128, 1024], BF16) for i in range(2)]
        vring = [sb(f"vring{i}", [128, 8, 128], BF16) for i in range(2)]
        kw = sb("kw", [128, 2, 768], BF16); vws = sb("vws", [128, 2, 6, 128], BF16)
        kcs = sb("kcs", [128, 4, CTX], BF16); vcs = sb("vcs", [128, 2, 512], BF16)
        c.P = [sb(f"P{i}", [128, NQ], BF16) for i in range(2)]; c.pn = 0
        c.zacc = [sb(f"zacc{i}", [128, NQ], F32) for i in range(2)]
        ua = [sb(f"ua{i}", [128, NQ + 2], F32) for i in range(2)]
        uc = [sb(f"uc{i}", [128, NQ + 30], F32) for i in range(2)]
        hcv = sb("hcv", [128, 4, NQ], F32)
        gm = {w: sb(f"gain_m_{w}", [128, 16], F32) for w in ("lat", "ctx")}
        gf = {w: sb(f"gain_f_{w}", [128, 16], F32) for w in ("lat", "ctx")}
        gq = sb("gq", [128, 1], F32); esink = sb("esink", [128, 4], F32)
        if last:
            gfin = sb("gfin", [128, 16], F32)
        S.add('sp', lambda e: e.dma_start(out=V[:, :], in_=vec[:, :]), writes=['V'], dma=True)
        S.add('sp', lambda e: e.dma_start(out=c.rotT[:, :], in_=rotT[:, :]), writes=['rotT'], dma=True)
        S.add('sp', lambda e: e.dma_start(out=masks[:].rearrange("p a b -> p (a b)"), in_=masksD[:, :]), writes=['masks'], dma=True)
        S.add('sp', lambda e: e.dma_start(out=kcs[:].rearrange("p a b -> p (a b)"), in_=kTc[:, :]), writes=['kvc'], dma=True)
        S.add('sp', lambda e: e.dma_start(out=vcs[:], in_=vcD.rearrange("(c p) f -> p c f", p=128)), writes=['kvc'], dma=True)
        S.add('dve', lambda e: e.memset(c.ones32[:, :], 1.0), writes=['ones'])
        S.add('dve', lambda e: e.memset(c.onesbf[:, :], 1.0), writes=['ones'])
        sqD = float(np.sqrt(D))
        for w in ("lat", "ctx"):
            for g, scn, nn in ((gm[w], f"sc_m_{w}", "norm_mix"), (gf[w], f"sc_f_{w}", "norm_ffn")):
                S.add('dve', lambda e, g=g, scn=scn: e.tensor_scalar(out=g[:, :], in0=V[:, vt.sl(scn)], scalar1=1.0, scalar2=sqD, op0=ALU.add, op1=ALU.mult),
                      reads=['V'], writes=['gains'])
                S.add('dve', lambda e, g=g, nn=nn: e.tensor_tensor(out=g[:, :], in0=g[:, :], in1=V[:, vt.sl(nn)], op=ALU.mult),
                      reads=['V', 'gains'], writes=['gains'])
        S.add('dve', lambda e: e.tensor_scalar(out=gq[:, :], in0=V[:, vt.sl("qk_q")], scalar1=float(np.sqrt(HD) * SCALE), scalar2=None, op0=ALU.mult),
              reads=['V'], writes=['gains'])
        S.add('act', lambda e: e.activation(out=esink[:, :], in_=V[:, vt.sl("sink")], func=AF.Exp), reads=['V'], writes=['esink'])
        if last:
            S.add('dve', lambda e: e.tensor_scalar(out=gfin[:, :], in0=V[:, vt.sl("norm_final")], scalar1=sqD, scalar2=None, op0=ALU.mult),
                  reads=['V'], writes=['gains'])
        if with_A1:
            A.ubuf = [c.t32[0], c.t32[1]]; A.ukeys = [('t32', 0), ('t32', 1)]
            setup_A(nc, S, c, A, V, vt, sfx="1")
            A.ubuf_keys = True
        WOFF = [0]

        def wnext(L=WL):
            off = WOFF[0]; WOFF[0] += L
            return off

        def wsmall(off):
            slot = c.wsn % 2; c.wsn += 1
            wbs = c.wsm[slot]
            S.add('pool', lambda e: e.dma_start(out=wbs[:, :], in_=c.wpack[:, off:off + 2048]), writes=[('wsm', slot)], dma=True)
            return wbs, ('wsm', slot)

        def do_tile(ti, t0, N, is_ctx):
            who = "ctx" if is_ctx else "lat"
            WOFF[0] = 0
            vs = lambda nm, i=None: V[:, vt.sl(f"{nm}_{who}", i)]
            if is_ctx:
                S.add('sp', lambda e: e.dma_start(out=c.xT[:, :, :CTX], in_=ctxT[:, :, :]), writes=['xT'], dma=True)
            else:
                S.add('sp', lambda e: e.dma_start(out=c.xT[:, :, :], in_=xT[:, :, t0:t0 + NQ]), writes=['xT'], dma=True)
                S.add('sp', lambda e: e.dma_start(out=c.cos[:, :], in_=cosT[:, t0:t0 + NQ]), writes=['rope'], dma=True)
                S.add('sp', lambda e: e.dma_start(out=c.sin[:, :], in_=sinT[:, t0:t0 + NQ]), writes=['rope'], dma=True)
            emit_norm(S, c, c.xT, c.hT, N, gm[who], vs("sh_m"), 'xT', extra_reads=['gains', 'V', 'ones'])
            wb, wk = emit_wload(S, c, wnext())
            usrc = uhc if is_ctx else uh
            for j in range(4):
                ub = ua[j % 2]; uk = ('ua', j % 2)
                S.add('sp', lambda e, ub=ub, j=j: e.dma_start(out=ub[:, :N + 2], in_=usrc[:, j, t0 + 14:t0 + 14 + N + 2]), writes=[uk], dma=True)
                acc = c.t32[3]; ak = ('t32', 3)
                for tap in range(3):
                    wcol = V[:, vt.off["conv_a"][0] + tap * 4 + j: vt.off["conv_a"][0] + tap * 4 + j + 1]
                    if tap == 0:
                        S.add('dve', lambda e, ub=ub, wcol=wcol: e.tensor_scalar(out=acc[:, :N], in0=ub[:, 0:N], scalar1=wcol, scalar2=None, op0=ALU.mult),
                              reads=[uk, 'V'], writes=[ak])
                    else:
                        S.add('dve', lambda e, ub=ub, wcol=wcol, tap=tap: e.scalar_tensor_tensor(out=acc[:, :N], in0=ub[:, tap:tap + N], scalar=wcol, in1=acc[:, :N], op0=ALU.mult, op1=ALU.add),
                              reads=[uk, 'V', ak], writes=[ak])
                ps, pk = emit_proj(S, c, wb, wk, j, 512, N)
                S.add('dve', lambda e, ps=ps, j=j: e.tensor_tensor(out=R1[:, j, :N], in0=acc[:, :N], in1=ps[:, :N], op=ALU.mult), reads=[ak, pk], writes=[('r1', j)])
            wb, wk = emit_wload(S, c, wnext())
            q32 = c.t32[6]; q32k = ('t32', 6)
            for hq in range(4):
                ps, pk = emit_proj(S, c, wb, wk, hq, 512, N)
                if is_ctx:
                    S.add('act', lambda e, ps=ps, hq=hq: e.activation(out=qb[:, hq, :N], in_=ps[:, :N], func=AF.Copy, scale=float(SCALE)), reads=[pk], writes=[('qb', hq)])
                else:
                    S.add('act', lambda e, ps=ps: e.activation(out=q32[:, :N], in_=ps[:, :N], func=AF.Copy, scale=float(SCALE)), reads=[pk], writes=[q32k])
                    emit_rope(S, c, q32, q32k, qb[:, hq, :N], ('qb', hq), N, c.cos[:, :N], c.sin[:, :N])
            wb, wk = emit_wload(S, c, wnext())
            for hq in range(4):
                ps, pk = emit_proj(S, c, wb, wk, hq, 512, N)
                if is_ctx:
                    emit_headnorm(S, c, ps, pk, gq[:, 0:1], qd[:, hq, :N], ('qd', hq), N)
                else:
                    emit_headnorm(S, c, ps, pk, gq[:, 0:1], q32[:, :N], q32k, N)
                    emit_rope(S, c, q32, q32k, qd[:, hq, :N], ('qd', hq), N, c.cos[:, :N], c.sin[:, :N])
            for j in range(4):
                ub = uc[j % 2]; uk = ('uc', j % 2)
                S.add('sp', lambda e, ub=ub, j=j: e.dma_start(out=ub[:, :N + 30], in_=usrc[:, 4 + j, t0:t0 + N + 30]), writes=[uk], dma=True)
                hk = ('hcv', j)
                for tap in range(31):
                    wcol = V[:, vt.off["conv_c"][0] + tap * 4 + j: vt.off["conv_c"][0] + tap * 4 + j + 1]
                    if tap == 0:
                        S.add('dve', lambda e, ub=ub, wcol=wcol, j=j: e.tensor_scalar(out=hcv[:, j, :N], in0=ub[:, 0:N], scalar1=wcol, scalar2=V[:, vt.sl("conv_cb", j)], op0=ALU.mult, op1=ALU.add),
                              reads=[uk, 'V'], writes=[hk])
                    else:
                        S.add('dve', lambda e, ub=ub, wcol=wcol, tap=tap, j=j: e.scalar_tensor_tensor(out=hcv[:, j, :N], in0=ub[:, tap:tap + N], scalar=wcol, in1=hcv[:, j, :N], op0=ALU.mult, op1=ALU.add),
                                  reads=[uk, 'V', hk], writes=[hk])
            at = Attn(S, c, N)
            if not is_ctx:
                for g in range(2):
                    for cg in range(NB_G):
                        ri = c.kvn % 2; c.kvn += 1
                        kr, vr, kvk = kring[ri], vring[ri], ('kvr', ri)
                        S.add('sp', lambda e, kr=kr, g=g, cg=cg: e.dma_start(out=kr[:, :], in_=kTg[:, g, cg * 1024:(cg + 1) * 1024]), writes=[kvk], dma=True)
                        S.add('sp', lambda e, vr=vr, g=g, cg=cg: e.dma_start(out=vr[:].rearrange("p a b -> p (a b)"), in_=vg[g, cg, :, :]), writes=[kvk], dma=True)
                        for ch in range(8):
                            for hq in (2 * g, 2 * g + 1):
                                at.push(qd[:, hq, :N], ('qd', hq), kr[:, ch * 128:(ch + 1) * 128], vr[:, ch, :], [kvk], hq % 2, cg == 0 and ch == 0, False)
                    for ch in range(2):
                        for hq in (2 * g, 2 * g + 1):
                            at.push(qd[:, hq, :N], ('qd', hq), kcs[:, 2 + g, ch * 128:(ch + 1) * 128], vcs[:, ch, (2 + g) * 128:(3 + g) * 128], ['kvc'], hq % 2, False, ch == 1)
                    at.flush()
                    for hq in (2 * g, 2 * g + 1):
                        at.final(hq % 2, None, R1[:, 12 + hq, :N], ('r1', 12 + hq))
                for g in range(2):
                    S.add('sp', lambda e, g=g: e.dma_start(out=kw[:, g, :], in_=kTw[:, g, t0:t0 + 768]), writes=[('kw', g)], dma=True)
                    S.add('sp', lambda e, g=g: e.dma_start(out=vws[:, g, :, :], in_=vw[g, t0:t0 + 768, :].rearrange("(c p) d -> p c d", p=128)), writes=[('kw', g)], dma=True)
                for g in range(2):
                    for hq in (2 * g, 2 * g + 1):
                        for j in range(6):
                            mi = 6 if (j == 0 and ti == 0) else (7 if (j == 5 and ti == NT - 1) else j)
                            at.push(qb[:, hq, :N], ('qb', hq), kw[:, g, j * 128:(j + 1) * 128], vws[:, g, j, :], [('kw', g)], hq % 2, j == 0, False, mask=masks[:, mi, :N])
                        for ch in range(2):
                            at.push(qb[:, hq, :N], ('qb', hq), kcs[:, g, ch * 128:(ch + 1) * 128], vcs[:, ch, g * 128:(g + 1) * 128], ['kvc'], hq % 2, False, ch == 1)
                    at.flush()
                    for hq in (2 * g, 2 * g + 1):
                        at.final(hq % 2, esink[:, hq:hq + 1], R1[:, 4 + hq, :N], ('r1', 4 + hq))
            else:
                for br, qq, qn, ko, r0 in ((1, qb, 'qb', 0, 4), (3, qd, 'qd', 2, 12)):
                    for g in range(2):
                        for hq in (2 * g, 2 * g + 1):
                            for ch in range(2):
                                at.push(qq[:, hq, :N], (qn, hq), kcs[:, ko + g, ch * 128:(ch + 1) * 128], vcs[:, ch, (ko + g) * 128:(ko + g + 1) * 128], ['kvc'], hq % 2, ch == 0, ch == 1)
                        at.flush()
                        for hq in (2 * g, 2 * g + 1):
                            at.final(hq % 2, esink[:, hq:hq + 1] if br == 1 else None, R1[:, r0 + hq, :N], ('r1', r0 + hq))
            i1 = c.psn % c.psmod; c.psn += 1
            i2 = c.psn % c.psmod; c.psn += 1
            psM, mk_ = c.ps[i1], ('ps', i1); psQ, qk_ = c.ps[i2], ('ps', i2)
            for j in range(4):
                S.add('pe', lambda e, j=j: e.matmul(psM[:, :N], c.ones32[:, :], hcv[:, j, :N], start=(j == 0), stop=(j == 3)), reads=[('hcv', j), 'ones'], writes=[mk_])
                sq = c.t32[j % 2]; sk = ('t32', j % 2)
                S.add('act', lambda e, j=j, sq=sq: e.activation(out=sq[:, :N], in_=hcv[:, j, :N], func=AF.Square), reads=[('hcv', j)], writes=[sk])
                S.add('pe', lambda e, j=j, sq=sq: e.matmul(psQ[:, :N], c.ones32[:, :], sq[:, :N], start=(j == 0), stop=(j == 3)), reads=[sk, 'ones'], writes=[qk_])
            m32 = c.t32[2]; m32k = ('t32', 2); v32 = c.t32[3]; v32k = ('t32', 3); t4 = c.t32[4]; t4k = ('t32', 4)
            S.add('dve', lambda e: e.tensor_scalar(out=m32[:, :N], in0=psM[:, :N], scalar1=1.0 / 512, scalar2=None, op0=ALU.mult), reads=[mk_], writes=[m32k])
            S.add('dve', lambda e: e.tensor_tensor(out=t4[:, :N], in0=m32[:, :N], in1=m32[:, :N], op=ALU.mult), reads=[m32k], writes=[t4k])
            S.add('dve', lambda e: e.scalar_tensor_tensor(out=v32[:, :N], in0=psQ[:, :N], scalar=1.0 / 512, in1=t4[:, :N], op0=ALU.mult, op1=ALU.subtract), reads=[qk_, t4k], writes=[v32k])
            S.add('act', lambda e: e.activation(out=v32[:, :N], in_=v32[:, :N], func=AF.Sqrt, bias=float(EPS)), reads=[v32k], writes=[v32k])
            S.add('dve', lambda e: e.reciprocal(out=v32[:, :N], in_=v32[:, :N]), reads=[v32k], writes=[v32k])
            for j in range(4):
                S.add('dve', lambda e, j=j: e.tensor_tensor(out=t4[:, :N], in0=hcv[:, j, :N], in1=m32[:, :N], op=ALU.subtract), reads=[('hcv', j), m32k], writes=[t4k])
                S.add('dve', lambda e: e.tensor_tensor(out=t4[:, :N], in0=t4[:, :N], in1=v32[:, :N], op=ALU.mult), reads=[t4k, v32k], writes=[t4k])
                S.add('act', lambda e, j=j: e.activation(out=R1[:, 8 + j, :N], in_=t4[:, :N], func=AF.Silu, bias=V[:, vt.sl("ln_b", j)], scale=V[:, vt.sl("ln_g", j)]),
                      reads=[t4k, 'V'], writes=[('r1', 8 + j)])
            for cc in range(16):
                wb, wk = emit_wload(S, c, wnext())
                wbs, wsk = wsmall(wnext(2048))
                acc = c.t32[0]; ak = ('t32', 0)
                for i in range(4):
                    psG, gk_ = emit_proj(S, c, wb, wk, i, 512, N)
                    psB, bk_ = emit_proj(S, c, wbs[:, i * 512:(i + 1) * 512], wsk, 0, 128, N, kcn=4, rhs=R1[:, 4 * i:4 * i + 4, :], rhs_keys=[('r1', 4 * i + k) for k in range(4)])
                    g32 = c.t32[1]; g32k = ('t32', 1)
                    S.add('act', lambda e, psG=psG, i=i, cc=cc: e.activation(out=g32[:, :N], in_=psG[:, :N], func=AF.Sigmoid, bias=V[:, vt.sl("b_gate", i * 16 + cc)]),
                          reads=[gk_, 'V'], writes=[g32k])
                    if i == 0:
                        S.add('dve', lambda e, psB=psB: e.tensor_tensor(out=acc[:, :N], in0=g32[:, :N], in1=psB[:, :N], op=ALU.mult), reads=[g32k, bk_], writes=[ak])
                    else:
                        S.add('dve', lambda e, psB=psB: e.tensor_tensor(out=g32[:, :N], in0=g32[:, :N], in1=psB[:, :N], op=ALU.mult), reads=[g32k, bk_], writes=[g32k])
                        if i < 3:
                            S.add('dve', lambda e: e.tensor_tensor(out=acc[:, :N], in0=acc[:, :N], in1=g32[:, :N], op=ALU.add), reads=[ak, g32k], writes=[ak])
                        else:
                            S.add('dve', lambda e, cc=cc: e.tensor_tensor(out=R1[:, 16 + cc, :N], in0=acc[:, :N], in1=g32[:, :N], op=ALU.add), reads=[ak, g32k], writes=[('r1', 16 + cc)])
            mkeys = [('r1', 16 + k) for k in range(16)]
            for ob in range(4):
                wb, wk = emit_wload(S, c, wnext())
                for jj in range(4):
                    cc = ob * 4 + jj
                    ps, pk = emit_proj(S, c, wb, wk, jj, 512, N, rhs=R1[:, 16:32, :], rhs_keys=mkeys)
                    S.add('dve', lambda e, ps=ps, cc=cc: e.scalar_tensor_tensor(out=c.xT[:, cc, :N], in0=ps[:, :N], scalar=vs("g_m", cc), in1=c.xT[:, cc, :N], op0=ALU.mult, op1=ALU.add),
                          reads=[pk, 'xT', 'V'], writes=['xT'])
            emit_norm(S, c, c.xT, c.hT, N, gf[who], vs("sh_f"), 'xT')
            for hf in range(2):
                for fb in range(11):
                    wb, wk = emit_wload(S, c, wnext())
                    for pr in range(2):
                        jl = 2 * fb + pr
                        psA, ak_ = emit_proj(S, c, wb, wk, 2 * pr, 512, N)
                        psB, bk_ = emit_proj(S, c, wb, wk, 2 * pr + 1, 512, N)
                        s32 = c.t32[jl % 2]; sk = ('t32', jl % 2)
                        S.add('act', lambda e, psA=psA, s32=s32: e.activation(out=s32[:, :N], in_=psA[:, :N], func=AF.Silu), reads=[ak_], writes=[sk])
                        S.add('dve', lambda e, psB=psB, s32=s32, jl=jl: e.tensor_tensor(out=R1[:, jl, :N], in0=s32[:, :N], in1=psB[:, :N], op=ALU.mult), reads=[sk, bk_], writes=[('r1', jl)])
                ukeys = [('r1', k) for k in range(22)]
                for cp in range(8):
                    wb, wk = emit_wload(S, c, wnext(5632), L=5632)
                    for o in range(2):
                        cc = cp * 2 + o
                        ps, pk = emit_proj(S, c, wb, wk, o, 256, N, kcn=22, rhs=R1[:, 0:22, :], rhs_keys=ukeys)
                        S.add('dve', lambda e, ps=ps, cc=cc: e.scalar_tensor_tensor(out=c.xT[:, cc, :N], in0=ps[:, :N], scalar=vs("g_f", cc), in1=c.xT[:, cc, :N], op0=ALU.mult, op1=ALU.add),
                              reads=[pk, 'xT', 'V'], writes=['xT'])
            assert WOFF[0] == BPACK, (WOFF[0], BPACK)
            if last:
                emit_norm(S, c, c.xT, None, N, gfin, None, 'xT', out_f32=c.xT, out_key='xT')
                S.add('sp', lambda e: e.dma_start(out=xo[:, :, t0:t0 + N], in_=c.xT[:, :, :N]), reads=['xT'], dma=True)
            else:
                if is_ctx:
                    S.add('sp', lambda e: e.dma_start(out=co[:, :, :], in_=c.xT[:, :, :CTX]), reads=['xT'], dma=True)
                else:
                    S.add('sp', lambda e: e.dma_start(out=xo[:, :, t0:t0 + N], in_=c.xT[:, :, :N]), reads=['xT'], dma=True)
                if with_A1:
                    emit_A_tile(S, c, A, NTOK if is_ctx else t0, N, is_ctx, load=False)

        c.kvn = 0
        for t in (range(NT) if tiles is None else tiles):
            do_tile(t, t * NQ, NQ, False)
        if not last and do_ctx:
            do_tile(NT, 0, CTX, True)
        with nc.Block() as block:
            S.emit(nc, block, st)
    c.nops = len(S.ops)
    return nc


def bf(a):
    return np.ascontiguousarray(a)


def inputs_B(l, inp, mod, xTs, ctxTs, Aout, with_A1, last, wpB, wpA1=None):
    vt = vt_B(with_A1, last)
    cosT, sinT = rope_tables(); rT = rot_matrix_T()
    maps = []
    kT = [np.asarray(Aout[r]["kT"]).reshape(128, 4, TOKA) for r in range(8)]
    vv = [np.asarray(Aout[r]["v"]) for r in range(8)]
    uT = [np.asarray(Aout[r]["uT"]) for r in range(8)]
    per_batch = {}
    for b in range(2):
        rs = range(4 * b, 4 * b + 4)
        kTg = np.concatenate([kT[r][:, 2:4, :NTOK] for r in rs], axis=2)
        vfull = np.concatenate([vv[r][:NTOK] for r in rs], axis=0)
        vg = np.stack([vfull[:, 256 + g * 128:256 + (g + 1) * 128].reshape(NB_G, 8, 128, 128).transpose(0, 2, 1, 3).reshape(NB_G * 128, 1024) for g in range(2)], 0)
        kb = np.concatenate([kT[r][:, 0:2, :NTOK] for r in rs], axis=2)
        kbp = np.zeros((128, 2, SEQ + 256), kb.dtype); kbp[:, :, 128:128 + SEQ] = kb
        vbp = np.zeros((2, SEQ + 256, 128), vfull.dtype)
        for g in range(2):
            vbp[g, 128:128 + SEQ] = vfull[:, g * 128:(g + 1) * 128]
        ufull = np.concatenate([uT[r][:, :NTOK] for r in rs], axis=1)
        up = np.zeros((1024, SEQ + 30), np.float32); up[:, 15:15 + SEQ] = ufull
        uhc = np.zeros((1024, CTX + 30), np.float32); uhc[:, 15:15 + CTX] = uT[4 * b][:, NTOK:]
        per_batch[b] = dict(kTg=bf(kTg.reshape(128, 2 * SEQ)), vg=bf(vg.reshape(2 * NB_G * 128, 1024)), kbp=kbp, vbp=vbp, up=up, uhc=uhc,
                            kTc=bf(kT[4 * b][:, :, NTOK:].reshape(128, 4 * CTX)), vc=bf(vv[4 * b][NTOK:]))
    for r in range(8):
        b = r // 4; T0 = (r % 4) * NTOK
        pb = per_batch[b]
        m = {"wpack": wpB, "xT": xTs[r], "ctxT": ctxTs[b], "vec": vec_B(vt, l, b, mod, inp, with_A1, last),
             "cosT": bf(cosT[:, T0:T0 + NTOK]), "sinT": bf(sinT[:, T0:T0 + NTOK]), "rotT": rT,
             "masks": band_masks(r % 4 == 0, r % 4 == 3),
             "kTg": pb["kTg"], "vg": pb["vg"],
             "kTw": bf(pb["kbp"][:, :, T0:T0 + NTOK + 256].reshape(128, 2 * (NTOK + 256))),
             "vw": bf(pb["vbp"][:, T0:T0 + NTOK + 256].reshape(2 * (NTOK + 256), 128)),
             "kTc": pb["kTc"], "vc": pb["vc"],
             "uh": bf(pb["up"][:, T0:T0 + NTOK + 30]), "uhc": pb["uhc"]}
        if with_A1:
            m["wpackA"] = wpA1
        maps.append(m)
    return maps


_CACHE = {}


def _prog(name, fn):
    if name not in _CACHE:
        _CACHE[name] = fn()
    return _CACHE[name]


def kernel(x, c, ctx, c_ctx, w_ada, b_ada, norm_mix, norm_ffn, w_in, b_gate, conv_a_w, sink_b,
           qk_norm_q, qk_norm_k, conv_c_w, conv_c_b, ln_c_g, ln_c_b, w_branch, w_out,
           w_ffn_in, w_ffn_out, norm_final):
    inp = dict(x=x, c=c, ctx=ctx, c_ctx=c_ctx, w_ada=w_ada, b_ada=b_ada, norm_mix=norm_mix, norm_ffn=norm_ffn, w_in=w_in,
               b_gate=b_gate, conv_a_w=conv_a_w, sink_b=sink_b, qk_norm_q=qk_norm_q, qk_norm_k=qk_norm_k, conv_c_w=conv_c_w,
               conv_c_b=conv_c_b, ln_c_g=ln_c_g, ln_c_b=ln_c_b, w_branch=w_branch, w_out=w_out, w_ffn_in=w_ffn_in,
               w_ffn_out=w_ffn_out, norm_final=norm_final)
    inp = {k: np.asarray(v, np.float32) for k, v in inp.items()}
    x = inp['x']; ctx = inp['ctx']
    cores = list(range(8))
    res = run_bass_kernel_spmd(_prog('M', build_M), inputs_M(inp['c'], inp['c_ctx'], inp['w_ada'], inp['b_ada']), core_ids=cores)
    mod = gather_M(res.results)
    vtA = vt_A()
    cosT, sinT = rope_tables(); rT = rot_matrix_T()
    xTs = [np.ascontiguousarray(x[r // 4, (r % 4) * NTOK:(r % 4 + 1) * NTOK].T) for r in range(8)]
    ctxTs = [np.ascontiguousarray(ctx[b].T) for b in range(2)]
    wpA0 = pack_A(inp['w_in'][0])
    maps = []
    for r in range(8):
        b = r // 4; t0 = (r % 4) * NTOK
        maps.append({"wpack": wpA0, "xT": xTs[r], "ctxT": ctxTs[b], "vec": vec_A(vtA, 0, b, mod, inp['norm_mix'], inp['qk_norm_k']),
                     "cosT": np.ascontiguousarray(cosT[:, t0:t0 + NTOK]), "sinT": np.ascontiguousarray(sinT[:, t0:t0 + NTOK]), "rotT": rT})
    res = run_bass_kernel_spmd(_prog('A', build_A), maps, core_ids=cores)
    Aout = [dict(uT=res.results[r]["uT"], kT=res.results[r]["kT"], v=res.results[r]["v"]) for r in range(8)]
    del maps, wpA0
    wpB = pack_B(inp['w_in'][0], inp['w_branch'][0], inp['w_out'][0], inp['w_ffn_in'][0], inp['w_ffn_out'][0])
    wpA1 = pack_A(inp['w_in'][1])
    maps = inputs_B(0, inp, mod, xTs, ctxTs, Aout, True, False, wpB, wpA1)
    res = run_bass_kernel_spmd(_prog('B0', lambda: build_B(last=False, with_A1=True)), maps, core_ids=cores)
    xTs = [np.ascontiguousarray(res.results[r]["xo"]) for r in range(8)]
    ctxTs = [np.ascontiguousarray(res.results[4 * b]["co"]) for b in range(2)]
    Aout = [dict(uT=res.results[r]["uT"], kT=res.results[r]["kT"], v=res.results[r]["v"]) for r in range(8)]
    del maps, wpB, wpA1, res
    wpB = pack_B(inp['w_in'][1], inp['w_branch'][1], inp['w_out'][1], inp['w_ffn_in'][1], inp['w_ffn_out'][1])
    maps = inputs_B(1, inp, mod, xTs, ctxTs, Aout, False, True, wpB)
    res = run_bass_kernel_spmd(_prog('B1', lambda: build_B(last=True, with_A1=False)), maps, core_ids=cores)
    out = np.empty((2, SEQ, D), np.float32)
    for r in range(8):
        out[r // 4, (r % 4) * NTOK:(r % 4 + 1) * NTOK, :] = res.results[r]["xo"].T
    return out
```

```python
import numpy as np
from contextlib import ExitStack
import concourse.bass as bass
import concourse.mybir as mybir
from concourse.bass_utils import run_bass_kernel_spmd
import ml_dtypes

F32 = mybir.dt.float32
BF16 = mybir.dt.bfloat16
AF = mybir.ActivationFunctionType
ALU = mybir.AluOpType
NPBF = ml_dtypes.bfloat16

D = 2048; KC = 16; SEQ = 16384; NTOK = 4096; CTX = 256; NT = 8; NQ = 512
HD = 128; EPS = 1e-6; SCALE = HD ** -0.5
FH = 5632; FKC = 44
TOKA = NTOK + CTX
WL = 8192


class Sched:
    COMPUTE = ('pe', 'act', 'dve')
    QUEUES = ('sp', 'pool')
    RING = 8
    SEM_LIMIT = 30000

    def __init__(self):
        self.ops = []
        self.lastw = {}
        self.readers = {}

    def add(self, eng, fn, reads=(), writes=(), dma=False):
        i = len(self.ops)
        raw = set(); war = set()
        for k in reads:
            w = self.lastw.get(k)
            if w is not None:
                raw.add(w)
        for k in writes:
            w = self.lastw.get(k)
            if w is not None:
                war.add(w)
            for r in self.readers.get(k, ()):
                war.add(r)
        for k in reads:
            self.readers.setdefault(k, []).append(i)
        for k in writes:
            self.lastw[k] = i
            self.readers[k] = []
        raw.discard(i); war.discard(i)
        self.ops.append([eng, fn, raw, war, dma])
        return i

    def emit(self, nc, block, stack):
        ops = self.ops
        n = len(ops)
        eff = []
        needed = set()
        for i, (eng, fn, raw, war, dma) in enumerate(ops):
            ds = set()
            for d in raw | war:
                de, _, _, _, ddma = ops[d]
                if not ddma and not dma and de == eng:
                    if eng == 'pe':
                        continue
                ds.add(d)
            eff.append(ds)
            needed |= ds
        sig = {}
        sems = {}
        def newsem(name):
            s = stack.enter_context(nc.semaphore(name))
            return s
        cur = {}
        cnt = {}
        dq = {q: [newsem(f"dq_{q}_{r}") for r in range(self.RING)] for q in self.QUEUES}
        dcount = {q: 0 for q in self.QUEUES}
        dma_prev = {}
        for i, (eng, fn, raw, war, dma) in enumerate(ops):
            if dma:
                k = dcount[eng]; dcount[eng] += 1
                s = dq[eng][k % self.RING]
                v = 16 * (k // self.RING + 1)
                sig[i] = (s, v)
                if k >= self.RING:
                    dma_prev[i] = (s, v - 16)
            elif i in needed:
                if eng not in cur or cnt[eng] >= self.SEM_LIMIT:
                    cur[eng] = newsem(f"c_{eng}_{len(sems)}")
                    sems[len(sems)] = cur[eng]
                    cnt[eng] = 0
                cnt[eng] += 1
                sig[i] = (cur[eng], cnt[eng])
        per = {e: [] for e in self.COMPUTE + self.QUEUES}
        for i, o in enumerate(ops):
            per[o[0]].append(i)
        final_waits = [sig[i] for i, o in enumerate(ops) if o[4]]

        def run(engname, e):
            waited = {}
            def w(sv):
                s, v = sv
                key = id(s)
                if waited.get(key, 0) >= v:
                    return
                waited[key] = v
                e.wait_ge(s, v)
            for i in per[engname]:
                eng, fn, raw, war, dma = ops[i]
                for d in sorted(eff[i]):
                    w(sig[d])
                if i in dma_prev:
                    w(dma_prev[i])
                ins = fn(e)
                if i in sig:
                    s, v = sig[i]
                    ins.then_inc(s, 16 if dma else 1)
            if engname == 'sp':
                last = {}
                for s, v in final_waits:
                    if last.get(id(s), (None, 0))[1] < v:
                        last[id(s)] = (s, v)
                for s, v in last.values():
                    w((s, v))

        block.sync(lambda e: run('sp', e))
        block.gpsimd(lambda e: run('pool', e))
        block.tensor(lambda e: run('pe', e))
        block.scalar(lambda e: run('act', e))
        block.vector(lambda e: run('dve', e))


def fm(v):
    v = np.asarray(v, np.float32)
    return np.ascontiguousarray(v.reshape(-1, 128).T)


def blk(W):
    K, Fb = W.shape
    kc = K // 128
    return W.reshape(kc, 128, Fb).transpose(1, 0, 2).reshape(128, kc * Fb)


OFF_BG, OFF_CG, OFF_HA = 0, 512, 1024
OFF_BQ, OFF_BK, OFF_BV = 1536, 2048, 2304
OFF_VAL, OFF_GATE = 2560, 3072
OFF_DQ, OFF_DK, OFF_DV = 3584, 4096, 4352
OFF_G = 4608


def cols(W, starts, width=128):
    return np.concatenate([W[:, s:s + width] for s in starts], axis=1)


def pack_A(w_in_l):
    W = np.asarray(w_in_l, np.float32)
    b = []
    b.append(blk(cols(W, [OFF_CG, OFF_HA, OFF_CG + 128, OFF_HA + 128])))
    b.append(blk(cols(W, [OFF_CG + 256, OFF_HA + 256, OFF_CG + 384, OFF_HA + 384])))
    b.append(blk(cols(W, [OFF_GATE, OFF_VAL, OFF_GATE + 128, OFF_VAL + 128])))
    b.append(blk(cols(W, [OFF_GATE + 256, OFF_VAL + 256, OFF_GATE + 384, OFF_VAL + 384])))
    b.append(blk(cols(W, [OFF_BK, OFF_BK + 128, OFF_DK, OFF_DK + 128])))
    b.append(blk(cols(W, [OFF_BV, OFF_BV + 128, OFF_DV, OFF_DV + 128])))
    return np.ascontiguousarray(np.concatenate(b, axis=1))


def rope_tables():
    inv = (10000.0 ** (-np.arange(0, 64, 2, dtype=np.float32) / np.float32(64))).astype(np.float32)
    t = np.arange(SEQ)
    row = (t // 64).astype(np.float32); col = (t % 64).astype(np.float32)
    ar = row[:, None] * inv; ac = col[:, None] * inv
    ang = np.concatenate([ar, ar, ac, ac], axis=-1).astype(np.float32)
    return np.cos(ang).astype(np.float32).T.copy(), np.sin(ang).astype(np.float32).T.copy()


def rot_matrix_T():
    R = np.zeros((128, 128), np.float32)
    for a in range(2):
        for i in range(32):
            R[a * 64 + i, a * 64 + 32 + i] = -1.0
            R[a * 64 + 32 + i, a * 64 + i] = 1.0
    return np.ascontiguousarray(R.T)


class VT:
    def __init__(self):
        self.off = {}
        self.n = 0
    def add(self, name, width):
        self.off[name] = (self.n, width)
        self.n += width
    def sl(self, name, i=None):
        o, w = self.off[name]
        if i is None:
            return slice(o, o + w)
        return slice(o + i, o + i + 1)


class Ctx:
    pass


def emit_norm(S, c, xT, hT, N, gain, shift, tagx, out_f32=None, extra_reads=(), out_key='xo'):
    ps = c.ps[c.psn % c.psmod]; pk = ('ps', c.psn % c.psmod); c.psn += 1
    for kc in range(KC):
        sq = c.sqb[kc % 2]; sk = ('sqb', kc % 2)
        S.add('act', lambda e, kc=kc, sq=sq: e.activation(out=sq[:, :N], in_=xT[:, kc, :N], func=AF.Square),
              reads=[tagx] + list(extra_reads), writes=[sk])
        S.add('pe', lambda e, kc=kc, sq=sq: e.matmul(ps[:, :N], c.onesbf[:, :], sq[:, :N], start=(kc == 0), stop=(kc == KC - 1)),
              reads=[sk, 'ones'], writes=[pk])
    rs = c.t32[2]; rk = ('t32', 2)
    S.add('act', lambda e: e.activation(out=rs[:, :N], in_=ps[:, :N], func=AF.Sqrt, bias=float(EPS * D)), reads=[pk], writes=[rk])
    S.add('dve', lambda e: e.reciprocal(out=rs[:, :N], in_=rs[:, :N]), reads=[rk], writes=[rk])
    for kc in range(KC):
        tmp = c.t32[kc % 2]; tk = ('t32', kc % 2)
        S.add('dve', lambda e, kc=kc, tmp=tmp: e.tensor_tensor(out=tmp[:, :N], in0=xT[:, kc, :N], in1=rs[:, :N], op=ALU.mult),
              reads=[tagx, rk], writes=[tk])
        if out_f32 is None:
            S.add('act', lambda e, kc=kc, tmp=tmp: e.activation(out=hT[:, kc, :N], in_=tmp[:, :N], func=AF.Identity,
                                                                 bias=shift[:, kc:kc + 1], scale=gain[:, kc:kc + 1]),
                  reads=[tk], writes=['hT'])
        else:
            S.add('act', lambda e, kc=kc, tmp=tmp: e.activation(out=out_f32[:, kc, :N], in_=tmp[:, :N], func=AF.Copy,
                                                                 scale=gain[:, kc:kc + 1]),
                  reads=[tk], writes=[out_key])


def emit_wload(S, c, off, L=WL, src=None):
    slot = c.wn % c.nw; c.wn += 1
    wb = c.wring[slot]
    wbf = getattr(c, 'wbf', None) if src is None else None
    if wbf is None:
        S.add('pool', lambda e: e.dma_start(out=wb[:, :L], in_=(c.wpack if src is None else src)[:, off:off + L]), writes=[('w', slot)], dma=True)
    elif c.wfirst:
        S.add('pool', lambda e: e.dma_start(out=wb[:, :L], in_=c.wpack[:, off:off + L]), writes=[('w', slot)], dma=True)
        S.add('sp', lambda e: e.dma_start(out=wbf[:, off:off + L], in_=wb[:, :L]), reads=[('w', slot)], writes=[('wbf', off)], dma=True)
    else:
        S.add('pool', lambda e: e.dma_start(out=wb[:, :L], in_=wbf[:, off:off + L]), reads=[('wbf', off)], writes=[('w', slot)], dma=True)
    return wb, ('w', slot)


def emit_proj(S, c, wb, wk, j, Fb, N, rhs_tag='hT', kcn=KC, rhs=None, rhs_keys=None, sub=128):
    idx = c.psn % c.psmod; c.psn += 1
    ps = c.ps[idx]; pk = ('ps', idx)
    src = c.hT if rhs is None else rhs
    def fn(e):
        for kc in range(kcn):
            ins = e.matmul(ps[:, :N], wb[:, kc * Fb + j * sub: kc * Fb + j * sub + 128], src[:, kc, :N],
                           start=(kc == 0), stop=(kc == kcn - 1))
        return ins
    S.add('pe', fn, reads=[wk] + ([rhs_tag] if rhs_keys is None else list(rhs_keys)), writes=[pk])
    return ps, pk


def emit_rope(S, c, src, sk, dst, dk, N, tok_cos, tok_sin):
    idx = c.psn % c.psmod; c.psn += 1
    ps = c.ps[idx]; pk = ('ps', idx)
    S.add('pe', lambda e: e.matmul(ps[:, :N], c.rotT[:, :], src[:, :N], start=True, stop=True), reads=[sk, 'rotT'], writes=[pk])
    t1 = c.t32[3]; k1 = ('t32', 3)
    t2 = c.t32[4]; k2 = ('t32', 4)
    S.add('dve', lambda e: e.tensor_tensor(out=t1[:, :N], in0=src[:, :N], in1=tok_cos, op=ALU.mult), reads=[sk, 'rope'], writes=[k1])
    S.add('dve', lambda e: e.tensor_tensor(out=t2[:, :N], in0=ps[:, :N], in1=tok_sin, op=ALU.mult), reads=[pk, 'rope'], writes=[k2])
    S.add('dve', lambda e: e.tensor_tensor(out=dst, in0=t1[:, :N], in1=t2[:, :N], op=ALU.add), reads=[k1, k2], writes=[dk])


def emit_headnorm(S, c, ps, pk, gvec, dst, dk, N):
    sq = c.sqb[0]; sk = ('sqb', 0)
    S.add('act', lambda e: e.activation(out=sq[:, :N], in_=ps[:, :N], func=AF.Square), reads=[pk], writes=[sk])
    idx = c.psn % c.psmod; c.psn += 1
    ps2 = c.ps[idx]; pk2 = ('ps', idx)
    S.add('pe', lambda e: e.matmul(ps2[:, :N], c.onesbf[:, :], sq[:, :N], start=True, stop=True), reads=[sk, 'ones'], writes=[pk2])
    rs = c.t32[6]; rk = ('t32', 6)
    S.add('act', lambda e: e.activation(out=rs[:, :N], in_=ps2[:, :N], func=AF.Sqrt, bias=float(EPS * HD * 1.0000001)), reads=[pk2], writes=[rk])
    S.add('dve', lambda e: e.reciprocal(out=rs[:, :N], in_=rs[:, :N]), reads=[rk], writes=[rk])
    S.add('dve', lambda e: e.scalar_tensor_tensor(out=dst, in0=ps[:, :N], scalar=gvec, in1=rs[:, :N], op0=ALU.mult, op1=ALU.mult),
          reads=[pk, rk], writes=[dk])


MCOLS = 1536; MFC = 12

def build_M():
    nc = bass.Bass("TRN2", target_bir_lowering=False)
    c = Ctx()
    c.wpack = nc.dram_tensor("wpack", [128, 2 * MFC * 2048], F32, kind="ExternalInput").ap()
    cT = nc.dram_tensor("cT", [128, 16 * 3], F32, kind="ExternalInput").ap()
    bT = nc.dram_tensor("bT", [128, 2 * MFC], F32, kind="ExternalInput").ap()
    out = nc.dram_tensor("mod", [128, 2 * MFC * 3], F32, kind="ExternalOutput").ap()
    S = Sched()
    with ExitStack() as st:
        sb = lambda name, shape, dt: st.enter_context(nc.sbuf_tensor(name, shape, dt))
        c.nw = 3; c.wn = 0; c.psn = 0; c.psmod = 8
        c.wring = [sb(f"w{i}", [128, 2048], BF16) for i in range(c.nw)]
        c.ps = [st.enter_context(nc.psum_tensor(f"ps{i}", [128, 512], F32)) for i in range(8)]
        c32 = sb("c32", [128, 48], F32); s32 = sb("s32", [128, 48], F32); sbf = sb("sbf", [128, 16, 3], BF16)
        b32 = sb("b32", [128, 2 * MFC], F32); o32 = sb("o32", [128, 2 * MFC, 3], F32)
        S.add('sp', lambda e: e.dma_start(out=c32[:, :], in_=cT[:, :]), writes=['c32'], dma=True)
        S.add('sp', lambda e: e.dma_start(out=b32[:, :], in_=bT[:, :]), writes=['b32'], dma=True)
        S.add('act', lambda e: e.activation(out=s32[:, :], in_=c32[:, :], func=AF.Silu), reads=['c32'], writes=['s32'])
        S.add('dve', lambda e: e.tensor_copy(out=sbf[:].rearrange("p k v -> p (k v)"), in_=s32[:, :]), reads=['s32'], writes=['sbf'])
        for i in range(2 * MFC):
            wb, wk = emit_wload(S, c, i * 2048, 2048)
            idx = c.psn % c.psmod; c.psn += 1
            ps = c.ps[idx]; pk = ('ps', idx)
            def fn(e, wb=wb, ps=ps):
                for kc in range(KC):
                    ins = e.matmul(ps[:, :3], wb[:, kc * 128:(kc + 1) * 128], sbf[:, kc, :], start=(kc == 0), stop=(kc == KC - 1))
                return ins
            S.add('pe', fn, reads=[wk, 'sbf'], writes=[pk])
            S.add('dve', lambda e, i=i, ps=ps: e.tensor_scalar(out=o32[:, i, :], in0=ps[:, :3], scalar1=b32[:, i:i + 1], scalar2=None, op0=ALU.add),
                  reads=[pk, 'b32'], writes=['o32'])
        S.add('sp', lambda e: e.dma_start(out=out[:, :], in_=o32[:].rearrange("p i v -> p (i v)")), reads=['o32'], dma=True)
        with nc.Block() as block:
            S.emit(nc, block, st)
    return nc


def inputs_M(c, c_ctx, w_ada, b_ada):
    cs = np.stack([fm(c[0]), fm(c[1]), fm(c_ctx)], axis=-1).reshape(128, 48)
    maps = []
    for r in range(8):
        blocks = []
        bs = []
        for l in range(2):
            for fc in range(MFC):
                c0 = r * MCOLS + fc * 128
                blocks.append(blk(np.asarray(w_ada[l][:, c0:c0 + 128], np.float32)))
                bs.append(np.asarray(b_ada[l][c0:c0 + 128], np.float32))
        maps.append({"wpack": np.ascontiguousarray(np.concatenate(blocks, axis=1)),
                     "cT": np.ascontiguousarray(cs), "bT": np.ascontiguousarray(np.stack(bs, axis=1))})
    return maps


def gather_M(results):
    mod = np.zeros((2, 3, 6 * D), np.float32)
    for r in range(8):
        o = results[r]["mod"].reshape(128, 2, MFC, 3)
        for l in range(2):
            for fc in range(MFC):
                c0 = r * MCOLS + fc * 128
                mod[l, :, c0:c0 + 128] = o[:, l, fc, :].T
    return mod


def vt_A():
    vt = VT()
    for nm in ("norm_mix", "sc_lat", "sh_lat", "sc_ctx", "sh_ctx"):
        vt.add(nm, 16)
    vt.add("qk_k", 1)
    return vt


def alloc_common(nc, st, c, n_t32=10, nw=3):
    sb = lambda name, shape, dt: st.enter_context(nc.sbuf_tensor(name, shape, dt))
    c.sb = sb
    c.nw = nw; c.wn = 0; c.psn = 0; c.psmod = 8
    c.wring = [sb(f"w{i}", [128, WL], BF16) for i in range(nw)]
    c.ps = [st.enter_context(nc.psum_tensor(f"ps{i}", [128, 512], F32)) for i in range(8)]
    c.t32 = [sb(f"t32_{i}", [128, 512], F32) for i in range(n_t32)]
    c.xT = sb("xT_sb", [128, KC, NQ], F32)
    c.hT = sb("hT_sb", [128, KC, NQ], BF16)
    c.ones32 = sb("ones32", [128, 128], F32)
    c.onesbf = sb("onesbf", [128, 128], BF16)
    c.sqb = [sb(f"sqb{i}", [128, NQ], BF16) for i in range(2)]
    c.rotT = sb("rotT_sb", [128, 128], F32)
    c.cos = sb("cos", [128, NQ], F32)
    c.sin = sb("sin", [128, NQ], F32)


def emit_A_tile(S, c, A, t0, N, is_ctx, load=True):
    if load:
        if is_ctx:
            S.add('sp', lambda e: e.dma_start(out=c.xT[:, :, :CTX], in_=A.ctxT[:, :, :]), writes=['xT'], dma=True)
        else:
            S.add('sp', lambda e: e.dma_start(out=c.xT[:, :, :], in_=A.xT[:, :, t0:t0 + NQ]), writes=['xT'], dma=True)
            S.add('sp', lambda e: e.dma_start(out=c.cos[:, :], in_=A.cosT[:, t0:t0 + NQ]), writes=['rope'], dma=True)
            S.add('sp', lambda e: e.dma_start(out=c.sin[:, :], in_=A.sinT[:, t0:t0 + NQ]), writes=['rope'], dma=True)
    gain, shift = (A.gc, A.shc) if is_ctx else (A.gl, A.shl)
    gk = A.gk
    emit_norm(S, c, c.xT, c.hT, N, gain, shift, 'xT', extra_reads=['gains', 'V', 'ones'])
    for b in range(4):
        wb, wk = emit_wload(S, c, A.woff + b * WL, src=A.wpack)
        for pr in range(2):
            j = (b % 2) * 2 + pr
            ps1, pk1 = emit_proj(S, c, wb, wk, 2 * pr, 512, N)
            ps2, pk2 = emit_proj(S, c, wb, wk, 2 * pr + 1, 512, N)
            tmp = c.t32[7]; tk = ('t32', 7)
            fnc = AF.Copy if b < 2 else AF.Sigmoid
            S.add('act', lambda e, ps1=ps1, fnc=fnc, tmp=tmp: e.activation(out=tmp[:, :N], in_=ps1[:, :N], func=fnc), reads=[pk1], writes=[tk])
            ub = A.ubuf[A.un % 2]; uk = A.ukeys[A.un % 2]; A.un += 1
            S.add('dve', lambda e, ub=ub, tmp=tmp, ps2=ps2: e.tensor_tensor(out=ub[:, :N], in0=tmp[:, :N], in1=ps2[:, :N], op=ALU.mult),
                  reads=[tk, pk2], writes=[uk])
            jj = j + (0 if b < 2 else 4)
            S.add('sp', lambda e, ub=ub, jj=jj: e.dma_start(out=A.uT[:, jj, t0:t0 + N], in_=ub[:, :N]), reads=[uk], dma=True)
    wb, wk = emit_wload(S, c, A.woff + 4 * WL, src=A.wpack)
    for hh in range(4):
        ps, pk = emit_proj(S, c, wb, wk, hh, 512, N)
        kb = A.kbuf[A.kn % 2]; kk = ('kbuf', A.kn % 2); A.kn += 1
        k32 = c.t32[6]; k32k = ('t32', 6)
        if hh < 2:
            if is_ctx:
                S.add('act', lambda e, ps=ps, kb=kb: e.activation(out=kb[:, :N], in_=ps[:, :N], func=AF.Copy), reads=[pk], writes=[kk])
            else:
                S.add('act', lambda e, ps=ps, k32=k32: e.activation(out=k32[:, :N], in_=ps[:, :N], func=AF.Copy), reads=[pk], writes=[k32k])
                emit_rope(S, c, k32, k32k, kb[:, :N], kk, N, c.cos[:, :N], c.sin[:, :N])
        else:
            if is_ctx:
                emit_headnorm(S, c, ps, pk, gk[:, 0:1], kb[:, :N], kk, N)
            else:
                emit_headnorm(S, c, ps, pk, gk[:, 0:1], k32[:, :N], k32k, N)
                emit_rope(S, c, k32, k32k, kb[:, :N], kk, N, c.cos[:, :N], c.sin[:, :N])
        S.add('sp', lambda e, kb=kb, hh=hh: e.dma_start(out=A.kT[:, hh, t0:t0 + N], in_=kb[:, :N]), reads=[kk], dma=True)
    wb, wk = emit_wload(S, c, A.woff + 5 * WL, src=A.wpack)
    for s in range(N // 128):
        idx = c.psn % c.psmod; c.psn += 1
        ps = c.ps[idx]; pk = ('ps', idx)
        def fn(e, s=s, ps=ps, wb=wb):
            for kc in range(KC):
                ins = e.matmul(ps[:, :], c.hT[:, kc, s * 128:(s + 1) * 128], wb[:, kc * 512:(kc + 1) * 512], start=(kc == 0), stop=(kc == KC - 1))
            return ins
        S.add('pe', fn, reads=[wk, 'hT'], writes=[pk])
        vb = A.vbuf[A.vn % 2]; vk = ('vbuf', A.vn % 2); A.vn += 1
        S.add('act', lambda e, ps=ps, vb=vb: e.activation(out=vb[:, :], in_=ps[:, :], func=AF.Copy), reads=[pk], writes=[vk])
        S.add('sp', lambda e, vb=vb, s=s: e.dma_start(out=A.vO[t0 + s * 128:t0 + (s + 1) * 128, :], in_=vb[:, :]), reads=[vk], dma=True)


def setup_A(nc, S, c, A, V, vt, sfx=""):
    sb = c.sb
    A.gl = sb("A_gain_lat", [128, 16], F32); A.gc = sb("A_gain_ctx", [128, 16], F32); A.gk = sb("A_gk", [128, 1], F32)
    A.shl = V[:, vt.sl("sh_lat" + sfx)]; A.shc = V[:, vt.sl("sh_ctx" + sfx)]
    A.kbuf = [sb(f"kbuf{i}", [128, NQ], BF16) for i in range(2)]
    A.vbuf = [sb(f"vbuf{i}", [128, 512], BF16) for i in range(2)]
    A.un = A.kn = A.vn = 0
    if not hasattr(A, 'ukeys'):
        A.ukeys = [('ubuf', 0), ('ubuf', 1)]
    sqD = float(np.sqrt(D))
    for g, scn in ((A.gl, "sc_lat" + sfx), (A.gc, "sc_ctx" + sfx)):
        S.add('dve', lambda e, g=g, scn=scn: e.tensor_scalar(out=g[:, :], in0=V[:, vt.sl(scn)], scalar1=1.0, scalar2=sqD, op0=ALU.add, op1=ALU.mult),
              reads=['V'], writes=['gains'])
        S.add('dve', lambda e, g=g: e.tensor_tensor(out=g[:, :], in0=g[:, :], in1=V[:, vt.sl("norm_mix" + sfx)], op=ALU.mult),
              reads=['V', 'gains'], writes=['gains'])
    S.add('dve', lambda e: e.tensor_scalar(out=A.gk[:, :], in0=V[:, vt.sl("qk_k" + sfx)], scalar1=float(np.sqrt(HD)), scalar2=None, op0=ALU.mult),
          reads=['V'], writes=['gains'])


def declare_A_outputs(nc, A):
    A.uT = nc.dram_tensor("uT", [1024, TOKA], F32, kind="ExternalOutput").ap().rearrange("(j p) t -> p j t", p=128)
    A.kT = nc.dram_tensor("kT", [128, 4 * TOKA], BF16, kind="ExternalOutput").ap().rearrange("p (h t) -> p h t", h=4)
    A.vO = nc.dram_tensor("v", [TOKA, 512], BF16, kind="ExternalOutput").ap()


def build_A():
    nc = bass.Bass("TRN2", target_bir_lowering=False)
    c = Ctx(); A = Ctx()
    vt = vt_A()
    A.wpack = nc.dram_tensor("wpack", [128, 6 * WL], F32, kind="ExternalInput").ap(); A.woff = 0
    A.xT = nc.dram_tensor("xT", [D, NTOK], F32, kind="ExternalInput").ap().rearrange("(k p) t -> p k t", p=128)
    A.ctxT = nc.dram_tensor("ctxT", [D, CTX], F32, kind="ExternalInput").ap().rearrange("(k p) t -> p k t", p=128)
    vec = nc.dram_tensor("vec", [128, vt.n], F32, kind="ExternalInput").ap()
    A.cosT = nc.dram_tensor("cosT", [128, NTOK], F32, kind="ExternalInput").ap()
    A.sinT = nc.dram_tensor("sinT", [128, NTOK], F32, kind="ExternalInput").ap()
    rotT = nc.dram_tensor("rotT", [128, 128], F32, kind="ExternalInput").ap()
    declare_A_outputs(nc, A)
    S = Sched()
    with ExitStack() as st:
        alloc_common(nc, st, c)
        V = c.sb("vecs", [128, vt.n], F32)
        A.ubuf = [c.sb(f"ubuf{i}", [128, NQ], F32) for i in range(2)]
        S.add('sp', lambda e: e.dma_start(out=V[:, :], in_=vec[:, :]), writes=['V'], dma=True)
        S.add('sp', lambda e: e.dma_start(out=c.rotT[:, :], in_=rotT[:, :]), writes=['rotT'], dma=True)
        S.add('dve', lambda e: e.memset(c.ones32[:, :], 1.0), writes=['ones'])
        S.add('dve', lambda e: e.memset(c.onesbf[:, :], 1.0), writes=['ones'])
        setup_A(nc, S, c, A, V, vt)
        for t in range(NT):
            emit_A_tile(S, c, A, t * NQ, NQ, False)
        emit_A_tile(S, c, A, NTOK, CTX, True)
        with nc.Block() as block:
            S.emit(nc, block, st)
    return nc


def vec_A(vt, l, b, mod, norm_mix, qk_norm_k):
    V = np.zeros((128, vt.n), np.float32)
    V[:, vt.sl("norm_mix")] = fm(norm_mix[l])
    V[:, vt.sl("sh_lat")] = fm(mod[l, b, 0:D]); V[:, vt.sl("sc_lat")] = fm(mod[l, b, D:2 * D])
    V[:, vt.sl("sh_ctx")] = fm(mod[l, 2, 0:D]); V[:, vt.sl("sc_ctx")] = fm(mod[l, 2, D:2 * D])
    V[:, vt.sl("qk_k")] = fm(qk_norm_k[l])
    return V


NB_G = 16
BPACK = 3 * WL + 16 * (WL + 2048) + 4 * WL + 22 * WL + 16 * 5632


def pack_B(w_in_l, w_branch_l, w_out_l, w_fi_l, w_fo_l):
    W = np.asarray(w_in_l, np.float32)
    b = [blk(W[:, OFF_BG:OFF_BG + 512]), blk(W[:, OFF_BQ:OFF_BQ + 512]), blk(W[:, OFF_DQ:OFF_DQ + 512])]
    for cc in range(16):
        b.append(blk(cols(W, [OFF_G + i * D + cc * 128 for i in range(4)])))
        b.append(np.concatenate([blk(np.asarray(w_branch_l[i][:, cc * 128:(cc + 1) * 128], np.float32)) for i in range(4)], axis=1))
    Wo = np.asarray(w_out_l, np.float32)
    for ob in range(4):
        b.append(blk(Wo[:, ob * 512:(ob + 1) * 512]))
    Wi = np.asarray(w_fi_l, np.float32); Wf = np.asarray(w_fo_l, np.float32)
    for hf in range(2):
        for fb in range(11):
            j0 = hf * 22 + 2 * fb
            b.append(blk(cols(Wi, [j0 * 128, FH + j0 * 128, (j0 + 1) * 128, FH + (j0 + 1) * 128])))
        for cp in range(8):
            b.append(blk(Wf[hf * 2816:(hf + 1) * 2816, cp * 256:(cp + 1) * 256]))
    out = np.ascontiguousarray(np.concatenate(b, axis=1))
    assert out.shape == (128, BPACK), out.shape
    return out


def vt_B(with_A1, last):
    vt = VT()
    for nm in ("norm_mix", "norm_ffn"):
        vt.add(nm, 16)
    for who in ("lat", "ctx"):
        for nm in ("sh_m", "sc_m", "g_m", "sh_f", "sc_f", "g_f"):
            vt.add(f"{nm}_{who}", 16)
    vt.add("b_gate", 64)
    vt.add("conv_a", 12)
    vt.add("conv_c", 124)
    vt.add("conv_cb", 4); vt.add("ln_g", 4); vt.add("ln_b", 4)
    vt.add("qk_q", 1); vt.add("sink", 4)
    if last:
        vt.add("norm_final", 16)
    if with_A1:
        for nm in ("norm_mix1", "sc_lat1", "sh_lat1", "sc_ctx1", "sh_ctx1"):
            vt.add(nm, 16)
        vt.add("qk_k1", 1)
    return vt


def vec_B(vt, l, b, mod, inp, with_A1, last):
    V = np.zeros((128, vt.n), np.float32)
    V[:, vt.sl("norm_mix")] = fm(inp['norm_mix'][l]); V[:, vt.sl("norm_ffn")] = fm(inp['norm_ffn'][l])
    for who, mv in (("lat", mod[l, b]), ("ctx", mod[l, 2])):
        for i, nm in enumerate(("sh_m", "sc_m", "g_m", "sh_f", "sc_f", "g_f")):
            V[:, vt.sl(f"{nm}_{who}")] = fm(mv[i * D:(i + 1) * D])
    V[:, vt.sl("b_gate")] = np.concatenate([fm(inp['b_gate'][l][i]) for i in range(4)], axis=1)
    V[:, vt.sl("conv_a")] = np.concatenate([fm(inp['conv_a_w'][l][j]) for j in range(3)], axis=1)
    V[:, vt.sl("conv_c")] = np.concatenate([fm(inp['conv_c_w'][l][j]) for j in range(31)], axis=1)
    V[:, vt.sl("conv_cb")] = fm(inp['conv_c_b'][l]); V[:, vt.sl("ln_g")] = fm(inp['ln_c_g'][l]); V[:, vt.sl("ln_b")] = fm(inp['ln_c_b'][l])
    V[:, vt.sl("qk_q")] = fm(inp['qk_norm_q'][l])
    V[:, vt.sl("sink")] = np.broadcast_to(np.asarray(inp['sink_b'][l], np.float32)[None, :], (128, 4))
    if last:
        V[:, vt.sl("norm_final")] = fm(inp['norm_final'])
    if with_A1:
        l1 = l + 1
        V[:, vt.sl("norm_mix1")] = fm(inp['norm_mix'][l1])
        V[:, vt.sl("sh_lat1")] = fm(mod[l1, b, 0:D]); V[:, vt.sl("sc_lat1")] = fm(mod[l1, b, D:2 * D])
        V[:, vt.sl("sh_ctx1")] = fm(mod[l1, 2, 0:D]); V[:, vt.sl("sc_ctx1")] = fm(mod[l1, 2, D:2 * D])
        V[:, vt.sl("qk_k1")] = fm(inp['qk_norm_k'][l1])
    return V


def band_masks(first, last_):
    kk = np.arange(128)[:, None]; q = np.arange(512)[None, :]
    m = np.zeros((128, 8, 512), np.float32)
    for j in range(6):
        m[:, j, :] = (np.abs((j - 1) * 128 + kk - q) <= 128)
    m[:, 6, :] = 0.0 if first else m[:, 0, :]
    m[:, 7, :] = 0.0 if last_ else m[:, 5, :]
    return m.reshape(128, 8 * 512).astype(NPBF)


class Attn:
    def __init__(self, S, c, N):
        self.S, self.c, self.N, self.pend = S, c, N, None

    def push(self, q, qk, KT, V, kvk, acc, first, last, mask=None):
        S, c, N = self.S, self.c, self.N
        idx = c.psn % c.psmod; c.psn += 1
        psS = c.ps[idx]; sk = ('ps', idx)
        S.add('pe', lambda e: e.matmul(psS[:, :N], KT, q, start=True, stop=True), reads=[qk] + list(kvk), writes=[sk])
        pi = c.pn % len(c.P); c.pn += 1
        P = c.P[pi]; pk = ('P', pi)
        S.add('act', lambda e: e.activation(out=P[:, :N], in_=psS[:, :N], func=AF.Exp), reads=[sk], writes=[pk])
        if mask is not None:
            S.add('dve', lambda e: e.tensor_tensor(out=P[:, :N], in0=P[:, :N], in1=mask, op=ALU.mult), reads=[pk, 'masks'], writes=[pk])
        prev = self.pend
        self.pend = (P, pk, V, list(kvk), acc, first, last)
        if prev is not None:
            self._pv(prev)

    def _pv(self, it):
        S, c, N = self.S, self.c, self.N
        P, pk, V, kvk, acc, first, last = it
        psO = c.ps[4 + 2 * acc]; ok = ('ps', 4 + 2 * acc)
        psZ = c.ps[5 + 2 * acc]; zk = ('ps', 5 + 2 * acc)
        S.add('pe', lambda e: e.matmul(psO[:, :N], V, P[:, :N], start=first, stop=last), reads=[pk] + kvk, writes=[ok])
        za = c.zacc[acc]; zak = ('zacc', acc)
        if first:
            S.add('dve', lambda e: e.tensor_copy(out=za[:, :N], in_=P[:, :N]), reads=[pk], writes=[zak])
        else:
            S.add('dve', lambda e: e.tensor_tensor(out=za[:, :N], in0=za[:, :N], in1=P[:, :N], op=ALU.add), reads=[pk, zak], writes=[zak])
        if last:
            S.add('pe', lambda e: e.matmul(psZ[:, :N], c.ones32[:, :], za[:, :N], start=True, stop=True), reads=[zak, 'ones'], writes=[zk])

    def flush(self):
        if self.pend is not None:
            self._pv(self.pend)
            self.pend = None

    def final(self, acc, sink, out, out_key):
        S, c, N = self.S, self.c, self.N
        psO = c.ps[4 + 2 * acc]; ok = ('ps', 4 + 2 * acc)
        psZ = c.ps[5 + 2 * acc]; zk = ('ps', 5 + 2 * acc)
        z = c.t32[5]; zkk = ('t32', 5)
        if sink is not None:
            S.add('dve', lambda e: e.tensor_scalar(out=z[:, :N], in0=psZ[:, :N], scalar1=sink, scalar2=None, op0=ALU.add), reads=[zk, 'esink'], writes=[zkk])
        else:
            S.add('dve', lambda e: e.tensor_copy(out=z[:, :N], in_=psZ[:, :N]), reads=[zk], writes=[zkk])
        S.add('dve', lambda e: e.reciprocal(out=z[:, :N], in_=z[:, :N]), reads=[zkk], writes=[zkk])
        S.add('dve', lambda e: e.tensor_tensor(out=out, in0=psO[:, :N], in1=z[:, :N], op=ALU.mult), reads=[ok, zkk], writes=[out_key])


def build_B(last=False, with_A1=False, tiles=None, do_ctx=True):
    nc = bass.Bass("TRN2", target_bir_lowering=False)
    c = Ctx(); A = Ctx()
    vt = vt_B(with_A1, last)
    dt = lambda name, shape, dty, kind="ExternalInput": nc.dram_tensor(name, shape, dty, kind=kind).ap()
    c.wpack = dt("wpack", [128, BPACK], F32)
    c.wbf = dt("wbf", [128, BPACK], BF16, kind="ExternalOutput")
    c.wfirst = True
    xT = dt("xT", [D, NTOK], F32).rearrange("(k p) t -> p k t", p=128)
    ctxT = dt("ctxT", [D, CTX], F32).rearrange("(k p) t -> p k t", p=128)
    vec = dt("vec", [128, vt.n], F32)
    cosT = dt("cosT", [128, NTOK], F32); sinT = dt("sinT", [128, NTOK], F32); rotT = dt("rotT", [128, 128], F32)
    masksD = dt("masks", [128, 8 * 512], BF16)
    kTg = dt("kTg", [128, 2 * SEQ], BF16).rearrange("p (g t) -> p g t", g=2)
    vg = dt("vg", [2 * NB_G * 128, 1024], BF16).rearrange("(g n p) f -> g n p f", g=2, n=NB_G)
    kTw = dt("kTw", [128, 2 * (NTOK + 256)], BF16).rearrange("p (g t) -> p g t", g=2)
    vw = dt("vw", [2 * (NTOK + 256), 128], BF16).rearrange("(g t) d -> g t d", g=2)
    kTc = dt("kTc", [128, 4 * CTX], BF16)
    vcD = dt("vc", [CTX, 512], BF16)
    uh = dt("uh", [1024, NTOK + 30], F32).rearrange("(j p) t -> p j t", p=128)
    uhc = dt("uhc", [1024, CTX + 30], F32).rearrange("(j p) t -> p j t", p=128)
    xo = dt("xo", [D, NTOK], F32, kind="ExternalOutput").rearrange("(k p) t -> p k t", p=128)
    if not last:
        co = dt("co", [D, CTX], F32, kind="ExternalOutput").rearrange("(k p) t -> p k t", p=128)
    if with_A1:
        A.wpack = dt("wpackA", [128, 6 * WL], F32); A.woff = 0
        declare_A_outputs(nc, A)
    S = Sched()
    with ExitStack() as st:
        alloc_common(nc, st, c, n_t32=8, nw=2)
        c.psmod = 4
        sb = c.sb
        V = sb("vecs", [128, vt.n], F32)
        c.wsm = [sb(f"wsm{i}", [128, 2048], BF16) for i in range(2)]; c.wsn = 0
        R1 = sb("R1", [128, 32, NQ], BF16)
        masks = sb("masks_sb", [128, 8, 512], BF16)
        qb = sb("qb", [128, 4, NQ], BF16); qd = sb("qd", [128, 4, NQ], BF16)
        kring = [sb(f"kring{i}", [128, 1024], BF16) for i in range(2)]
        vring = [sb(f"vring{i}", [128, 8, 128], BF16) for i in range(2)]
        kw = sb("kw", [128, 2, 768], BF16); vws = sb("vws", [128, 2, 6, 128], BF16)
        kcs = sb("kcs", [128, 4, CTX], BF16); vcs = sb("vcs", [128, 2, 512], BF16)
        c.P = [sb(f"P{i}", [128, NQ], BF16) for i in range(2)]; c.pn = 0
        c.zacc = [sb(f"zacc{i}", [128, NQ], F32) for i in range(2)]
        ua = [sb(f"ua{i}", [128, NQ + 2], F32) for i in range(2)]
        uc = [sb(f"uc{i}", [128, NQ + 30], F32) for i in range(2)]
        hcv = sb("hcv", [128, 4, NQ], F32)
        gm = {w: sb(f"gain_m_{w}", [128, 16], F32) for w in ("lat", "ctx")}
        gf = {w: sb(f"gain_f_{w}", [128, 16], F32) for w in ("lat", "ctx")}
        gq = sb("gq", [128, 1], F32); esink = sb("esink", [128, 4], F32)
        if last:
            gfin = sb("gfin", [128, 16], F32)
        S.add('sp', lambda e: e.dma_start(out=V[:, :], in_=vec[:, :]), writes=['V'], dma=True)
        S.add('sp', lambda e: e.dma_start(out=c.rotT[:, :], in_=rotT[:, :]), writes=['rotT'], dma=True)
        S.add('sp', lambda e: e.dma_start(out=masks[:].rearrange("p a b -> p (a b)"), in_=masksD[:, :]), writes=['masks'], dma=True)
        S.add('sp', lambda e: e.dma_start(out=kcs[:].rearrange("p a b -> p (a b)"), in_=kTc[:, :]), writes=['kvc'], dma=True)
        S.add('sp', lambda e: e.dma_start(out=vcs[:], in_=vcD.rearrange("(c p) f -> p c f", p=128)), writes=['kvc'], dma=True)
        S.add('dve', lambda e: e.memset(c.ones32[:, :], 1.0), writes=['ones'])
        S.add('dve', lambda e: e.memset(c.onesbf[:, :], 1.0), writes=['ones'])
        sqD = float(np.sqrt(D))
        for w in ("lat", "ctx"):
            for g, scn, nn in ((gm[w], f"sc_m_{w}", "norm_mix"), (gf[w], f"sc_f_{w}", "norm_ffn")):
                S.add('dve', lambda e, g=g, scn=scn: e.tensor_scalar(out=g[:, :], in0=V[:, vt.sl(scn)], scalar1=1.0, scalar2=sqD, op0=ALU.add, op1=ALU.mult),
                      reads=['V'], writes=['gains'])
                S.add('dve', lambda e, g=g, nn=nn: e.tensor_tensor(out=g[:, :], in0=g[:, :], in1=V[:, vt.sl(nn)], op=ALU.mult),
                      reads=['V', 'gains'], writes=['gains'])
        S.add('dve', lambda e: e.tensor_scalar(out=gq[:, :], in0=V[:, vt.sl("qk_q")], scalar1=float(np.sqrt(HD) * SCALE), scalar2=None, op0=ALU.mult),
              reads=['V'], writes=['gains'])
        S.add('act', lambda e: e.activation(out=esink[:, :], in_=V[:, vt.sl("sink")], func=AF.Exp), reads=['V'], writes=['esink'])
        if last:
            S.add('dve', lambda e: e.tensor_scalar(out=gfin[:, :], in0=V[:, vt.sl("norm_final")], scalar1=sqD, scalar2=None, op0=ALU.mult),
                  reads=['V'], writes=['gains'])
        if with_A1:
            A.ubuf = [c.t32[0], c.t32[1]]; A.ukeys = [('t32', 0), ('t32', 1)]
            setup_A(nc, S, c, A, V, vt, sfx="1")
            A.ubuf_keys = True
        WOFF = [0]

        def wnext(L=WL):
            off = WOFF[0]; WOFF[0] += L
            return off

        def wsmall(off):
            slot = c.wsn % 2; c.wsn += 1
            wbs = c.wsm[slot]
            if c.wfirst:
                S.add('pool', lambda e: e.dma_start(out=wbs[:, :], in_=c.wpack[:, off:off + 2048]), writes=[('wsm', slot)], dma=True)
                S.add('sp', lambda e: e.dma_start(out=c.wbf[:, off:off + 2048], in_=wbs[:, :]), reads=[('wsm', slot)], writes=[('wbf', off)], dma=True)
            else:
                S.add('pool', lambda e: e.dma_start(out=wbs[:, :], in_=c.wbf[:, off:off + 2048]), reads=[('wbf', off)], writes=[('wsm', slot)], dma=True)
            return wbs, ('wsm', slot)

        def do_tile(ti, t0, N, is_ctx):
            who = "ctx" if is_ctx else "lat"
            WOFF[0] = 0
            vs = lambda nm, i=None: V[:, vt.sl(f"{nm}_{who}", i)]
            if is_ctx:
                S.add('sp', lambda e: e.dma_start(out=c.xT[:, :, :CTX], in_=ctxT[:, :, :]), writes=['xT'], dma=True)
            else:
                S.add('sp', lambda e: e.dma_start(out=c.xT[:, :, :], in_=xT[:, :, t0:t0 + NQ]), writes=['xT'], dma=True)
                S.add('sp', lambda e: e.dma_start(out=c.cos[:, :], in_=cosT[:, t0:t0 + NQ]), writes=['rope'], dma=True)
                S.add('sp', lambda e: e.dma_start(out=c.sin[:, :], in_=sinT[:, t0:t0 + NQ]), writes=['rope'], dma=True)
            emit_norm(S, c, c.xT, c.hT, N, gm[who], vs("sh_m"), 'xT', extra_reads=['gains', 'V', 'ones'])
            wb, wk = emit_wload(S, c, wnext())
            usrc = uhc if is_ctx else uh
            for j in range(4):
                ub = ua[j % 2]; uk = ('ua', j % 2)
                S.add('sp', lambda e, ub=ub, j=j: e.dma_start(out=ub[:, :N + 2], in_=usrc[:, j, t0 + 14:t0 + 14 + N + 2]), writes=[uk], dma=True)
                acc = c.t32[3]; ak = ('t32', 3)
                for tap in range(3):
                    wcol = V[:, vt.off["conv_a"][0] + tap * 4 + j: vt.off["conv_a"][0] + tap * 4 + j + 1]
                    if tap == 0:
                        S.add('dve', lambda e, ub=ub, wcol=wcol: e.tensor_scalar(out=acc[:, :N], in0=ub[:, 0:N], scalar1=wcol, scalar2=None, op0=ALU.mult),
                              reads=[uk, 'V'], writes=[ak])
                    else:
                        S.add('dve', lambda e, ub=ub, wcol=wcol, tap=tap: e.scalar_tensor_tensor(out=acc[:, :N], in0=ub[:, tap:tap + N], scalar=wcol, in1=acc[:, :N], op0=ALU.mult, op1=ALU.add),
                              reads=[uk, 'V', ak], writes=[ak])
                ps, pk = emit_proj(S, c, wb, wk, j, 512, N)
                S.add('dve', lambda e, ps=ps, j=j: e.tensor_tensor(out=R1[:, j, :N], in0=acc[:, :N], in1=ps[:, :N], op=ALU.mult), reads=[ak, pk], writes=[('r1', j)])
            wb, wk = emit_wload(S, c, wnext())
            q32 = c.t32[6]; q32k = ('t32', 6)
            for hq in range(4):
                ps, pk = emit_proj(S, c, wb, wk, hq, 512, N)
                if is_ctx:
                    S.add('act', lambda e, ps=ps, hq=hq: e.activation(out=qb[:, hq, :N], in_=ps[:, :N], func=AF.Copy, scale=float(SCALE)), reads=[pk], writes=[('qb', hq)])
                else:
                    S.add('act', lambda e, ps=ps: e.activation(out=q32[:, :N], in_=ps[:, :N], func=AF.Copy, scale=float(SCALE)), reads=[pk], writes=[q32k])
                    emit_rope(S, c, q32, q32k, qb[:, hq, :N], ('qb', hq), N, c.cos[:, :N], c.sin[:, :N])
            wb, wk = emit_wload(S, c, wnext())
            for hq in range(4):
                ps, pk = emit_proj(S, c, wb, wk, hq, 512, N)
                if is_ctx:
                    emit_headnorm(S, c, ps, pk, gq[:, 0:1], qd[:, hq, :N], ('qd', hq), N)
                else:
                    emit_headnorm(S, c, ps, pk, gq[:, 0:1], q32[:, :N], q32k, N)
                    emit_rope(S, c, q32, q32k, qd[:, hq, :N], ('qd', hq), N, c.cos[:, :N], c.sin[:, :N])
            for j in range(4):
                ub = uc[j % 2]; uk = ('uc', j % 2)
                S.add('sp', lambda e, ub=ub, j=j: e.dma_start(out=ub[:, :N + 30], in_=usrc[:, 4 + j, t0:t0 + N + 30]), writes=[uk], dma=True)
                hk = ('hcv', j)
                for tap in range(31):
                    wcol = V[:, vt.off["conv_c"][0] + tap * 4 + j: vt.off["conv_c"][0] + tap * 4 + j + 1]
                    if tap == 0:
                        S.add('dve', lambda e, ub=ub, wcol=wcol, j=j: e.tensor_scalar(out=hcv[:, j, :N], in0=ub[:, 0:N], scalar1=wcol, scalar2=V[:, vt.sl("conv_cb", j)], op0=ALU.mult, op1=ALU.add),
                              reads=[uk, 'V'], writes=[hk])
                    else:
                        S.add('dve', lambda e, ub=ub, wcol=wcol, tap=tap, j=j: e.scalar_tensor_tensor(out=hcv[:, j, :N], in0=ub[:, tap:tap + N], scalar=wcol, in1=hcv[:, j, :N], op0=ALU.mult, op1=ALU.add),
                                  reads=[uk, 'V', hk], writes=[hk])
            at = Attn(S, c, N)
            if not is_ctx:
                for g in range(2):
                    for cg in range(NB_G):
                        ri = c.kvn % 2; c.kvn += 1
                        kr, vr, kvk = kring[ri], vring[ri], ('kvr', ri)
                        S.add('sp', lambda e, kr=kr, g=g, cg=cg: e.dma_start(out=kr[:, :], in_=kTg[:, g, cg * 1024:(cg + 1) * 1024]), writes=[kvk], dma=True)
                        S.add('sp', lambda e, vr=vr, g=g, cg=cg: e.dma_start(out=vr[:].rearrange("p a b -> p (a b)"), in_=vg[g, cg, :, :]), writes=[kvk], dma=True)
                        for ch in range(8):
                            for hq in (2 * g, 2 * g + 1):
                                at.push(qd[:, hq, :N], ('qd', hq), kr[:, ch * 128:(ch + 1) * 128], vr[:, ch, :], [kvk], hq % 2, cg == 0 and ch == 0, False)
                    for ch in range(2):
                        for hq in (2 * g, 2 * g + 1):
                            at.push(qd[:, hq, :N], ('qd', hq), kcs[:, 2 + g, ch * 128:(ch + 1) * 128], vcs[:, ch, (2 + g) * 128:(3 + g) * 128], ['kvc'], hq % 2, False, ch == 1)
                    at.flush()
                    for hq in (2 * g, 2 * g + 1):
                        at.final(hq % 2, None, R1[:, 12 + hq, :N], ('r1', 12 + hq))
                for g in range(2):
                    S.add('sp', lambda e, g=g: e.dma_start(out=kw[:, g, :], in_=kTw[:, g, t0:t0 + 768]), writes=[('kw', g)], dma=True)
                    S.add('sp', lambda e, g=g: e.dma_start(out=vws[:, g, :, :], in_=vw[g, t0:t0 + 768, :].rearrange("(c p) d -> p c d", p=128)), writes=[('kw', g)], dma=True)
                for g in range(2):
                    for hq in (2 * g, 2 * g + 1):
                        for j in range(6):
                            mi = 6 if (j == 0 and ti == 0) else (7 if (j == 5 and ti == NT - 1) else j)
                            at.push(qb[:, hq, :N], ('qb', hq), kw[:, g, j * 128:(j + 1) * 128], vws[:, g, j, :], [('kw', g)], hq % 2, j == 0, False, mask=masks[:, mi, :N])
                        for ch in range(2):
                            at.push(qb[:, hq, :N], ('qb', hq), kcs[:, g, ch * 128:(ch + 1) * 128], vcs[:, ch, g * 128:(g + 1) * 128], ['kvc'], hq % 2, False, ch == 1)
                    at.flush()
                    for hq in (2 * g, 2 * g + 1):
                        at.final(hq % 2, esink[:, hq:hq + 1], R1[:, 4 + hq, :N], ('r1', 4 + hq))
            else:
                for br, qq, qn, ko, r0 in ((1, qb, 'qb', 0, 4), (3, qd, 'qd', 2, 12)):
                    for g in range(2):
                        for hq in (2 * g, 2 * g + 1):
                            for ch in range(2):
                                at.push(qq[:, hq, :N], (qn, hq), kcs[:, ko + g, ch * 128:(ch + 1) * 128], vcs[:, ch, (ko + g) * 128:(ko + g + 1) * 128], ['kvc'], hq % 2, ch == 0, ch == 1)
                        at.flush()
                        for hq in (2 * g, 2 * g + 1):
                            at.final(hq % 2, esink[:, hq:hq + 1] if br == 1 else None, R1[:, r0 + hq, :N], ('r1', r0 + hq))
            i1 = c.psn % c.psmod; c.psn += 1
            i2 = c.psn % c.psmod; c.psn += 1
            psM, mk_ = c.ps[i1], ('ps', i1); psQ, qk_ = c.ps[i2], ('ps', i2)
            for j in range(4):
                S.add('pe', lambda e, j=j: e.matmul(psM[:, :N], c.ones32[:, :], hcv[:, j, :N], start=(j == 0), stop=(j == 3)), reads=[('hcv', j), 'ones'], writes=[mk_])
                sq = c.t32[j % 2]; sk = ('t32', j % 2)
                S.add('act', lambda e, j=j, sq=sq: e.activation(out=sq[:, :N], in_=hcv[:, j, :N], func=AF.Square), reads=[('hcv', j)], writes=[sk])
                S.add('pe', lambda e, j=j, sq=sq: e.matmul(psQ[:, :N], c.ones32[:, :], sq[:, :N], start=(j == 0), stop=(j == 3)), reads=[sk, 'ones'], writes=[qk_])
            m32 = c.t32[2]; m32k = ('t32', 2); v32 = c.t32[3]; v32k = ('t32', 3); t4 = c.t32[4]; t4k = ('t32', 4)
            S.add('dve', lambda e: e.tensor_scalar(out=m32[:, :N], in0=psM[:, :N], scalar1=1.0 / 512, scalar2=None, op0=ALU.mult), reads=[mk_], writes=[m32k])
            S.add('dve', lambda e: e.tensor_tensor(out=t4[:, :N], in0=m32[:, :N], in1=m32[:, :N], op=ALU.mult), reads=[m32k], writes=[t4k])
            S.add('dve', lambda e: e.scalar_tensor_tensor(out=v32[:, :N], in0=psQ[:, :N], scalar=1.0 / 512, in1=t4[:, :N], op0=ALU.mult, op1=ALU.subtract), reads=[qk_, t4k], writes=[v32k])
            S.add('act', lambda e: e.activation(out=v32[:, :N], in_=v32[:, :N], func=AF.Sqrt, bias=float(EPS)), reads=[v32k], writes=[v32k])
            S.add('dve', lambda e: e.reciprocal(out=v32[:, :N], in_=v32[:, :N]), reads=[v32k], writes=[v32k])
            for j in range(4):
                S.add('dve', lambda e, j=j: e.tensor_tensor(out=t4[:, :N], in0=hcv[:, j, :N], in1=m32[:, :N], op=ALU.subtract), reads=[('hcv', j), m32k], writes=[t4k])
                S.add('dve', lambda e: e.tensor_tensor(out=t4[:, :N], in0=t4[:, :N], in1=v32[:, :N], op=ALU.mult), reads=[t4k, v32k], writes=[t4k])
                S.add('act', lambda e, j=j: e.activation(out=R1[:, 8 + j, :N], in_=t4[:, :N], func=AF.Silu, bias=V[:, vt.sl("ln_b", j)], scale=V[:, vt.sl("ln_g", j)]),
                      reads=[t4k, 'V'], writes=[('r1', 8 + j)])
            for cc in range(16):
                wb, wk = emit_wload(S, c, wnext())
                wbs, wsk = wsmall(wnext(2048))
                acc = c.t32[0]; ak = ('t32', 0)
                for i in range(4):
                    psG, gk_ = emit_proj(S, c, wb, wk, i, 512, N)
                    psB, bk_ = emit_proj(S, c, wbs[:, i * 512:(i + 1) * 512], wsk, 0, 128, N, kcn=4, rhs=R1[:, 4 * i:4 * i + 4, :], rhs_keys=[('r1', 4 * i + k) for k in range(4)])
                    g32 = c.t32[1]; g32k = ('t32', 1)
                    S.add('act', lambda e, psG=psG, i=i, cc=cc: e.activation(out=g32[:, :N], in_=psG[:, :N], func=AF.Sigmoid, bias=V[:, vt.sl("b_gate", i * 16 + cc)]),
                          reads=[gk_, 'V'], writes=[g32k])
                    if i == 0:
                        S.add('dve', lambda e, psB=psB: e.tensor_tensor(out=acc[:, :N], in0=g32[:, :N], in1=psB[:, :N], op=ALU.mult), reads=[g32k, bk_], writes=[ak])
                    else:
                        S.add('dve', lambda e, psB=psB: e.tensor_tensor(out=g32[:, :N], in0=g32[:, :N], in1=psB[:, :N], op=ALU.mult), reads=[g32k, bk_], writes=[g32k])
                        if i < 3:
                            S.add('dve', lambda e: e.tensor_tensor(out=acc[:, :N], in0=acc[:, :N], in1=g32[:, :N], op=ALU.add), reads=[ak, g32k], writes=[ak])
                        else:
                            S.add('dve', lambda e, cc=cc: e.tensor_tensor(out=R1[:, 16 + cc, :N], in0=acc[:, :N], in1=g32[:, :N], op=ALU.add), reads=[ak, g32k], writes=[('r1', 16 + cc)])
            mkeys = [('r1', 16 + k) for k in range(16)]
            for ob in range(4):
                wb, wk = emit_wload(S, c, wnext())
                for jj in range(4):
                    cc = ob * 4 + jj
                    ps, pk = emit_proj(S, c, wb, wk, jj, 512, N, rhs=R1[:, 16:32, :], rhs_keys=mkeys)
                    S.add('dve', lambda e, ps=ps, cc=cc: e.scalar_tensor_tensor(out=c.xT[:, cc, :N], in0=ps[:, :N], scalar=vs("g_m", cc), in1=c.xT[:, cc, :N], op0=ALU.mult, op1=ALU.add),
                          reads=[pk, 'xT', 'V'], writes=['xT'])
            emit_norm(S, c, c.xT, c.hT, N, gf[who], vs("sh_f"), 'xT')
            for hf in range(2):
                for fb in range(11):
                    wb, wk = emit_wload(S, c, wnext())
                    for pr in range(2):
                        jl = 2 * fb + pr
                        psA, ak_ = emit_proj(S, c, wb, wk, 2 * pr, 512, N)
                        psB, bk_ = emit_proj(S, c, wb, wk, 2 * pr + 1, 512, N)
                        s32 = c.t32[jl % 2]; sk = ('t32', jl % 2)
                        S.add('act', lambda e, psA=psA, s32=s32: e.activation(out=s32[:, :N], in_=psA[:, :N], func=AF.Silu), reads=[ak_], writes=[sk])
                        S.add('dve', lambda e, psB=psB, s32=s32, jl=jl: e.tensor_tensor(out=R1[:, jl, :N], in0=s32[:, :N], in1=psB[:, :N], op=ALU.mult), reads=[sk, bk_], writes=[('r1', jl)])
                ukeys = [('r1', k) for k in range(22)]
                for cp in range(8):
                    wb, wk = emit_wload(S, c, wnext(5632), L=5632)
                    for o in range(2):
                        cc = cp * 2 + o
                        ps, pk = emit_proj(S, c, wb, wk, o, 256, N, kcn=22, rhs=R1[:, 0:22, :], rhs_keys=ukeys)
                        S.add('dve', lambda e, ps=ps, cc=cc: e.scalar_tensor_tensor(out=c.xT[:, cc, :N], in0=ps[:, :N], scalar=vs("g_f", cc), in1=c.xT[:, cc, :N], op0=ALU.mult, op1=ALU.add),
                              reads=[pk, 'xT', 'V'], writes=['xT'])
            assert WOFF[0] == BPACK, (WOFF[0], BPACK)
            if last:
                emit_norm(S, c, c.xT, None, N, gfin, None, 'xT', out_f32=c.xT, out_key='xT')
                S.add('sp', lambda e: e.dma_start(out=xo[:, :, t0:t0 + N], in_=c.xT[:, :, :N]), reads=['xT'], dma=True)
            else:
                if is_ctx:
                    S.add('sp', lambda e: e.dma_start(out=co[:, :, :], in_=c.xT[:, :, :CTX]), reads=['xT'], dma=True)
                else:
                    S.add('sp', lambda e: e.dma_start(out=xo[:, :, t0:t0 + N], in_=c.xT[:, :, :N]), reads=['xT'], dma=True)
                if with_A1:
                    emit_A_tile(S, c, A, NTOK if is_ctx else t0, N, is_ctx, load=False)

        c.kvn = 0
        for t in (range(NT) if tiles is None else tiles):
            do_tile(t, t * NQ, NQ, False)
            c.wfirst = False
        if not last and do_ctx:
            do_tile(NT, 0, CTX, True)
        with nc.Block() as block:
            S.emit(nc, block, st)
    c.nops = len(S.ops)
    return nc


def bf(a):
    return np.ascontiguousarray(a)


def inputs_B(l, inp, mod, xTs, ctxTs, Aout, with_A1, last, wpB, wpA1=None):
    vt = vt_B(with_A1, last)
    cosT, sinT = rope_tables(); rT = rot_matrix_T()
    maps = []
    kT = [np.asarray(Aout[r]["kT"]).reshape(128, 4, TOKA) for r in range(8)]
    vv = [np.asarray(Aout[r]["v"]) for r in range(8)]
    uT = [np.asarray(Aout[r]["uT"]) for r in range(8)]
    per_batch = {}
    for b in range(2):
        rs = range(4 * b, 4 * b + 4)
        kTg = np.concatenate([kT[r][:, 2:4, :NTOK] for r in rs], axis=2)
        vfull = np.concatenate([vv[r][:NTOK] for r in rs], axis=0)
        vg = np.stack([vfull[:, 256 + g * 128:256 + (g + 1) * 128].reshape(NB_G, 8, 128, 128).transpose(0, 2, 1, 3).reshape(NB_G * 128, 1024) for g in range(2)], 0)
        kb = np.concatenate([kT[r][:, 0:2, :NTOK] for r in rs], axis=2)
        kbp = np.zeros((128, 2, SEQ + 256), kb.dtype); kbp[:, :, 128:128 + SEQ] = kb
        vbp = np.zeros((2, SEQ + 256, 128), vfull.dtype)
        for g in range(2):
            vbp[g, 128:128 + SEQ] = vfull[:, g * 128:(g + 1) * 128]
        ufull = np.concatenate([uT[r][:, :NTOK] for r in rs], axis=1)
        up = np.zeros((1024, SEQ + 30), np.float32); up[:, 15:15 + SEQ] = ufull
        uhc = np.zeros((1024, CTX + 30), np.float32); uhc[:, 15:15 + CTX] = uT[4 * b][:, NTOK:]
        per_batch[b] = dict(kTg=bf(kTg.reshape(128, 2 * SEQ)), vg=bf(vg.reshape(2 * NB_G * 128, 1024)), kbp=kbp, vbp=vbp, up=up, uhc=uhc,
                            kTc=bf(kT[4 * b][:, :, NTOK:].reshape(128, 4 * CTX)), vc=bf(vv[4 * b][NTOK:]))
    for r in range(8):
        b = r // 4; T0 = (r % 4) * NTOK
        pb = per_batch[b]
        m = {"wpack": wpB, "xT": xTs[r], "ctxT": ctxTs[b], "vec": vec_B(vt, l, b, mod, inp, with_A1, last),
             "cosT": bf(cosT[:, T0:T0 + NTOK]), "sinT": bf(sinT[:, T0:T0 + NTOK]), "rotT": rT,
             "masks": band_masks(r % 4 == 0, r % 4 == 3),
             "kTg": pb["kTg"], "vg": pb["vg"],
             "kTw": bf(pb["kbp"][:, :, T0:T0 + NTOK + 256].reshape(128, 2 * (NTOK + 256))),
             "vw": bf(pb["vbp"][:, T0:T0 + NTOK + 256].reshape(2 * (NTOK + 256), 128)),
             "kTc": pb["kTc"], "vc": pb["vc"],
             "uh": bf(pb["up"][:, T0:T0 + NTOK + 30]), "uhc": pb["uhc"]}
        if with_A1:
            m["wpackA"] = wpA1
        maps.append(m)
    return maps


_CACHE = {}


def _prog(name, fn):
    if name not in _CACHE:
        _CACHE[name] = fn()
    return _CACHE[name]


def kernel(x, c, ctx, c_ctx, w_ada, b_ada, norm_mix, norm_ffn, w_in, b_gate, conv_a_w, sink_b,
           qk_norm_q, qk_norm_k, conv_c_w, conv_c_b, ln_c_g, ln_c_b, w_branch, w_out,
           w_ffn_in, w_ffn_out, norm_final):
    inp = dict(x=x, c=c, ctx=ctx, c_ctx=c_ctx, w_ada=w_ada, b_ada=b_ada, norm_mix=norm_mix, norm_ffn=norm_ffn, w_in=w_in,
               b_gate=b_gate, conv_a_w=conv_a_w, sink_b=sink_b, qk_norm_q=qk_norm_q, qk_norm_k=qk_norm_k, conv_c_w=conv_c_w,
               conv_c_b=conv_c_b, ln_c_g=ln_c_g, ln_c_b=ln_c_b, w_branch=w_branch, w_out=w_out, w_ffn_in=w_ffn_in,
               w_ffn_out=w_ffn_out, norm_final=norm_final)
    inp = {k: np.asarray(v, np.float32) for k, v in inp.items()}
    x = inp['x']; ctx = inp['ctx']
    cores = list(range(8))
    res = run_bass_kernel_spmd(_prog('M', build_M), inputs_M(inp['c'], inp['c_ctx'], inp['w_ada'], inp['b_ada']), core_ids=cores)
    mod = gather_M(res.results)
    vtA = vt_A()
    cosT, sinT = rope_tables(); rT = rot_matrix_T()
    xTs = [np.ascontiguousarray(x[r // 4, (r % 4) * NTOK:(r % 4 + 1) * NTOK].T) for r in range(8)]
    ctxTs = [np.ascontiguousarray(ctx[b].T) for b in range(2)]
    wpA0 = pack_A(inp['w_in'][0])
    maps = []
    for r in range(8):
        b = r // 4; t0 = (r % 4) * NTOK
        maps.append({"wpack": wpA0, "xT": xTs[r], "ctxT": ctxTs[b], "vec": vec_A(vtA, 0, b, mod, inp['norm_mix'], inp['qk_norm_k']),
                     "cosT": np.ascontiguousarray(cosT[:, t0:t0 + NTOK]), "sinT": np.ascontiguousarray(sinT[:, t0:t0 + NTOK]), "rotT": rT})
    res = run_bass_kernel_spmd(_prog('A', build_A), maps, core_ids=cores)
    Aout = [dict(uT=res.results[r]["uT"], kT=res.results[r]["kT"], v=res.results[r]["v"]) for r in range(8)]
    del maps, wpA0
    wpB = pack_B(inp['w_in'][0], inp['w_branch'][0], inp['w_out'][0], inp['w_ffn_in'][0], inp['w_ffn_out'][0])
    wpA1 = pack_A(inp['w_in'][1])
    maps = inputs_B(0, inp, mod, xTs, ctxTs, Aout, True, False, wpB, wpA1)
    res = run_bass_kernel_spmd(_prog('B0', lambda: build_B(last=False, with_A1=True)), maps, core_ids=cores)
    xTs = [np.ascontiguousarray(res.results[r]["xo"]) for r in range(8)]
    ctxTs = [np.ascontiguousarray(res.results[4 * b]["co"]) for b in range(2)]
    Aout = [dict(uT=res.results[r]["uT"], kT=res.results[r]["kT"], v=res.results[r]["v"]) for r in range(8)]
    del maps, wpB, wpA1, res
    wpB = pack_B(inp['w_in'][1], inp['w_branch'][1], inp['w_out'][1], inp['w_ffn_in'][1], inp['w_ffn_out'][1])
    maps = inputs_B(1, inp, mod, xTs, ctxTs, Aout, False, True, wpB)
    res = run_bass_kernel_spmd(_prog('B1', lambda: build_B(last=True, with_A1=False)), maps, core_ids=cores)
    out = np.empty((2, SEQ, D), np.float32)
    for r in range(8):
        out[r // 4, (r % 4) * NTOK:(r % 4 + 1) * NTOK, :] = res.results[r]["xo"].T
    return out
```
